# Optimizing a Trainium2 kernel written in Bass

```python
import math
import jax, jax.numpy as jnp
from jax import lax
import numpy as np

D_MODEL = 1024
BATCH = 8
SEQ = 4096
DEPTH = 1

CHUNK = 64
CONV_K = 4
EPS = 1e-6

GDN_HEADS = 8
GDN_DK = 64
GDN_DV = 64
GDN_QK = GDN_HEADS * GDN_DK
GDN_V = GDN_HEADS * GDN_DV

SSM_HEADS = 8
SSM_P = 64
SSM_N = 64
SSM_GROUPS = 2
SSM_INNER = SSM_HEADS * SSM_P
SSM_BC = SSM_GROUPS * SSM_N

D_MIX = GDN_V + SSM_INNER
D_FF = 4 * D_MODEL
N_MOD = 6

DT_MIN = 0.001
DT_MAX = 0.1

GDN_CONV_CH = 2 * GDN_QK + GDN_V
SSM_CONV_CH = SSM_INNER + 2 * SSM_BC
IN_SIZES = (GDN_CONV_CH, GDN_V, GDN_HEADS, GDN_HEADS, SSM_CONV_CH, SSM_INNER, SSM_HEADS)
D_IN_PROJ = sum(IN_SIZES)
IN_SPLITS = tuple(int(s) for s in np.cumsum(IN_SIZES)[:-1])

kernel_name = "gdn_mamba2_parallel_hybrid_adaln_block"


def rmsnorm(x, w):
    x32 = x.astype(jnp.float32)
    x32 = x32 * lax.rsqrt(jnp.mean(x32 * x32, axis=-1, keepdims=True) + EPS)
    return x32.astype(x.dtype) * w


def l2norm(x):
    x32 = x.astype(jnp.float32)
    return x32 * lax.rsqrt(jnp.sum(x32 * x32, axis=-1, keepdims=True) + EPS)


def causal_dwconv(x, w, b=None):
    y = lax.conv_general_dilated(
        x, w[:, None, :].astype(x.dtype), window_strides=(1,),
        padding=((CONV_K - 1, 0),), dimension_numbers=("NWC", "WIO", "NWC"),
        feature_group_count=x.shape[-1])
    return y if b is None else y + b


def _to_chunks(t):
    b, l, h = t.shape[:3]
    t = t.reshape((b, l // CHUNK, CHUNK, h) + t.shape[3:])
    return jnp.moveaxis(t, 3, 1)


def _from_chunks(t):
    t = jnp.moveaxis(t, 1, 3)
    return t.reshape((t.shape[0], t.shape[1] * t.shape[2]) + t.shape[3:])


def _chunk_masks():
    i = jnp.arange(CHUNK)
    return i[:, None] >= i[None, :], i[:, None] > i[None, :]


def gated_delta_rule(q, k, v, g, beta):
    f32 = jnp.float32
    q, k, v = (_to_chunks(t.astype(f32)) for t in (q, k, v))
    g, beta = _to_chunks(g.astype(f32)), _to_chunks(beta.astype(f32))
    q = q * (q.shape[-1] ** -0.5)
    incl, strict = _chunk_masks()
    G = jnp.cumsum(g, axis=-1)
    decay = jnp.exp(jnp.where(incl, G[..., :, None] - G[..., None, :], -jnp.inf))
    kb = k * beta[..., None]
    a = jnp.where(strict, jnp.einsum("bhnid,bhnjd->bhnij", kb, k) * decay, 0.0)
    eye = jnp.eye(CHUNK, dtype=f32)
    t_inv = lax.linalg.triangular_solve(eye + a, jnp.broadcast_to(eye, a.shape),
                                        left_side=True, lower=True, unit_diagonal=True)
    u = t_inv @ (v * beta[..., None])
    w = t_inv @ (kb * jnp.exp(G)[..., None])
    qk = jnp.einsum("bhnid,bhnjd->bhnij", q, k) * decay
    qg = q * jnp.exp(G)[..., None]
    kd = k * jnp.exp(G[..., -1:] - G)[..., None]
    gl = jnp.exp(G[..., -1])

    def step(S, inp):
        u_n, w_n, qk_n, qg_n, kd_n, gl_n = inp
        v_new = u_n - w_n @ S
        o = qg_n @ S + qk_n @ v_new
        S = S * gl_n[..., None, None] + jnp.einsum("bhcd,bhce->bhde", kd_n, v_new)
        return S, o

    xs = tuple(jnp.moveaxis(t, 2, 0) for t in (u, w, qk, qg, kd, gl))
    S0 = jnp.zeros(q.shape[:2] + (q.shape[-1], v.shape[-1]), f32)
    _, o = lax.scan(step, S0, xs)
    return _from_chunks(jnp.moveaxis(o, 0, 2))


def ssd_chunked(x, dt, a_neg, b_mat, c_mat):
    f32 = jnp.float32
    x, b_mat, c_mat = (_to_chunks(t.astype(f32)) for t in (x, b_mat, c_mat))
    dt = _to_chunks(dt.astype(f32))
    acum = jnp.cumsum(dt * a_neg.astype(f32)[:, None, None], axis=-1)
    incl, _ = _chunk_masks()
    decay = jnp.exp(jnp.where(incl, acum[..., :, None] - acum[..., None, :], -jnp.inf))
    scores = jnp.einsum("bhnis,bhnjs->bhnij", c_mat, b_mat) * decay * dt[..., None, :]
    y_diag = scores @ x
    states = jnp.einsum("bhnjs,bhnjp->bhnsp",
                        b_mat * (jnp.exp(acum[..., -1:] - acum) * dt)[..., None], x)
    gl = jnp.exp(acum[..., -1])

    def step(h, inp):
        st, g_ = inp
        return h * g_[..., None, None] + st, h

    h0 = jnp.zeros(x.shape[:2] + (SSM_N, SSM_P), f32)
    _, h_prev = lax.scan(step, h0, (jnp.moveaxis(states, 2, 0), jnp.moveaxis(gl, 2, 0)))
    h_prev = jnp.moveaxis(h_prev, 0, 2)
    y_off = jnp.einsum("bhnis,bhnsp->bhnip", c_mat * jnp.exp(acum)[..., None], h_prev)
    return _from_chunks(y_diag + y_off)


def hybrid_mixer(h, w_in, gdn_conv_w, gdn_A_log, gdn_dt_bias, gdn_norm_w, ssm_conv_w,
                 ssm_conv_b, ssm_A_log, ssm_dt_bias, ssm_D, ssm_norm_w, w_out):
    f32 = jnp.float32
    b, l, _ = h.shape
    proj = h @ w_in
    gdn_qkv, gdn_z, gdn_b, gdn_a, ssm_xbc, ssm_z, ssm_dt = jnp.split(proj, IN_SPLITS, axis=-1)

    qkv = jax.nn.silu(causal_dwconv(gdn_qkv, gdn_conv_w))
    q, k, v = jnp.split(qkv, [GDN_QK, 2 * GDN_QK], axis=-1)
    q = l2norm(q.reshape(b, l, GDN_HEADS, GDN_DK))
    k = l2norm(k.reshape(b, l, GDN_HEADS, GDN_DK))
    v = v.reshape(b, l, GDN_HEADS, GDN_DV)
    g = -jnp.exp(gdn_A_log.astype(f32)) * jax.nn.softplus(gdn_a.astype(f32) + gdn_dt_bias)
    beta = jax.nn.sigmoid(gdn_b.astype(f32))
    o = gated_delta_rule(q, k, v, g, beta)
    o = rmsnorm(o, gdn_norm_w) * jax.nn.silu(gdn_z.reshape(b, l, GDN_HEADS, GDN_DV).astype(f32))
    o = o.reshape(b, l, GDN_V).astype(h.dtype)

    xbc = jax.nn.silu(causal_dwconv(ssm_xbc, ssm_conv_w, ssm_conv_b))
    xs, bm, cm = jnp.split(xbc, [SSM_INNER, SSM_INNER + SSM_BC], axis=-1)
    xs = xs.reshape(b, l, SSM_HEADS, SSM_P)
    rep = SSM_HEADS // SSM_GROUPS
    bm = jnp.repeat(bm.reshape(b, l, SSM_GROUPS, SSM_N), rep, axis=2)
    cm = jnp.repeat(cm.reshape(b, l, SSM_GROUPS, SSM_N), rep, axis=2)
    dt = jax.nn.softplus(ssm_dt.astype(f32) + ssm_dt_bias)
    y = ssd_chunked(xs, dt, -jnp.exp(ssm_A_log.astype(f32)), bm, cm)
    y = y + ssm_D.astype(f32)[:, None] * xs.astype(f32)
    gshape = (b, l, SSM_GROUPS, SSM_INNER // SSM_GROUPS)
    y = y.reshape(gshape) * jax.nn.silu(ssm_z.reshape(gshape).astype(f32))
    y = rmsnorm(y, ssm_norm_w.reshape(SSM_GROUPS, SSM_INNER // SSM_GROUPS))
    y = y.reshape(b, l, SSM_INNER).astype(h.dtype)

    return jnp.concatenate([o, y], axis=-1) @ w_out


def sqrelu_mlp(h, w1, w2):
    return jnp.square(jax.nn.relu(h @ w1)) @ w2


def setup_inputs(seed: int = 0) -> dict:
    key = jax.random.key(seed)
    ks = jax.random.split(key, 24)
    f32 = jnp.float32

    def nrm(k, shape, scale):
        return jax.random.normal(k, shape, f32) * scale

    def gain(k, shape):
        return 1.0 + 0.02 * jax.random.normal(k, shape, f32)

    def dt_bias_init(k, n):
        u = jax.random.uniform(k, (DEPTH, n), f32)
        dt = jnp.exp(u * (math.log(DT_MAX) - math.log(DT_MIN)) + math.log(DT_MIN))
        return dt + jnp.log(-jnp.expm1(-dt))

    def a_log_init(k, n):
        return jnp.log(jax.random.uniform(k, (DEPTH, n), f32, 1.0, 16.0))

    return {
        "x": nrm(ks[0], (BATCH, SEQ, D_MODEL), 1.0),
        "c": nrm(ks[1], (BATCH, D_MODEL), 1.0),
        "ln1_w": gain(ks[2], (DEPTH, D_MODEL)),
        "ln2_w": gain(ks[3], (DEPTH, D_MODEL)),
        "ada_w": nrm(ks[4], (DEPTH, D_MODEL, N_MOD * D_MODEL), 0.5 * D_MODEL ** -0.5),
        "ada_b": nrm(ks[5], (DEPTH, N_MOD * D_MODEL), 0.02),
        "w_in": nrm(ks[6], (DEPTH, D_MODEL, D_IN_PROJ), D_MODEL ** -0.5),
        "gdn_conv_w": nrm(ks[7], (DEPTH, CONV_K, GDN_CONV_CH), CONV_K ** -0.5),
        "gdn_A_log": a_log_init(ks[8], GDN_HEADS),
        "gdn_dt_bias": dt_bias_init(ks[9], GDN_HEADS),
        "gdn_norm_w": gain(ks[10], (DEPTH, GDN_DV)),
        "ssm_conv_w": nrm(ks[11], (DEPTH, CONV_K, SSM_CONV_CH), CONV_K ** -0.5),
        "ssm_conv_b": nrm(ks[12], (DEPTH, SSM_CONV_CH), 0.02),
        "ssm_A_log": a_log_init(ks[13], SSM_HEADS),
        "ssm_dt_bias": dt_bias_init(ks[14], SSM_HEADS),
        "ssm_D": gain(ks[15], (DEPTH, SSM_HEADS)),
        "ssm_norm_w": gain(ks[16], (DEPTH, SSM_INNER)),
        "w_out": nrm(ks[17], (DEPTH, D_MIX, D_MODEL), D_MIX ** -0.5),
        "w_ff1": nrm(ks[18], (DEPTH, D_MODEL, D_FF), D_MODEL ** -0.5),
        "w_ff2": nrm(ks[19], (DEPTH, D_FF, D_MODEL), D_FF ** -0.5),
        "final_norm_w": gain(ks[20], (D_MODEL,)),
    }


def reference(x, c, ln1_w, ln2_w, ada_w, ada_b, w_in, gdn_conv_w, gdn_A_log, gdn_dt_bias,
              gdn_norm_w, ssm_conv_w, ssm_conv_b, ssm_A_log, ssm_dt_bias, ssm_D, ssm_norm_w,
              w_out, w_ff1, w_ff2, final_norm_w):
    b = x.shape[0]
    c_act = jax.nn.silu(c)
    for layer in range(DEPTH):
        mod = (c_act @ ada_w[layer] + ada_b[layer]).reshape(b, N_MOD, D_MODEL)[:, :, None, :]
        shift1, scale1, gate1, shift2, scale2, gate2 = (mod[:, i] for i in range(N_MOD))
        h = rmsnorm(x, ln1_w[layer]) * (1.0 + scale1) + shift1
        x = x + gate1 * hybrid_mixer(
            h, w_in[layer], gdn_conv_w[layer], gdn_A_log[layer], gdn_dt_bias[layer],
            gdn_norm_w[layer], ssm_conv_w[layer], ssm_conv_b[layer], ssm_A_log[layer],
            ssm_dt_bias[layer], ssm_D[layer], ssm_norm_w[layer], w_out[layer])
        h = rmsnorm(x, ln2_w[layer]) * (1.0 + scale2) + shift2
        x = x + gate2 * sqrelu_mlp(h, w_ff1[layer], w_ff2[layer])
    return rmsnorm(x, final_norm_w)
```

```python
import math
from contextlib import ExitStack
import numpy as np
import concourse.bass as bass
import concourse.mybir as mybir
from concourse.bass_utils import run_bass_kernel_spmd

F32 = mybir.dt.float32
BF16 = mybir.dt.bfloat16
AF = mybir.ActivationFunctionType
ALU = mybir.AluOpType
AX = mybir.AxisListType

L = 4096
D = 1024
KC = 8
NCORE = 8
GT = 256
NG = L // GT
TPG = GT // 128
DFF = 4096
G2T = 256
NG2 = L // G2T
EPS = 1e-6
NIN = 3352
O_QKV, O_XBC, O_ZG, O_ZS, O_GATE = 0, 1536, 2304, 2816, 3328
NEG = -30000.0

DEBUG = {}
KNOB = {"ng": NG, "mixer": True, "ng2": NG2}


class Tracker:
    ENGS = ("pe", "act", "dve", "pool", "sp")

    def __init__(self, sems):
        self.sems = sems
        self.lists = {e: [] for e in self.ENGS}
        self.cnt = {s: 0 for s in sems}
        self.waited = {e: {} for e in self.ENGS}
        self.lastw = {}
        self.reads = {}

    def _deps(self, eng, incsem, reads, writes):
        need = {}

        def add(s, v):
            if v > need.get(s, 0):
                need[s] = v

        for k in reads:
            lw = self.lastw.get(k)
            if lw is not None:
                if lw[0] == "pe" and eng == "pe":
                    continue
                add(*lw)
        skip_same = (incsem == "pe") or incsem.startswith("d_")
        for k in writes:
            lw = self.lastw.get(k)
            if lw is not None and (lw[0] != incsem or not skip_same):
                add(*lw)
            for s, v in self.reads.get(k, {}).items():
                if s != incsem or not skip_same:
                    add(s, v)
        waits = []
        for s, v in need.items():
            if self.waited[eng].get(s, 0) < v:
                self.waited[eng][s] = v
                waits.append((s, v))
        return waits

    def op(self, eng, fn, reads=(), writes=(), incsem=None, incval=1):
        incsem = incsem or eng
        waits = self._deps(eng, incsem, reads, writes)
        self.cnt[incsem] += incval
        v = self.cnt[incsem]
        self.lists[eng].append((waits, fn, incsem, incval))
        for k in reads:
            self.reads.setdefault(k, {})[incsem] = v
        for k in writes:
            self.lastw[k] = (incsem, v)
            self.reads[k] = {}

    def dma(self, queue, slot, out, in_, reads=(), writes=(), **kw):
        self.op(queue, lambda e: e.dma_start(out=out, in_=in_, **kw), reads, writes, incsem=slot, incval=16)

    def final_wait(self, eng, keys):
        waits = self._deps(eng, "__none__", keys, keys)
        self.lists[eng].append((waits, None, None, 0))

    def emit(self, block):
        def run(name):
            def f(eng):
                for waits, fn, incsem, incval in self.lists[name]:
                    for s, v in waits:
                        eng.wait_ge(self.sems[s], v)
                    if fn is not None:
                        fn(eng).then_inc(self.sems[incsem], incval)
                self.lists[name] = []
            return f
        block.tensor(run("pe"))
        block.scalar(run("act"))
        block.vector(run("dve"))
        block.gpsimd(run("pool"))
        block.sync(run("sp"))


def build_program(debug=None):
    debug = debug or {}
    nc = bass.Bass("TRN2", target_bir_lowering=False)

    def din(name, shape):
        return nc.dram_tensor(name, list(shape), F32, kind="ExternalInput").ap()

    x_d = din("x", [L, D])
    c_d = din("c_l", [128, KC])
    win_d = din("w_in_r", [D, NIN])
    wout_d = din("w_out", [D, D])
    w1_d = din("w_ff1", [D, DFF])
    w2_d = din("w_ff2", [DFF, D])
    adaw_d = din("ada_w", [D, 6 * D])
    adab_d = din("ada_b_b", [128, 6 * D])
    lnw_d = din("lnw_b", [128, 3 * D])
    cw_d = din("cw", [128, 18 * 4])
    cb_d = din("cb", [128, 18])
    hp_d = din("hp", [128, 40])
    gnw_d = din("gnw", [128, 64])
    snw_d = din("snw", [128, 512])
    cst_d = din("cst", [128, 5 * 128 + 2])
    sel_d = din("sel", [128, 24 * 128])
    lm_d = din("lm", [128, 8 * 128])
    out_d = nc.dram_tensor("out", [L, D], F32, kind="ExternalOutput").ap()
    x1_d = nc.dram_tensor("x1_scr", [L, D], F32, kind="Internal").ap()
    dbg_d = {}
    for name, shape in debug.items():
        dbg_d[name] = nc.dram_tensor("dbg_" + name, list(shape), F32, kind="ExternalOutput").ap()

    es = ExitStack()
    with es:
        sem_names = ["pe", "act", "dve", "pool", "d_w", "d_c", "d_ada0", "d_ada1", "d_x0", "d_x1", "d_xr0", "d_xr1",
                     "d_st0", "d_st1", "d_w2", "d_p2x0", "d_p2x1", "d_p2x2", "d_p2x3", "d_dbg", "d_adb0", "d_adb1", "d_win", "d_wout", "d_sel", "d_lm", "d_w1", "d_k_cst_f", "d_k_lnw", "d_k_c_sb",
                     "d_k_hp", "d_k_gnw", "d_k_snw", "d_k_cw", "d_k_cbias", "d_po0", "d_po1", "d_po2", "d_po3", "d_k_lnw1", "d_k_lnwf", "d_w1_0", "d_w1_1", "d_w1_2", "d_w1_3"]
        sems = {n: es.enter_context(nc.semaphore("s_" + n)) for n in sem_names}
        tr = Tracker(sems)

        def sb(name, shape, dt=F32, stack=es):
            return stack.enter_context(nc.sbuf_tensor("sb_" + name, list(shape), dt))

        ps = [es.enter_context(nc.psum_tensor(f"ps{i}", [128, 512], F32)) for i in range(8)]
        bank_ctr = [0, 0]
        NB_MAIN = 5

        def bank(pool=0):
            if pool == 0:
                i = bank_ctr[0] % NB_MAIN
            else:
                i = NB_MAIN + bank_ctr[1] % (8 - NB_MAIN)
            bank_ctr[pool] += 1
            k = ("ps", i)
            if k in tr.lastw and not tr.reads.get(k):
                raise RuntimeError(f"PSUM bank {i} re-allocated before its consumer was emitted")
            return ps[i], k

        def mm(out, lhsT, rhs, start, stop, r, w):
            tr.op("pe", lambda e: e.matmul(out, lhsT, rhs, start=start, stop=stop, skip_group_check=True), r, w)

        def tp(out, in_, ident, r, w):
            tr.op("pe", lambda e: e.transpose(out, in_, ident), r, w)

        def act(out, in_, func, r, w, bias=None, scale=None, accum_out=None, eng="act"):
            kw = {}
            if bias is not None:
                kw["bias"] = bias
            if scale is not None:
                kw["scale"] = scale
            if accum_out is not None:
                kw["accum_out"] = accum_out
            tr.op("act", lambda e: e.activation(out=out, in_=in_, func=func, **kw), r, w)

        def tt(eng, out, in0, in1, op, r, w):
            tr.op(eng, lambda e: e.tensor_tensor(out=out, in0=in0, in1=in1, op=op), r, w)

        def ts(eng, out, in0, s1, s2, op0, op1, r, w):
            if op1 is None:
                tr.op(eng, lambda e: e.tensor_scalar(out=out, in0=in0, scalar1=s1, scalar2=None, op0=op0), r, w)
            else:
                tr.op(eng, lambda e: e.tensor_scalar(out=out, in0=in0, scalar1=s1, scalar2=s2, op0=op0, op1=op1), r, w)

        def stt(eng, out, in0, scalar, in1, op0, op1, r, w):
            tr.op(eng, lambda e: e.scalar_tensor_tensor(out=out, in0=in0, scalar=scalar, in1=in1, op0=op0, op1=op1), r, w)

        def cp(eng, out, in_, r, w):
            if eng == "act":
                tr.op("act", lambda e: e.copy(out=out, in_=in_), r, w)
            else:
                tr.op(eng, lambda e: e.tensor_copy(out=out, in_=in_), r, w)

        def red(eng, out, in_, r, w):
            tr.op(eng, lambda e: e.tensor_reduce(out=out, in_=in_, axis=AX.X, op=ALU.add), r, w)

        def dbg(name, ap, key):
            if name in dbg_d:
                tr.dma("sp", "d_dbg", dbg_d[name], ap, reads=[key], writes=[("dbgout", name)])

        def bc_h(ap2, n):
            P = ap2.shape[0]
            return ap2.unsqueeze(2).to_broadcast([P, ap2.shape[1], n])

        def v3(ap, h):
            return ap.rearrange("p (h d) -> p h d", h=h)

        cst_f = sb("cst_f", [128, 5 * 128 + 2])
        cst_b = sb("cst_b", [128, 5 * 128 + 2], BF16)
        c_sb = sb("c_sb", [128, KC])
        modb = sb("modb", [128, 3, D])

        ident_f = cst_f[:, 0:128]
        triu_f = cst_f[:, 128:256]
        ones_f = cst_f[:, 256:384]
        ident_b = cst_b[:, 0:128]
        mask_s_b = cst_b[:, 384:512]
        mask_i_b = cst_b[:, 512:640]

        for (dst, src, key) in [(cst_f, cst_d, "cst_f"), (c_sb, c_d, "c_sb")]:
            tr.dma("sp", "d_k_" + key, dst[:], src, reads=[], writes=[key])
        cp("dve", cst_b[:], cst_f[:], ["cst_f"], ["cst_b"])
        act(c_sb[:], c_sb[:], AF.Silu, ["c_sb"], ["c_sb"])

        ada_ctr = [0]
        MW = 256

        def compute_mod(first_chunk, adaw, adab, cB, cBk, lnw, lnwk, stage):
            for j in range(3):
                for half in range(D // MW):
                    col0 = (first_chunk + j) * D + half * MW
                    s = ada_ctr[0] % 2
                    ada_ctr[0] += 1
                    tr.dma("sp", f"d_ada{s}", stage[s][:], adaw_d[:, col0:col0 + MW].rearrange("(k p) n -> p k n", p=128),
                           reads=[], writes=[("adst", s)])
                    cp("act", adaw[s][:], stage[s][:], [("adst", s)], [("adaw", s)])
                    tr.dma("sp", f"d_adb{s}", adab[s][:], adab_d[:, col0:col0 + MW], reads=[], writes=[("adab", s)])
                    pt, pk = bank()
                    for k in range(KC):
                        mm(pt[:, 0:MW], cB[:, k, :], adaw[s][:, k, :], k == 0, k == KC - 1, [cBk, ("adaw", s)], [pk])
                    dst = modb[:, j, half * MW:(half + 1) * MW]
                    tt("dve", dst, pt[:, 0:MW], adab[s][:], ALU.add, [pk, ("adab", s)], ["modb"])
                    if j == 1:
                        stt("dve", dst, dst, 1.0, lnw[:, half * MW:(half + 1) * MW],
                            ALU.add, ALU.mult, ["modb", lnwk], ["modb"])

        ms = ExitStack()
        with ms:
            adab0 = [sb(f"adab{i}", [128, MW], F32, ms) for i in range(2)]
            adaw0 = [sb(f"adaw{i}", [128, KC, MW], BF16, ms) for i in range(2)]
            lnw0 = sb("lnw0", [128, D], F32, ms)
            cB0 = sb("cB0", [128, KC, 128], BF16, ms)
            tr.dma("sp", "d_k_lnw", lnw0[:], lnw_d[:, 0:D], reads=[], writes=["lnw0"])
            cp("dve", cB0[:], c_sb[:].unsqueeze(2).to_broadcast([128, KC, 128]), ["c_sb"], ["cB0"])
            adst0 = [sb(f"adst{i}", [128, KC, MW], F32, ms) for i in range(2)]
            compute_mod(0, adaw0, adab0, cB0, "cB0", lnw0, "lnw0", adst0)
            with nc.Block() as block:
                tr.emit(block)

        p1 = ExitStack()
        with p1:
            def sb1(name, shape, dt=F32):
                return sb(name, shape, dt, stack=p1)

            w_in = sb1("w_in", [128, KC, NIN], BF16)
            w_out = sb1("w_out", [128, KC, D], BF16)
            half_n = NIN // 2
            for k in range(KC):
                for pc in range(2):
                    tr.dma("pool", "d_win", w_in[:, k, pc * half_n:(pc + 1) * half_n],
                           win_d[k * 128:(k + 1) * 128, pc * half_n:(pc + 1) * half_n], reads=[], writes=["w_in"])
            for k in range(KC):
                tr.dma("pool", "d_wout", w_out[:, k, :], wout_d[k * 128:(k + 1) * 128, :], reads=[], writes=["w_out"])

            sel_b = sb1("sel_b", [128, 24 * 128], BF16)
            hp = sb1("hp", [128, 40])
            gnw = sb1("gnw", [128, 64])
            snw = sb1("snw", [128, 512])
            cw = sb1("cw", [128, 72])
            cbias = sb1("cbias", [128, 18])
            diagD = sb1("diagD", [128, 8, 128], BF16)
            for (dst, src, key) in [(hp, hp_d, "hp"), (gnw, gnw_d, "gnw"), (snw, snw_d, "snw"), (cw, cw_d, "cw"), (cbias, cb_d, "cbias")]:
                tr.dma("sp", "d_k_" + key, dst[:], src, reads=[], writes=[key])
            tr.dma("pool", "d_sel", sel_b[:, 0:1536], sel_d[:, 0:1536], reads=[], writes=["sel_b"])
            tr.dma("pool", "d_sel", sel_b[:, 1536:3072], sel_d[:, 1536:3072], reads=[], writes=["sel_b"])
            act(hp[:, 0:8], hp[:, 0:8], AF.Exp, ["hp"], ["hp"])
            act(hp[:, 16:24], hp[:, 16:24], AF.Exp, ["hp"], ["hp"])
            ts("dve", hp[:, 0:8], hp[:, 0:8], -1.0, None, ALU.mult, None, ["hp"], ["hp"])
            ts("dve", hp[:, 16:24], hp[:, 16:24], -1.0, None, ALU.mult, None, ["hp"], ["hp"])
            for h in range(8):
                ts("dve", diagD[:, h, :], ident_f, hp[:, 32 + h:33 + h], None, ALU.mult, None, ["hp", "cst_f"], ["diagD"])

            xt = [sb1(f"xt{i}", [128, D]) for i in range(1)]
            xr = [sb1(f"xr{i}", [128, D]) for i in range(1)]
            hbf = sb1("hbf", [128, D], BF16)
            hT = sb1("hT", [128, KC, GT], BF16)
            nstat = sb1("nstat", [128, 8])
            pre = [sb1(f"pre{i}", [128, GT + 3]) for i in range(2)]
            cacc = [sb1(f"cacc{i}", [128, GT]) for i in range(2)]
            halo = sb1("halo", [128, 18, 3])
            fm = [sb1(f"fm{i}", [128, 18, GT], BF16) for i in range(2)]
            szb = [sb1(f"sz{i}", [128, TPG, 1024], BF16) for i in range(2)]
            grawb = [sb1(f"graw{i}", [128, TPG, 24]) for i in range(2)]
            gscs = [sb1(f"gsc{i}", [128, 96]) for i in range(TPG)]
            TBs = [sb1(f"TB{i}", [128, 64]) for i in range(TPG)]
            CTbs = [sb1(f"CTb{i}", [128, 24]) for i in range(TPG)]
            SCs = [sb1(f"SC{i}", [128, 64]) for i in range(TPG)]
            RCs = [sb1(f"RC{i}", [128, 96], BF16) for i in range(TPG)]
            RCTs = [sb1(f"RCT{i}", [128, 256], BF16) for i in range(TPG)]
            kq_ms = [sb1(f"kq_m{i}", [128, 8, 2, 128], BF16) for i in range(TPG)]
            C_ms = [sb1(f"C_m{i}", [128, 2, 128], BF16) for i in range(TPG)]
            lnsss = [sb1(f"lnss{i}", [128, 16]) for i in range(TPG)]
            op_t = sb1("op_t", [128, D])
            sqf = op_t
            ssq = sb1("ssq", [128, 16])
            kgs = [sb1(f"kg{i}", [128, 512], BF16) for i in range(TPG)]
            kds = [sb1(f"kd{i}", [128, 512], BF16) for i in range(TPG)]
            Evs = [sb1(f"Ev{i}", [128, 512], BF16) for i in range(TPG)]
            xtoks = [sb1(f"xtok{i}", [128, 640], BF16) for i in range(TPG)]
            xws = [sb1(f"xw{i}", [128, 512], BF16) for i in range(TPG)]
            Eb = [sb1(f"Eb{i}", [128, 4, 128]) for i in range(2)]
            lm_b = sb1("lm_b", [128, 8, 128], BF16)
            tr.dma("pool", "d_lm", lm_b[:], lm_d.rearrange("p (l i) -> p l i", l=8), reads=[], writes=["lm_b"])
            Mp = [sb1(f"Mp{i}", [128, 8, 128], BF16) for i in range(2)]
            Np = [sb1(f"Np{i}", [128, 8, 128], BF16) for i in range(2)]
            Y = sb1("Y", [128, 8, 128], BF16)
            Yt = Mp[1]
            Pb = Np[1]
            qkT = sb1("qkT", [128, 8, 128], BF16)
            scT = sb1("scT", [128, 8, 128], BF16)
            nwT = sb1("nwT", [128, 8, 128], BF16)
            vn = sb1("vn", [128, 512], BF16)
            S_f = sb1("S_f", [128, 4, 64])
            S_b = sb1("S_b", [128, 4, 64], BF16)
            Sdec = sb1("Sdec", [128, 4, 64])
            H_f = sb1("H_f", [128, 256])
            H_b = sb1("H_b", [128, 256], BF16)
            Hdec = sb1("Hdec", [128, 256])
            o_t = sb1("o_t", [128, 512])
            o_f = sb1("o_f", [128, 512])
            y_t = sb1("y_t", [128, 512])
            y_f = sb1("y_f", [128, 512])
            mix = sb1("mix", [128, D], BF16)
            mixT = sb1("mixT", [128, KC, 128], BF16)
            print("phase1 sbuf remaining", nc.sbuf_bytes_remaining)

            tr.op("dve", lambda e: e.memset(halo[:], 0.0), [], ["halo"])
            for i_ in range(TPG):
                tr.op("dve", lambda e, b_=RCTs[i_]: e.memset(b_[:], 0.0), [], [("RCT", i_)])
            tr.op("dve", lambda e: e.memset(nwT[:], 0.0), [], [("nwT", 0, 0), ("nwT", 0, 1), ("nwT", 1, 0), ("nwT", 1, 1)])
            tr.op("dve", lambda e: e.memset(S_f[:], 0.0), [], ["S_f"])
            tr.op("dve", lambda e: e.memset(S_b[:], 0.0), [], ["S_b"])
            tr.op("dve", lambda e: e.memset(H_f[:], 0.0), [], ["H_f"])
            tr.op("dve", lambda e: e.memset(H_b[:], 0.0), [], ["H_b"])

            xctr = [0]

            def norm_to_hT(src_d, src_keys, tok0, hT_buf, hT_key, col0, slots, slot_names, hb, hbk, ns_, nsk):
                s = xctr[0] % len(slots)
                xctr[0] += 1
                xb = slots[s]
                xk = (slot_names, s)
                tr.dma("sp", f"{slot_names}{s}", xb[:], src_d[tok0:tok0 + 128, :], reads=src_keys, writes=[xk])
                act(hb[:], xb[:], AF.Square, [xk], [hbk, nsk], accum_out=ns_[:, 0:1])
                act(ns_[:, 1:2], ns_[:, 0:1], AF.Ln, [nsk], [nsk], bias=EPS, scale=1.0 / D)
                act(ns_[:, 2:3], ns_[:, 1:2], AF.Exp, [nsk], [nsk], scale=-0.5)
                return xb, xk

            def norm_finish(xb, xk, hT_buf, hT_key, col0, hb, hbk, ns_, nsk, dst=None, dstk=None, pool=0):
                dst = dst if dst is not None else xb
                dstk = dstk if dstk is not None else xk
                stt("dve", dst[:], xb[:], ns_[:, 2:3], modb[:, 1, :], ALU.mult, ALU.mult, [xk, nsk, "modb"], [dstk])
                tt("dve", hb[:], dst[:], modb[:, 0, :], ALU.add, [dstk, "modb"], [hbk])
                pt, pk = bank(pool)
                ptb = pt[:].bitcast(BF16)
                for k in range(KC):
                    tp(ptb[:, k * 128:(k + 1) * 128], hb[:, k * 128:(k + 1) * 128], ident_b, [hbk, "cst_b"], [pk])
                cp("act", hT_buf[:, :, col0:col0 + 128], ptb.rearrange("p (k t) -> p k t", k=KC), [pk], [hT_key])

            SENT = object()

            def fm_gen(g):
                gp_ = g % 2
                fmk_ = ("fm", gp_)
                F_ = fm[gp_]
                for t in range(TPG):
                    xb, xk = norm_to_hT(x_d, [], g * GT + t * 128, hT, "hT", t * 128, xt, "d_x", hbf, "hbf", nstat, "nstat")
                    norm_finish(xb, xk, hT, "hT", t * 128, hbf, "hbf", nstat, "nstat", pool=1)
                    yield
                for c0 in range(0, 18, 2):
                    pts = []
                    for c in (c0, c0 + 1):
                        pt, pk = bank(1)
                        for k in range(KC):
                            mm(pt[:, 0:GT], w_in[:, k, c * 128:(c + 1) * 128], hT[:, k, :], k == 0, k == KC - 1, ["w_in", "hT"], [pk])
                        pts.append((pt, pk))
                    for i, c in enumerate((c0, c0 + 1)):
                        pt, pk = pts[i]
                        pb, pbk, ca, cak = pre[i], ("pre", i), cacc[i], ("cacc", i)
                        cp("dve", pb[:, 0:3], halo[:, c, :], ["halo"], [pbk])
                        cp("act", pb[:, 3:GT + 3], pt[:, 0:GT], [pk], [pbk])
                        cp("dve", halo[:, c, :], pb[:, GT:GT + 3], [pbk], ["halo"])
                        tr.op("act", lambda e, ca=ca, pt=pt, c=c: e.activation(out=ca[:], in_=pt[:, 0:GT], func=AF.Copy, scale=cw[:, c * 4 + 3:c * 4 + 4]),
                              [pk, "cw"], [cak])
                    for i, c in enumerate((c0, c0 + 1)):
                        pb, pbk, ca, cak = pre[i], ("pre", i), cacc[i], ("cacc", i)
                        stt("dve", ca[:], pb[:, 2:GT + 2], cw[:, c * 4 + 2:c * 4 + 3], ca[:], ALU.mult, ALU.add, [pbk, cak, "cw"], [cak])
                        stt("dve", ca[:], pb[:, 1:GT + 1], cw[:, c * 4 + 1:c * 4 + 2], ca[:], ALU.mult, ALU.add, [pbk, cak, "cw"], [cak])
                        stt("dve", ca[:], pb[:, 0:GT], cw[:, c * 4 + 0:c * 4 + 1], ca[:], ALU.mult, ALU.add, [pbk, cak, "cw"], [cak])
                    for i, c in enumerate((c0, c0 + 1)):
                        ca, cak = cacc[i], ("cacc", i)
                        act(F_[:, c, :], ca[:], AF.Silu, [cak, "cbias"], [fmk_], bias=cbias[:, c:c + 1])
                    yield
                yield from tm_gen(g)

            def tm_gen(g):
                q = g % 2
                for t in range(TPG):
                    cs = slice(t * 128, (t + 1) * 128)
                    for zi, off in enumerate((O_ZG, O_ZS)):
                        pt, pk = bank(1)
                        for k in range(KC):
                            mm(pt[:], hT[:, k, cs], w_in[:, k, off:off + 512], k == 0, k == KC - 1, ["w_in", "hT"], [pk])
                        act(szb[q][:, t, zi * 512:(zi + 1) * 512], pt[:], AF.Silu, [pk], [("sz", q, t)])
                        yield
                    pt, pk = bank(1)
                    for k in range(KC):
                        mm(pt[:, 0:24], hT[:, k, cs], w_in[:, k, O_GATE:O_GATE + 24], k == 0, k == KC - 1, ["w_in", "hT"], [pk])
                    cp("dve", grawb[q][:, t, :], pt[:, 0:24], [pk], [("graw", q, t)])
                    yield

            for _ in fm_gen(0):
                pass
            pend = [iter(())]
            for g in range(KNOB["ng"]):
                gp = g % 2
                fmk = ("fm", gp)
                F = fm[gp]
                sz = szb[gp]
                graw = grawb[gp]
                fmg = fm_gen(g + 1) if (g + 1 < KNOB["ng"] and KNOB.get("fm_il", True)) else iter(())

                for t in range(TPG):
                    ti = g * TPG + t
                    cs = slice(t * 128, (t + 1) * 128)
                    if not KNOB["mixer"]:
                        continue
                    gsc, TB, CTb, SC, RC, RCT = gscs[t], TBs[t], CTbs[t], SCs[t], RCs[t], RCTs[t]
                    kg, kd, Ev, xtok, xw, kq_m, C_m, lnss = kgs[t], kds[t], Evs[t], xtoks[t], xws[t], kq_ms[t], C_ms[t], lnsss[t]
                    gr = graw[:, t, :]
                    grk = ("graw", gp, t)
                    act(gsc[:, 0:8], gr[:, 0:8], AF.Exp, [grk], [("gsc0", t)], scale=-1.0)
                    act(gsc[:, 8:16], gsc[:, 0:8], AF.Ln, [("gsc0", t)], [("gsc1", t)], bias=1.0)
                    tt("dve", gsc[:, 16:24], gr[:, 8:16], hp[:, 8:16], ALU.add, [grk, "hp"], [("gsc2", t)])
                    act(gsc[:, 24:32], gsc[:, 16:24], AF.Exp, [("gsc2", t)], [("gsc3", t)])
                    act(gsc[:, 32:40], gsc[:, 24:32], AF.Ln, [("gsc3", t)], [("gsc4", t)], bias=1.0)
                    tt("dve", gsc[:, 40:48], gr[:, 16:24], hp[:, 24:32], ALU.add, [grk, "hp"], [("gsc5", t)])
                    act(gsc[:, 48:56], gsc[:, 40:48], AF.Exp, [("gsc5", t)], [("gsc6", t)])
                    act(gsc[:, 56:64], gsc[:, 48:56], AF.Ln, [("gsc6", t)], [("gsc7", t)], bias=1.0)
                    act(gsc[:, 64:72], gsc[:, 56:64], AF.Ln, [("gsc7", t)], [("gsc8", t)])
                    tt("dve", gsc[:, 72:80], gsc[:, 32:40], hp[:, 0:8], ALU.mult, [("gsc4", t), "hp"], [("ga", t)])
                    tt("dve", gsc[:, 80:88], gsc[:, 56:64], hp[:, 16:24], ALU.mult, [("gsc7", t), "hp"], [("ga", t)])
                    pg, pgk = bank()
                    mm(pg[:, 0:16], triu_f, gsc[:, 72:88], True, True, ["cst_f", ("ga", t)], [pgk])
                    mm(pg[:, 16:32], ones_f, gsc[:, 72:88], True, True, ["cst_f", ("ga", t)], [pgk])
                    cp("dve", TB[:, 16:24], pg[:, 8:16], [pgk], [("TB2", t)])
                    cp("dve", TB[:, 48:64], pg[:, 16:32], [pgk], [("TB67", t)])
                    Gk = ("Gs", t)
                    cp("dve", gsc[:, 88:96], pg[:, 0:8], [pgk], [Gk])
                    G = gsc[:, 88:96]

                    pqk, pqkk = bank()
                    pqkb = pqk[:].bitcast(BF16)
                    for c in range(8):
                        tp(pqkb[:, c * 128:(c + 1) * 128], F[:, c, cs], ident_b, [fmk, "cst_b"], [pqkk])
                    pv, pvk = bank()
                    pvb = pv[:].bitcast(BF16)
                    for c in range(4):
                        tp(pvb[:, c * 128:(c + 1) * 128], F[:, 8 + c, cs], ident_b, [fmk, "cst_b"], [pvk])
                    px, pxk = bank()
                    pxb = px[:].bitcast(BF16)
                    for c in range(5):
                        tp(pxb[:, c * 128:(c + 1) * 128], F[:, 12 + c, cs], ident_b, [fmk, "cst_b"], [pxk])
                    act(sqf[:], pqkb[:, 0:1024], AF.Square, [pqkk], ["op_t"])
                    red("dve", lnss[:], v3(sqf[:], 16), ["op_t"], [("lnss", t)])
                    act(lnss[:], lnss[:], AF.Ln, [("lnss", t)], [("lnss", t)], bias=EPS)
                    lq = lnss[:, 0:8]
                    lk = lnss[:, 8:16]
                    tt("dve", gsc[:, 0:8], lk, gsc[:, 8:16], ALU.add, [("lnss", t), ("gsc1", t)], [("gsc0", t)])
                    stt("dve", TB[:, 8:16], gsc[:, 0:8], -0.5, G, ALU.mult, ALU.add, [("gsc0", t), Gk], [("TB1", t)])
                    stt("dve", TB[:, 0:8], lq, -0.5, G, ALU.mult, ALU.add, [("lnss", t), Gk], [("TB0", t)])
                    ts("dve", TB[:, 0:8], TB[:, 0:8], -math.log(8.0), None, ALU.add, None, [("TB0", t)], [("TB0", t)])
                    stt("dve", CTb[:, 8:16], G, -2.0, TB[:, 8:16], ALU.mult, ALU.add, [Gk, ("TB1", t)], [("CT1", t)])
                    stt("dve", CTb[:, 0:8], lk, -0.5, G, ALU.mult, ALU.subtract, [("lnss", t), Gk], [("CT0", t)])
                    tt("dve", CTb[:, 16:24], gsc[:, 64:72], TB[:, 16:24], ALU.subtract, [("gsc8", t), ("TB2", t)], [("CT2", t)])
                    tt("dve", TB[:, 24:32], CTb[:, 0:8], TB[:, 48:56], ALU.add, [("CT0", t), ("TB67", t)], [("TB3", t)])
                    ts("dve", TB[:, 32:40], gsc[:, 8:16], -0.5, None, ALU.mult, None, [("gsc1", t)], [("TB4", t)])
                    tt("dve", TB[:, 40:48], CTb[:, 16:24], TB[:, 56:64], ALU.add, [("CT2", t), ("TB67", t)], [("TB5", t)])
                    TBk = [("TB0", t), ("TB1", t), ("TB2", t), ("TB3", t), ("TB4", t), ("TB5", t), ("TB67", t)]
                    act(SC[:], TB[:], AF.Exp, TBk, [("SC", t)])
                    cp("dve", RC[:, 0:24], TB[:, 0:24], [("TB0", t), ("TB1", t), ("TB2", t)], [("RC0", t)])
                    tt("dve", RC[:, 24:48], TB[:, 0:24], RC[:, 0:24], ALU.subtract, [("TB0", t), ("TB1", t), ("TB2", t), ("RC0", t)], [("RC1", t)])
                    cp("dve", RC[:, 48:72], CTb[:], [("CT0", t), ("CT1", t), ("CT2", t)], [("RC2", t)])
                    tt("dve", RC[:, 72:96], CTb[:], RC[:, 48:72], ALU.subtract, [("CT0", t), ("CT1", t), ("CT2", t), ("RC2", t)], [("RC3", t)])
                    prc, prck = bank()
                    prcb = prc[:].bitcast(BF16)
                    tp(prcb[0:48, 0:128], RC[:, 0:48], ident_b, [("RC0", t), ("RC1", t), "cst_b"], [prck])
                    tp(prcb[0:48, 128:256], RC[:, 48:96], ident_b, [("RC2", t), ("RC3", t), "cst_b"], [prck])
                    cp("act", RCT[0:48, :], prcb[0:48, 0:256], [prck], [("RCT", t)])
                    tt("dve", v3(kg[:], 8), v3(pqkb[:, 512:1024], 8), bc_h(SC[:, 8:16], 64), ALU.mult, [pqkk, ("SC", t)], [("kg", t)])
                    tt("dve", v3(kd[:], 8), v3(pqkb[:, 512:1024], 8), bc_h(SC[:, 24:32], 64), ALU.mult, [pqkk, ("SC", t)], [("kd", t)])
                    tt("dve", v3(Ev[:], 8), v3(pvb[:, 0:512], 8), bc_h(SC[:, 32:40], 64), ALU.mult, [pvk, ("SC", t)], [("Ev", t)])
                    cp("act", xtok[:], pxb[:, 0:640], [pxk], [("xtok", t)])
                    tt("dve", v3(xw[:], 8), v3(pxb[:, 0:512], 8), bc_h(SC[:, 40:48], 64), ALU.mult, [pxk, ("SC", t)], [("xw", t)])

                    for h2 in range(2):
                        msk = cst_f[:, 640 + h2:641 + h2]
                        tr.op("act", lambda e, h2=h2, msk=msk, F=F, cs=cs, kq_m=kq_m: e.activation(out=kq_m[:, :, h2, :], in_=F[:, 0:8, cs], func=AF.Copy, scale=msk),
                              [fmk, "cst_f"], [("kq_m", t, h2)])
                        ts("dve", C_m[:, h2, :], F[:, 17, cs], msk, None, ALU.mult, None, [fmk, "cst_f"], [("C_m", t, h2)])
                for t in range(TPG):
                    ti = g * TPG + t
                    cs = slice(t * 128, (t + 1) * 128)
                    if not KNOB["mixer"]:
                        s = 0
                        tr.dma("sp", f"d_xr{s}", xr[s][:], x_d[ti * 128:(ti + 1) * 128, :], reads=[], writes=[("xr", s)])
                        tr.dma("sp", f"d_st{s}", x1_d[ti * 128:(ti + 1) * 128, :], xr[s][:], reads=[("xr", s)], writes=[("x1d", ti)])
                        continue
                    SC, RCT = SCs[t], RCTs[t]
                    kg, kd, Ev, xtok, xw, kq_m, C_m = kgs[t], kds[t], Evs[t], xtoks[t], xws[t], kq_ms[t], C_ms[t]
                    KQM = [("kq_m", t, 0), ("kq_m", t, 1)]
                    CM = [("C_m", t, 0), ("C_m", t, 1)]
                    def ssd_chain():
                        pcb, pcbk = bank(1)
                        for g2 in range(2):
                            mm(pcb[:, g2 * 128:(g2 + 1) * 128], F[:, 16, cs], C_m[:, g2, :], True, True, [fmk] + CM, [pcbk])
                        for hh in range(2):
                            hs = slice(hh * 4, (hh + 1) * 4)
                            pe_, pek = bank(1)
                            for h4 in range(4):
                                h = hh * 4 + h4
                                selrh = sel_b[:, (16 + h) * 128:(16 + h + 1) * 128]
                                o = pe_[:, h4 * 128:(h4 + 1) * 128]
                                mm(o, selrh, RCT[:, 0:128], True, False, ["sel_b", ("RCT", t)], [pek])
                                mm(o, RCT[:, 128:256], selrh, False, False, ["sel_b", ("RCT", t)], [pek])
                                mm(o, ident_b, mask_i_b, False, True, ["cst_b"], [pek])
                            act(Eb[hh][:], v3(pe_[:], 4), AF.Exp, [pek], [("Eb", hh)])
                            tt("dve", scT[:, hs, :], pcb[:, hh * 128:(hh + 1) * 128].unsqueeze(1).to_broadcast([128, 4, 128]),
                               Eb[hh][:], ALU.mult, [pcbk, ("Eb", hh)], [("scT", hh)])
                            yield
                        pyd, pydk = bank(1)
                        for h in range(8):
                            o = pyd[:, h * 64:(h + 1) * 64]
                            mm(o, scT[:, h, :], xtok[:, h * 64:(h + 1) * 64], True, False, [("scT", h // 4), ("xtok", t)], [pydk])
                            mm(o, diagD[:, h, :], xtok[:, h * 64:(h + 1) * 64], False, True, ["diagD", ("xtok", t)], [pydk])
                        pyo, pyok = bank(1)
                        phn, phnk = bank(1)
                        for g2 in range(2):
                            rs = slice(64 * g2, 64 * g2 + 64)
                            mm(pyo[:, g2 * 256:(g2 + 1) * 256], C_m[:, g2, :], H_b[:, :], True, True, CM + ["H_b"], [pyok])
                        mm(phn[:], xtok[:, 512:640], xw[:], True, True, [("xtok", t), ("xw", t)], [phnk])
                        for g2 in range(2):
                            rs = slice(64 * g2, 64 * g2 + 64)
                            tt("pool", v3(Hdec[rs], 4), v3(H_f[rs], 4), bc_h(SC[rs, 56 + 4 * g2:60 + 4 * g2], 64), ALU.mult, ["H_f", ("SC", t)], ["Hdec"])
                        yield
                        tt("dve", v3(y_t[:], 8), v3(pyo[:], 8), bc_h(SC[:, 16:24], 64), ALU.mult, [pyok, ("SC", t)], ["y_t"])
                        for g2 in range(2):
                            rs = slice(64 * g2, 64 * g2 + 64)
                            src = phn[rs, g2 * 256:(g2 + 1) * 256]
                            tt("dve", H_b[rs], Hdec[rs], src, ALU.add, ["Hdec", phnk], ["H_b"])
                            tt("dve", H_f[rs], Hdec[rs], src, ALU.add, ["Hdec", phnk], ["H_f"])
                        yield
                        tt("dve", y_f[:], y_t[:], pyd[:], ALU.add, ["y_t", pydk], ["y_f"])
                        if ti in (0, 1):
                            dbg(f"yf_{ti}", y_f[:], "y_f")
                        tt("pool", y_f[:], y_f[:], sz[:, t, 512:1024], ALU.mult, ["y_f", ("sz", gp, t)], ["y_f"])
                        yield
                        for g2 in range(2):
                            act(y_t[:, g2 * 256:(g2 + 1) * 256], y_f[:, g2 * 256:(g2 + 1) * 256], AF.Square, ["y_f"], ["y_t", "ssq"],
                                accum_out=ssq[:, 8 + g2:9 + g2])
                        act(ssq[:, 8:10], ssq[:, 8:10], AF.Ln, ["ssq"], ["ssq"], bias=EPS, scale=1.0 / 256)
                        act(ssq[:, 8:10], ssq[:, 8:10], AF.Exp, ["ssq"], ["ssq"], scale=-0.5)
                        yield
                        tt("dve", v3(y_t[:], 2), v3(y_f[:], 2), ssq[:, 8:10].unsqueeze(2).to_broadcast([128, 2, 256]), ALU.mult, ["y_f", "ssq"], ["y_t"])
                        tt("pool", mix[:, 512:1024], y_t[:], snw[:], ALU.mult, ["y_t", "snw"], ["mix"])

                    ssd = ssd_chain()

                    def side():
                        if next(pend[0], SENT) is SENT:
                            if next(ssd, SENT) is SENT:
                                next(fmg, None)
                    pkk = [bank(), bank()]
                    pkq = [bank(), bank()]
                    for hh in range(2):
                        a_, ak = pkk[hh]
                        b_, bk = pkq[hh]
                        for h4 in range(4):
                            h = hh * 4 + h4
                            kTc = F[:, 4 + h // 2, cs]
                            mm(a_[:, h4 * 128:(h4 + 1) * 128], kTc, kq_m[:, 4 + h // 2, h % 2, :], True, True, [fmk] + KQM, [ak])
                            mm(b_[:, h4 * 128:(h4 + 1) * 128], kTc, kq_m[:, h // 2, h % 2, :], True, True, [fmk] + KQM, [bk])
                    ectr = 0
                    for r, mk in ((1, mask_s_b), (0, mask_i_b)):
                        for hh in range(2):
                            hs = slice(hh * 4, (hh + 1) * 4)
                            pe_, pek = bank()
                            for h4 in range(4):
                                h = hh * 4 + h4
                                selrh = sel_b[:, (r * 8 + h) * 128:(r * 8 + h + 1) * 128]
                                o = pe_[:, h4 * 128:(h4 + 1) * 128]
                                mm(o, selrh, RCT[:, 0:128], True, False, ["sel_b", ("RCT", t)], [pek])
                                mm(o, RCT[:, 128:256], selrh, False, False, ["sel_b", ("RCT", t)], [pek])
                                mm(o, ident_b, mk, False, True, ["cst_b"], [pek])
                            E = Eb[ectr % 2]
                            Ek = ("Eb", ectr % 2)
                            ectr += 1
                            act(E[:], v3(pe_[:], 4), AF.Exp, [pek], [Ek])
                            if r == 1:
                                stt("dve", Mp[0][:, hs, :], v3(pkk[hh][0][:], 4), -1.0, E[:], ALU.mult, ALU.mult, [pkk[hh][1], Ek], [("Mp", 0, hh)])
                                if ti == 0 and hh == 0:
                                    dbg("E1_0", E[:].rearrange("p h i -> p (h i)"), Ek)
                            else:
                                tt("dve", qkT[:, hs, :], v3(pkq[hh][0][:], 4), E[:], ALU.mult, [pkq[hh][1], Ek], [("qkT", hh)])
                            side()
                    if KNOB.get("mstop") == 3:
                        s = 0
                        tr.dma("sp", f"d_xr{s}", xr[s][:], x_d[ti * 128:(ti + 1) * 128, :], reads=[], writes=[("xr", s)])
                        tr.dma("sp", f"d_st{s}", x1_d[ti * 128:(ti + 1) * 128, :], xr[s][:], reads=[("xr", s)], writes=[("x1d", ti)])
                        continue
                    bc4 = lambda a: a.unsqueeze(1).to_broadcast([128, 4, 128])
                    for hh in range(2):
                        hs = slice(hh * 4, (hh + 1) * 4)
                        pn, pnk = bank()
                        pnb = pn[:].bitcast(BF16)
                        for h4 in range(4):
                            tp(pnb[:, h4 * 128:(h4 + 1) * 128], Mp[0][:, hh * 4 + h4, :], ident_b, [("Mp", 0, hh), "cst_b"], [pnk])
                        cp("act", Np[0][:, hs, :], v3(pnb[:, 0:512], 4), [pnk], [("Np", 0, hh)])
                        tt("dve", Y[:, hs, :], Mp[0][:, hs, :], bc4(lm_b[:, 1, :]), ALU.mult, [("Mp", 0, hh), "lm_b"], [("Y", hh)])
                        tt("dve", Y[:, hs, :], Y[:, hs, :], bc4(ident_b), ALU.add, [("Y", hh), "cst_b"], [("Y", hh)])
                        tt("dve", Yt[:, hs, :], Np[0][:, hs, :], bc4(lm_b[:, 0, :]), ALU.mult, [("Np", 0, hh), "lm_b"], [("Yt", hh)])
                        tt("dve", Yt[:, hs, :], Yt[:, hs, :], bc4(ident_b), ALU.add, [("Yt", hh), "cst_b"], [("Yt", hh)])
                        side()
                    for lvl in range(1, 7):
                        pPs = []
                        for hh in range(2):
                            pP, pPk = bank()
                            for h4 in range(4):
                                mm(pP[:, h4 * 128:(h4 + 1) * 128], Np[0][:, hh * 4 + h4, :], Y[:, hh * 4 + h4, :], True, True, [("Np", 0, hh), ("Y", hh)], [pPk])
                            pPs.append((pP, pPk))
                        for hh in range(2):
                            hs = slice(hh * 4, (hh + 1) * 4)
                            tt("dve", Pb[:, hs, :], v3(pPs[hh][0][:], 4), bc4(lm_b[:, 1 + lvl, :]), ALU.mult, [pPs[hh][1], "lm_b"], [("Pb", hh)])
                        side()
                        upd = []
                        for hh in range(2):
                            Ytk, Pk = ("Yt", hh), ("Pb", hh)
                            pY, pYk = bank()
                            for h4 in range(4):
                                h = hh * 4 + h4
                                o = pY[:, h4 * 128:(h4 + 1) * 128]
                                mm(o, ident_b, Y[:, h, :], True, False, ["cst_b", ("Y", hh)], [pYk])
                                mm(o, Yt[:, h, :], Pb[:, h, :], False, True, [Ytk, Pk], [pYk])
                            pYt, pYtk = (None, None)
                            if lvl < 6:
                                pYt, pYtk = bank()
                                for h4 in range(4):
                                    h = hh * 4 + h4
                                    o = pYt[:, h4 * 128:(h4 + 1) * 128]
                                    mm(o, Pb[:, h, :], Yt[:, h, :], True, True, [Ytk, Pk], [pYtk])
                            upd.append((pY, pYk, pYt, pYtk))
                        for hh in range(2):
                            hs = slice(hh * 4, (hh + 1) * 4)
                            pY, pYk, pYt, pYtk = upd[hh]
                            cp("act", Y[:, hs, :], v3(pY[:], 4), [pYk], [("Y", hh)])
                            if lvl < 6:
                                tt("dve", Yt[:, hs, :], Yt[:, hs, :], v3(pYt[:], 4), ALU.add, [("Yt", hh), pYtk], [("Yt", hh)])
                        side()
                    if KNOB.get("mstop") == 5:
                        s = 0
                        tr.dma("sp", f"d_xr{s}", xr[s][:], x_d[ti * 128:(ti + 1) * 128, :], reads=[], writes=[("xr", s)])
                        tr.dma("sp", f"d_st{s}", x1_d[ti * 128:(ti + 1) * 128, :], xr[s][:], reads=[("xr", s)], writes=[("x1d", ti)])
                        continue
                    for hh in range(2):
                        pw, pwk = bank()
                        for h4 in range(4):
                            h = hh * 4 + h4
                            pr = h // 2
                            mm(pw[:, h4 * 128:(h4 + 1) * 128], kg[:, pr * 128:(pr + 1) * 128], Y[:, h, :], True, True, [("kg", t), ("Y", hh)], [pwk])
                        pw4 = v3(pw[:], 4)
                        tr.op("act", lambda e, pw4=pw4, hh=hh: e.mul(out=nwT[0:64, hh * 4:hh * 4 + 4:2, :], in_=pw4[0:64, 0::2, :], mul=-1.0), [pwk], [("nwT", hh, 0)])
                        tr.op("act", lambda e, pw4=pw4, hh=hh: e.mul(out=nwT[64:128, hh * 4 + 1:hh * 4 + 4:2, :], in_=pw4[64:128, 1::2, :], mul=-1.0), [pwk], [("nwT", hh, 1)])
                    if KNOB.get("mstop") == 6:
                        s = 0
                        tr.dma("sp", f"d_xr{s}", xr[s][:], x_d[ti * 128:(ti + 1) * 128, :], reads=[], writes=[("xr", s)])
                        tr.dma("sp", f"d_st{s}", x1_d[ti * 128:(ti + 1) * 128, :], xr[s][:], reads=[("xr", s)], writes=[("x1d", ti)])
                        continue
                    for _ in pend[0]:
                        pass
                    Yks = [("Y", 0), ("Y", 1)]
                    pvn, pvnk = bank()
                    for h in range(8):
                        po = 64 * (h % 2)
                        o = pvn[:, h * 64:(h + 1) * 64]
                        mm(o, Y[:, h, :], Ev[:, h * 64:(h + 1) * 64], True, False, Yks + [("Ev", t)], [pvnk])
                        mm(o, nwT[:, h, :], S_b[:, h // 2, :], False, True, [("nwT", h // 4, h % 2), "S_b"], [pvnk])
                    po1, po1k = bank()
                    for h in range(8):
                        po = 64 * (h % 2)
                        mm(po1[:, h * 64:(h + 1) * 64], kq_m[:, h // 2, h % 2, :], S_b[:, h // 2, :], True, True, KQM + ["S_b"], [po1k])
                    tt("dve", v3(vn[:], 8), v3(pvn[:], 8), bc_h(SC[:, 32:40], 64), ALU.mult, [pvnk, ("SC", t)], ["vn"])
                    side()
                    glg = SC[:, 48:56].rearrange("p (pr two) -> p pr two", two=2)
                    for h2 in range(2):
                        rs = slice(64 * h2, 64 * h2 + 64)
                        tt("pool", Sdec[rs], S_f[rs], glg[rs, :, h2].unsqueeze(2).to_broadcast([64, 4, 64]), ALU.mult, ["S_f", ("SC", t)], ["Sdec"])
                    po2, po2k = bank()
                    psn, psnk = bank()
                    for h in range(8):
                        mm(po2[:, h * 64:(h + 1) * 64], qkT[:, h, :], vn[:, h * 64:(h + 1) * 64], True, True, [("qkT", h // 4), "vn"], [po2k])
                    for pr in range(4):
                        mm(psn[:, pr * 128:(pr + 1) * 128], kd[:, pr * 128:(pr + 1) * 128], vn[:, pr * 128:(pr + 1) * 128], True, True, [("kd", t), "vn"], [psnk])
                    psn4 = v3(psn[:], 4)
                    for h2 in range(2):
                        rs = slice(64 * h2, 64 * h2 + 64)
                        src = psn4[rs, :, 64 * h2:64 * h2 + 64]
                        tt("dve", S_b[rs], Sdec[rs], src, ALU.add, ["Sdec", psnk], ["S_b"])
                        tt("dve", S_f[rs], Sdec[rs], src, ALU.add, ["Sdec", psnk], ["S_f"])
                    tt("dve", v3(o_t[:], 8), v3(po1[:], 8), bc_h(SC[:, 0:8], 64), ALU.mult, [po1k, ("SC", t)], ["o_t"])
                    tt("dve", o_f[:], o_t[:], po2[:], ALU.add, ["o_t", po2k], ["o_f"])
                    side()
                    if ti in (0, 1):
                        dbg(f"of_{ti}", o_f[:], "o_f")
                        dbg(f"SC_{ti}", SC[:], ("SC", t))
                    if KNOB.get("mstop") == 7:
                        s = 0
                        tr.dma("sp", f"d_xr{s}", xr[s][:], x_d[ti * 128:(ti + 1) * 128, :], reads=[], writes=[("xr", s)])
                        tr.dma("sp", f"d_st{s}", x1_d[ti * 128:(ti + 1) * 128, :], xr[s][:], reads=[("xr", s)], writes=[("x1d", ti)])
                        continue
                    act(sqf[:, 0:512], o_f[:], AF.Square, ["o_f"], ["op_t"])
                    red("dve", ssq[:, 0:8], v3(sqf[:, 0:512], 8), ["op_t"], ["ssq"])
                    act(ssq[:, 0:8], ssq[:, 0:8], AF.Ln, ["ssq"], ["ssq"], bias=EPS, scale=1.0 / 64)
                    act(ssq[:, 0:8], ssq[:, 0:8], AF.Exp, ["ssq"], ["ssq"], scale=-0.5)
                    tt("dve", v3(o_t[:], 8), v3(o_f[:], 8), bc_h(ssq[:, 0:8], 64), ALU.mult, ["o_f", "ssq"], ["o_t"])
                    tt("pool", v3(o_t[:], 8), v3(o_t[:], 8), gnw[:].unsqueeze(1).to_broadcast([128, 8, 64]), ALU.mult, ["o_t", "gnw"], ["o_t"])
                    tt("pool", mix[:, 0:512], o_t[:], sz[:, t, 0:512], ALU.mult, ["o_t", ("sz", gp, t)], ["mix"])

                    if KNOB.get("mstop") == 8:
                        s = 0
                        tr.dma("sp", f"d_xr{s}", xr[s][:], x_d[ti * 128:(ti + 1) * 128, :], reads=[], writes=[("xr", s)])
                        tr.dma("sp", f"d_st{s}", x1_d[ti * 128:(ti + 1) * 128, :], xr[s][:], reads=[("xr", s)], writes=[("x1d", ti)])
                        continue
                    if KNOB.get("mstop") == 9:
                        s = 0
                        tr.dma("sp", f"d_xr{s}", xr[s][:], x_d[ti * 128:(ti + 1) * 128, :], reads=[], writes=[("xr", s)])
                        tr.dma("sp", f"d_st{s}", x1_d[ti * 128:(ti + 1) * 128, :], xr[s][:], reads=[("xr", s)], writes=[("x1d", ti)])
                        continue
                    for _ in ssd:
                        pass
                    def tail_chain(ti=ti):
                        s_ = 0
                        pmt, pmtk = bank(1)
                        pmtb = pmt[:].bitcast(BF16)
                        for c in range(KC):
                            tp(pmtb[:, c * 128:(c + 1) * 128], mix[:, c * 128:(c + 1) * 128], ident_b, ["mix", "cst_b"], [pmtk])
                        cp("act", mixT[:], pmtb.rearrange("p (k t) -> p k t", k=KC), [pmtk], ["mixT"])
                        tr.dma("sp", f"d_xr{s_}", xr[s_][:], x_d[ti * 128:(ti + 1) * 128, :], reads=[], writes=[("xr", s_)])
                        yield
                        for n in range(2):
                            pop, popk = bank(1)
                            for c in range(KC):
                                mm(pop[:], mixT[:, c, :], w_out[:, c, n * 512:(n + 1) * 512], c == 0, c == KC - 1, ["mixT", "w_out"], [popk])
                            ns = slice(n * 512, (n + 1) * 512)
                            tt("dve", op_t[:, ns], pop[:], modb[:, 2, ns], ALU.mult, [popk, "modb"], ["op_t"])
                            tt("pool", xr[s_][:, ns], op_t[:, ns], xr[s_][:, ns], ALU.add, ["op_t", ("xr", s_)], [("xr", s_)])
                            if n == 0:
                                yield
                        tr.dma("sp", f"d_st{s_}", x1_d[ti * 128:(ti + 1) * 128, :], xr[s_][:], reads=[("xr", s_)], writes=[("x1d", ti)])
                        if ti in (0, 1):
                            dbg(f"x1_{ti}", xr[s_][:], ("xr", s_))

                    for _ in pend[0]:
                        pass
                    pend[0] = tail_chain()
                if not KNOB.get("fm_il", True) and g + 1 < KNOB["ng"]:
                    fmg = fm_gen(g + 1)
                for _ in fmg:
                    pass
            for _ in pend[0]:
                pass
            with nc.Block() as block:
                tr.emit(block)

        p2 = ExitStack()
        with p2:
            def sb2(name, shape, dt=F32):
                return sb(name, shape, dt, stack=p2)

            W1 = sb2("W1", [128, KC, DFF], BF16)
            W2 = sb2("W2", [128, 32, D], BF16)
            ms = ExitStack()
            with ms:
                adab1 = [sb(f"adabb{i}", [128, MW], F32, ms) for i in range(2)]
                adaw1 = [sb(f"adawb{i}", [128, KC, MW], BF16, ms) for i in range(2)]
                lnw1 = sb("lnw1", [128, D], F32, ms)
                cB1 = sb("cB1", [128, KC, 128], BF16, ms)
                tr.dma("sp", "d_k_lnw1", lnw1[:], lnw_d[:, D:2 * D], reads=[], writes=["lnw1"])
                cp("dve", cB1[:], c_sb[:].unsqueeze(2).to_broadcast([128, KC, 128]), ["c_sb"], ["cB1"])
                adst1 = [sb(f"adstb{i}", [128, KC, MW], F32, ms) for i in range(2)]
                compute_mod(3, adaw1, adab1, cB1, "cB1", lnw1, "lnw1", adst1)
                for pc in range(4):
                    for k in range(KC):
                        tr.dma("pool", f"d_w1_{pc}", W1[:, k, pc * 1024:(pc + 1) * 1024], w1_d[k * 128:(k + 1) * 128, pc * 1024:(pc + 1) * 1024],
                               reads=[], writes=[("W1", pc)])
                for k in range(32):
                    tr.dma("pool", "d_w2", W2[:, k, :], w2_d[k * 128:(k + 1) * 128, :], reads=[], writes=["W2"])
                with nc.Block() as block:
                    tr.emit(block)
            lnwf = sb2("lnwf", [128, D])
            tr.dma("sp", "d_k_lnwf", lnwf[:], lnw_d[:, 2 * D:3 * D], reads=[], writes=["lnwf"])
            x1k = [sb2(f"x1k{i}", [128, D]) for i in range(4)]
            hbf2 = sb2("hbf2", [128, D], BF16)
            h2T = [sb2(f"h2T{i}", [128, KC, G2T], BF16) for i in range(2)]
            nst2 = sb2("nst2", [128, 8])
            rl = [sb2(f"rl{i}", [128, G2T]) for i in range(2)]
            aT = sb2("aT", [128, 32, G2T], BF16)
            f_t = sb2("f_t", [128, D])
            hn2 = sb2("hn2", [128, D])
            print("phase2 sbuf remaining", nc.sbuf_bytes_remaining)

            def p2_norm(g):
                tiles = []
                hb = h2T[g % 2]
                hk = ("h2T", g % 2)
                for t in range(2):
                    ti = g * 2 + t
                    xb, xk = norm_to_hT(x1_d, [("x1d", ti)], ti * 128, hb, hk, t * 128, x1k, "d_p2x", hbf2, "hbf2", nst2, "nst2")
                    norm_finish(xb, xk, hb, hk, t * 128, hbf2, "hbf2", nst2, "nst2", dst=hn2, dstk="hn2")
                    tiles.append((xb, xk))
                return tiles

            def p2_ffn1(g):
                hb = h2T[g % 2]
                hk = ("h2T", g % 2)
                for c in range(32):
                    pt, pk = bank()
                    for k in range(KC):
                        mm(pt[:, 0:G2T], W1[:, k, c * 128:(c + 1) * 128], hb[:, k, :], k == 0, k == KC - 1, [("W1", c // 8), hk], [pk])
                    r_ = rl[c % 2]
                    rk = ("rl", c % 2)
                    act(r_[:], pt[:, 0:G2T], AF.Relu, [pk], [rk])
                    tt("pool" if c % 2 else "dve", aT[:, c, :], r_[:], r_[:], ALU.mult, [rk], [("aT", c)])

            def p2_ffn2(g, tiles):
                for t in range(2):
                    ti = g * 2 + t
                    xb, xk = tiles[t]
                    for n in range(2):
                        pt, pk = bank()
                        for c in range(32):
                            mm(pt[:], aT[:, c, t * 128:(t + 1) * 128], W2[:, c, n * 512:(n + 1) * 512], c == 0, c == 31, [("aT", c), "W2"], [pk])
                        ns = slice(n * 512, (n + 1) * 512)
                        tt("dve", f_t[:, ns], pt[:], modb[:, 2, ns], ALU.mult, [pk, "modb"], ["f_t"])
                        tt("pool", xb[:, ns], f_t[:, ns], xb[:, ns], ALU.add, ["f_t", xk], [xk])
                    act(hbf2[:], xb[:], AF.Square, [xk], ["hbf2", "nst2b"], accum_out=nst2[:, 4:5])
                    act(nst2[:, 5:6], nst2[:, 4:5], AF.Ln, ["nst2b"], ["nst2b"], bias=EPS, scale=1.0 / D)
                    act(nst2[:, 6:7], nst2[:, 5:6], AF.Exp, ["nst2b"], ["nst2b"], scale=-0.5)
                    stt("dve", xb[:], xb[:], nst2[:, 6:7], lnwf[:], ALU.mult, ALU.mult, [xk, "nst2b", "lnwf"], [xk])
                    slot = xk[1]
                    tr.dma("sp", f"d_po{slot}", out_d[ti * 128:(ti + 1) * 128, :], xb[:], reads=[xk], writes=[("outd", slot)])

            n2 = KNOB["ng2"]
            nxt_tiles = p2_norm(0) if n2 > 0 else None
            for g in range(n2):
                tiles = nxt_tiles
                p2_ffn1(g)
                if g + 1 < n2:
                    nxt_tiles = p2_norm(g + 1)
                p2_ffn2(g, tiles)
            tr.final_wait("sp", [("outd", i) for i in range(4)] + [("dbgout", n) for n in dbg_d])
            print("sem counts", {k: v for k, v in tr.cnt.items()})
            with nc.Block() as block:
                tr.emit(block)
    return nc


def host_constants():
    ident = np.eye(128, dtype=np.float32)
    k = np.arange(128)
    triu = (k[:, None] <= k[None, :]).astype(np.float32)
    ones = np.ones((128, 128), np.float32)
    mask_s = np.where(k[None, :] > k[:, None], 0.0, NEG).astype(np.float32)
    mask_i = np.where(k[None, :] >= k[:, None], 0.0, NEG).astype(np.float32)
    m0 = (k < 64).astype(np.float32)[:, None]
    cst = np.concatenate([ident, triu, ones, mask_s, mask_i, m0, 1.0 - m0], axis=1)
    sel = np.zeros((128, 24, 128), np.float32)
    for r in range(3):
        for h in range(8):
            sel[r * 8 + h, r * 8 + h, :] = 1.0
            sel[24 + r * 8 + h, r * 8 + h, :] = 1.0
    def lmask(b):
        i = k[:, None]
        j = k[None, :]
        return ((i // (2 * b) == j // (2 * b)) & ((i // b) % 2 == 1) & ((j // b) % 2 == 0)).astype(np.float32)
    lms = [lmask(1)] + [lmask(b).T for b in (1, 2, 4, 8, 16, 32, 64)]
    lm = np.concatenate(lms, axis=1)
    return np.ascontiguousarray(cst), np.ascontiguousarray(sel.reshape(128, 24 * 128)), np.ascontiguousarray(lm)


def prep_inputs(inputs):
    f = lambda a: np.ascontiguousarray(np.asarray(a, dtype=np.float32))
    w_in = f(inputs["w_in"])[0]
    perm = np.concatenate([np.arange(0, 1536), np.arange(2064, 2832), np.arange(1536, 2048), np.arange(2832, 3344),
                           np.arange(2048, 2056), np.arange(2056, 2064), np.arange(3344, 3352)])
    w_in_r = np.ascontiguousarray(w_in[:, perm])
    gcw = f(inputs["gdn_conv_w"])[0]
    scw = f(inputs["ssm_conv_w"])[0]
    allw = np.concatenate([gcw, scw], axis=1)
    cw = np.ascontiguousarray(allw.reshape(4, 18, 128).transpose(2, 1, 0).reshape(128, 72))
    cb_full = np.concatenate([np.zeros(1536, np.float32), f(inputs["ssm_conv_b"])[0]])
    cb = np.ascontiguousarray(cb_full.reshape(18, 128).T)
    hp_row = np.concatenate([f(inputs["gdn_A_log"])[0], f(inputs["gdn_dt_bias"])[0], f(inputs["ssm_A_log"])[0],
                             f(inputs["ssm_dt_bias"])[0], f(inputs["ssm_D"])[0]])
    bc = lambda row: np.ascontiguousarray(np.broadcast_to(row[None, :], (128, row.shape[0])))
    lnw_row = np.concatenate([f(inputs["ln1_w"])[0], f(inputs["ln2_w"])[0], f(inputs["final_norm_w"])])
    cst, sel, lm = host_constants()
    shared = {
        "w_in_r": w_in_r, "w_out": f(inputs["w_out"])[0], "w_ff1": f(inputs["w_ff1"])[0], "w_ff2": f(inputs["w_ff2"])[0],
        "ada_w": f(inputs["ada_w"])[0], "ada_b_b": bc(f(inputs["ada_b"])[0]), "lnw_b": bc(lnw_row), "cw": cw, "cb": cb,
        "hp": bc(hp_row), "gnw": bc(f(inputs["gdn_norm_w"])[0]), "snw": bc(f(inputs["ssm_norm_w"])[0]),
        "cst": cst, "sel": sel, "lm": lm,
    }
    x = f(inputs["x"])
    c = f(inputs["c"])
    in_maps = []
    for b in range(NCORE):
        m = dict(shared)
        m["x"] = np.ascontiguousarray(x[b])
        m["c_l"] = np.ascontiguousarray(c[b].reshape(KC, 128).T)
        in_maps.append(m)
    return in_maps


def kernel(**inputs):
    in_maps = prep_inputs(inputs)
    nc = build_program(DEBUG)
    res = run_bass_kernel_spmd(nc, in_maps, core_ids=list(range(NCORE)))
    out = np.stack([np.asarray(r["out"], dtype=np.float32) for r in res.results], axis=0)
    if DEBUG:
        kernel.debug = [{k: np.asarray(v) for k, v in r.items() if k.startswith("dbg_")} for r in res.results]
    return out
```

```python
import math
from contextlib import ExitStack
import numpy as np
import concourse.bass as bass
import concourse.mybir as mybir
from concourse.bass_utils import run_bass_kernel_spmd

F32 = mybir.dt.float32
BF16 = mybir.dt.bfloat16
AF = mybir.ActivationFunctionType
ALU = mybir.AluOpType
AX = mybir.AxisListType

L = 4096
D = 1024
KC = 8
NCORE = 8
GT = 256
NG = L // GT
TPG = GT // 128
DFF = 4096
G2T = 256
NG2 = L // G2T
EPS = 1e-6
NIN = 3352
O_QKV, O_XBC, O_ZG, O_ZS, O_GATE = 0, 1536, 2304, 2816, 3328
NEG = -30000.0

DEBUG = {}
KNOB = {"ng": NG, "mixer": True, "ng2": NG2}


class Tracker:
    ENGS = ("pe", "act", "dve", "pool", "sp")

    def __init__(self, sems):
        self.sems = sems
        self.lists = {e: [] for e in self.ENGS}
        self.cnt = {s: 0 for s in sems}
        self.waited = {e: {} for e in self.ENGS}
        self.lastw = {}
        self.reads = {}

    def _deps(self, eng, incsem, reads, writes):
        need = {}

        def add(s, v):
            if v > need.get(s, 0):
                need[s] = v

        for k in reads:
            lw = self.lastw.get(k)
            if lw is not None:
                if lw[0] == "pe" and eng == "pe":
                    continue
                add(*lw)
        skip_same = (incsem == "pe") or incsem.startswith("d_")
        for k in writes:
            lw = self.lastw.get(k)
            if lw is not None and (lw[0] != incsem or not skip_same):
                add(*lw)
            for s, v in self.reads.get(k, {}).items():
                if s != incsem or not skip_same:
                    add(s, v)
        waits = []
        for s, v in need.items():
            if self.waited[eng].get(s, 0) < v:
                self.waited[eng][s] = v
                waits.append((s, v))
        return waits

    def op(self, eng, fn, reads=(), writes=(), incsem=None, incval=1):
        incsem = incsem or eng
        waits = self._deps(eng, incsem, reads, writes)
        self.cnt[incsem] += incval
        v = self.cnt[incsem]
        self.lists[eng].append((waits, fn, incsem, incval))
        for k in reads:
            self.reads.setdefault(k, {})[incsem] = v
        for k in writes:
            self.lastw[k] = (incsem, v)
            self.reads[k] = {}

    def dma(self, queue, slot, out, in_, reads=(), writes=(), **kw):
        self.op(queue, lambda e: e.dma_start(out=out, in_=in_, **kw), reads, writes, incsem=slot, incval=16)

    def final_wait(self, eng, keys):
        waits = self._deps(eng, "__none__", keys, keys)
        self.lists[eng].append((waits, None, None, 0))

    def emit(self, block):
        def run(name):
            def f(eng):
                for waits, fn, incsem, incval in self.lists[name]:
                    for s, v in waits:
                        eng.wait_ge(self.sems[s], v)
                    if fn is not None:
                        fn(eng).then_inc(self.sems[incsem], incval)
                self.lists[name] = []
            return f
        block.tensor(run("pe"))
        block.scalar(run("act"))
        block.vector(run("dve"))
        block.gpsimd(run("pool"))
        block.sync(run("sp"))


def build_program(debug=None):
    debug = debug or {}
    nc = bass.Bass("TRN2", target_bir_lowering=False)

    def din(name, shape):
        return nc.dram_tensor(name, list(shape), F32, kind="ExternalInput").ap()

    x_d = din("x", [L, D])
    c_d = din("c_l", [128, KC])
    win_d = din("w_in_r", [D, NIN])
    wout_d = din("w_out", [D, D])
    w1_d = din("w_ff1", [D, DFF])
    w2_d = din("w_ff2", [DFF, D])
    adaw_d = din("ada_w", [D, 6 * D])
    adab_d = din("ada_b_b", [128, 6 * D])
    lnw_d = din("lnw_b", [128, 3 * D])
    cw_d = din("cw", [128, 18 * 4])
    cb_d = din("cb", [128, 18])
    hp_d = din("hp", [128, 40])
    gnw_d = din("gnw", [128, 64])
    snw_d = din("snw", [128, 512])
    cst_d = din("cst", [128, 5 * 128 + 2])
    sel_d = din("sel", [128, 24 * 128])
    lm_d = din("lm", [128, 8 * 128])
    out_d = nc.dram_tensor("out", [L, D], F32, kind="ExternalOutput").ap()
    x1_d = nc.dram_tensor("x1_scr", [L, D], F32, kind="Internal").ap()
    dbg_d = {}
    for name, shape in debug.items():
        dbg_d[name] = nc.dram_tensor("dbg_" + name, list(shape), F32, kind="ExternalOutput").ap()

    es = ExitStack()
    with es:
        sem_names = ["pe", "act", "dve", "pool", "d_w", "d_c", "d_ada0", "d_ada1", "d_x0", "d_x1", "d_xr0", "d_xr1",
                     "d_st0", "d_st1", "d_w2", "d_p2x0", "d_p2x1", "d_p2x2", "d_p2x3", "d_dbg", "d_adb0", "d_adb1", "d_win", "d_wout", "d_sel", "d_lm", "d_w1", "d_k_cst_f", "d_k_lnw", "d_k_c_sb",
                     "d_k_hp", "d_k_gnw", "d_k_snw", "d_k_cw", "d_k_cbias", "d_po0", "d_po1", "d_po2", "d_po3", "d_k_lnw1", "d_k_lnwf", "d_w1_0", "d_w1_1", "d_w1_2", "d_w1_3"]
        sems = {n: es.enter_context(nc.semaphore("s_" + n)) for n in sem_names}
        tr = Tracker(sems)

        def sb(name, shape, dt=F32, stack=es):
            return stack.enter_context(nc.sbuf_tensor("sb_" + name, list(shape), dt))

        ps = [es.enter_context(nc.psum_tensor(f"ps{i}", [128, 512], F32)) for i in range(8)]
        bank_ctr = [0, 0]
        NB_MAIN = 5

        def bank(pool=0):
            if pool == 0:
                i = bank_ctr[0] % NB_MAIN
            else:
                i = NB_MAIN + bank_ctr[1] % (8 - NB_MAIN)
            bank_ctr[pool] += 1
            k = ("ps", i)
            if k in tr.lastw and not tr.reads.get(k):
                raise RuntimeError(f"PSUM bank {i} re-allocated before its consumer was emitted")
            return ps[i], k

        def mm(out, lhsT, rhs, start, stop, r, w):
            tr.op("pe", lambda e: e.matmul(out, lhsT, rhs, start=start, stop=stop, skip_group_check=True), r, w)

        def tp(out, in_, ident, r, w):
            tr.op("pe", lambda e: e.transpose(out, in_, ident), r, w)

        def act(out, in_, func, r, w, bias=None, scale=None, accum_out=None, eng="act"):
            kw = {}
            if bias is not None:
                kw["bias"] = bias
            if scale is not None:
                kw["scale"] = scale
            if accum_out is not None:
                kw["accum_out"] = accum_out
            tr.op("act", lambda e: e.activation(out=out, in_=in_, func=func, **kw), r, w)

        def tt(eng, out, in0, in1, op, r, w):
            tr.op(eng, lambda e: e.tensor_tensor(out=out, in0=in0, in1=in1, op=op), r, w)

        def ts(eng, out, in0, s1, s2, op0, op1, r, w):
            if op1 is None:
                tr.op(eng, lambda e: e.tensor_scalar(out=out, in0=in0, scalar1=s1, scalar2=None, op0=op0), r, w)
            else:
                tr.op(eng, lambda e: e.tensor_scalar(out=out, in0=in0, scalar1=s1, scalar2=s2, op0=op0, op1=op1), r, w)

        def stt(eng, out, in0, scalar, in1, op0, op1, r, w):
            tr.op(eng, lambda e: e.scalar_tensor_tensor(out=out, in0=in0, scalar=scalar, in1=in1, op0=op0, op1=op1), r, w)

        def cp(eng, out, in_, r, w):
            if eng == "act":
                tr.op("act", lambda e: e.copy(out=out, in_=in_), r, w)
            else:
                tr.op(eng, lambda e: e.tensor_copy(out=out, in_=in_), r, w)

        def red(eng, out, in_, r, w):
            tr.op(eng, lambda e: e.tensor_reduce(out=out, in_=in_, axis=AX.X, op=ALU.add), r, w)

        def dbg(name, ap, key):
            if name in dbg_d:
                tr.dma("sp", "d_dbg", dbg_d[name], ap, reads=[key], writes=[("dbgout", name)])

        def bc_h(ap2, n):
            P = ap2.shape[0]
            return ap2.unsqueeze(2).to_broadcast([P, ap2.shape[1], n])

        def v3(ap, h):
            return ap.rearrange("p (h d) -> p h d", h=h)

        cst_f = sb("cst_f", [128, 5 * 128 + 2])
        cst_b = sb("cst_b", [128, 5 * 128 + 2], BF16)
        c_sb = sb("c_sb", [128, KC])
        modb = sb("modb", [128, 3, D])

        ident_f = cst_f[:, 0:128]
        triu_f = cst_f[:, 128:256]
        ones_f = cst_f[:, 256:384]
        ident_b = cst_b[:, 0:128]
        mask_s_b = cst_b[:, 384:512]
        mask_i_b = cst_b[:, 512:640]

        for (dst, src, key) in [(cst_f, cst_d, "cst_f"), (c_sb, c_d, "c_sb")]:
            tr.dma("sp", "d_k_" + key, dst[:], src, reads=[], writes=[key])
        cp("dve", cst_b[:], cst_f[:], ["cst_f"], ["cst_b"])
        act(c_sb[:], c_sb[:], AF.Silu, ["c_sb"], ["c_sb"])

        ada_ctr = [0]
        MW = 256

        def compute_mod(first_chunk, adaw, adab, cB, cBk, lnw, lnwk, stage):
            for j in range(3):
                for half in range(D // MW):
                    col0 = (first_chunk + j) * D + half * MW
                    s = ada_ctr[0] % 2
                    ada_ctr[0] += 1
                    tr.dma("sp", f"d_ada{s}", stage[s][:], adaw_d[:, col0:col0 + MW].rearrange("(k p) n -> p k n", p=128),
                           reads=[], writes=[("adst", s)])
                    cp("act", adaw[s][:], stage[s][:], [("adst", s)], [("adaw", s)])
                    tr.dma("sp", f"d_adb{s}", adab[s][:], adab_d[:, col0:col0 + MW], reads=[], writes=[("adab", s)])
                    pt, pk = bank()
                    for k in range(KC):
                        mm(pt[:, 0:MW], cB[:, k, :], adaw[s][:, k, :], k == 0, k == KC - 1, [cBk, ("adaw", s)], [pk])
                    dst = modb[:, j, half * MW:(half + 1) * MW]
                    tt("dve", dst, pt[:, 0:MW], adab[s][:], ALU.add, [pk, ("adab", s)], ["modb"])
                    if j == 1:
                        stt("dve", dst, dst, 1.0, lnw[:, half * MW:(half + 1) * MW],
                            ALU.add, ALU.mult, ["modb", lnwk], ["modb"])

        ms = ExitStack()
        with ms:
            adab0 = [sb(f"adab{i}", [128, MW], F32, ms) for i in range(2)]
            adaw0 = [sb(f"adaw{i}", [128, KC, MW], BF16, ms) for i in range(2)]
            lnw0 = sb("lnw0", [128, D], F32, ms)
            cB0 = sb("cB0", [128, KC, 128], BF16, ms)
            tr.dma("sp", "d_k_lnw", lnw0[:], lnw_d[:, 0:D], reads=[], writes=["lnw0"])
            cp("dve", cB0[:], c_sb[:].unsqueeze(2).to_broadcast([128, KC, 128]), ["c_sb"], ["cB0"])
            adst0 = [sb(f"adst{i}", [128, KC, MW], F32, ms) for i in range(2)]
            compute_mod(0, adaw0, adab0, cB0, "cB0", lnw0, "lnw0", adst0)
            with nc.Block() as block:
                tr.emit(block)

        p1 = ExitStack()
        with p1:
            def sb1(name, shape, dt=F32):
                return sb(name, shape, dt, stack=p1)

            w_in = sb1("w_in", [128, KC, NIN], BF16)
            w_out = sb1("w_out", [128, KC, D], BF16)
            half_n = NIN // 2
            for k in range(KC):
                for pc in range(2):
                    tr.dma("pool", "d_win", w_in[:, k, pc * half_n:(pc + 1) * half_n],
                           win_d[k * 128:(k + 1) * 128, pc * half_n:(pc + 1) * half_n], reads=[], writes=["w_in"])
            for k in range(KC):
                tr.dma("pool", "d_wout", w_out[:, k, :], wout_d[k * 128:(k + 1) * 128, :], reads=[], writes=["w_out"])

            sel_b = sb1("sel_b", [128, 24 * 128], BF16)
            hp = sb1("hp", [128, 40])
            gnw = sb1("gnw", [128, 64])
            snw = sb1("snw", [128, 512])
            cw = sb1("cw", [128, 72])
            cbias = sb1("cbias", [128, 18])
            diagD = sb1("diagD", [128, 8, 128], BF16)
            for (dst, src, key) in [(hp, hp_d, "hp"), (gnw, gnw_d, "gnw"), (snw, snw_d, "snw"), (cw, cw_d, "cw"), (cbias, cb_d, "cbias")]:
                tr.dma("sp", "d_k_" + key, dst[:], src, reads=[], writes=[key])
            tr.dma("pool", "d_sel", sel_b[:, 0:1536], sel_d[:, 0:1536], reads=[], writes=["sel_b"])
            tr.dma("pool", "d_sel", sel_b[:, 1536:3072], sel_d[:, 1536:3072], reads=[], writes=["sel_b"])
            act(hp[:, 0:8], hp[:, 0:8], AF.Exp, ["hp"], ["hp"])
            act(hp[:, 16:24], hp[:, 16:24], AF.Exp, ["hp"], ["hp"])
            ts("dve", hp[:, 0:8], hp[:, 0:8], -1.0, None, ALU.mult, None, ["hp"], ["hp"])
            ts("dve", hp[:, 16:24], hp[:, 16:24], -1.0, None, ALU.mult, None, ["hp"], ["hp"])
            for h in range(8):
                ts("dve", diagD[:, h, :], ident_f, hp[:, 32 + h:33 + h], None, ALU.mult, None, ["hp", "cst_f"], ["diagD"])

            xt = [sb1(f"xt{i}", [128, D]) for i in range(1)]
            xr = [sb1(f"xr{i}", [128, D]) for i in range(1)]
            hbf = sb1("hbf", [128, D], BF16)
            hT = sb1("hT", [128, KC, GT], BF16)
            nstat = sb1("nstat", [128, 8])
            pre2 = sb1("pre2", [128, 2, GT + 3])
            pre = [pre2[:, 0, :], pre2[:, 1, :]]
            cacc = [sb1(f"cacc{i}", [128, GT]) for i in range(2)]
            halo = sb1("halo", [128, 18, 3])
            fm = [sb1(f"fm{i}", [128, 18, GT], BF16) for i in range(2)]
            sz = sb1("sz", [128, TPG, 1024], BF16)
            graw = sb1("graw", [128, TPG, 24])
            gscs = [sb1(f"gsc{i}", [128, 96]) for i in range(TPG)]
            TBs = [sb1(f"TB{i}", [128, 64]) for i in range(TPG)]
            CTbs = [sb1(f"CTb{i}", [128, 24]) for i in range(TPG)]
            SCs = [sb1(f"SC{i}", [128, 64]) for i in range(TPG)]
            RCs = [sb1(f"RC{i}", [128, 96], BF16) for i in range(TPG)]
            RCTs = [sb1(f"RCT{i}", [128, 256], BF16) for i in range(TPG)]
            kq_ms = [sb1(f"kq_m{i}", [128, 8, 2, 128], BF16) for i in range(TPG)]
            C_ms = [sb1(f"C_m{i}", [128, 2, 128], BF16) for i in range(TPG)]
            lnsss = [sb1(f"lnss{i}", [128, 16]) for i in range(TPG)]
            op_t = sb1("op_t", [128, D])
            sqf = op_t
            ssq = sb1("ssq", [128, 16])
            kgs = [sb1(f"kg{i}", [128, 512], BF16) for i in range(TPG)]
            kds = [sb1(f"kd{i}", [128, 512], BF16) for i in range(TPG)]
            Evs = [sb1(f"Ev{i}", [128, 512], BF16) for i in range(TPG)]
            xtoks = [sb1(f"xtok{i}", [128, 640], BF16) for i in range(TPG)]
            xws = [sb1(f"xw{i}", [128, 512], BF16) for i in range(TPG)]
            Eb = [sb1(f"Eb{i}", [128, 4, 128]) for i in range(2)]
            Eb3 = sb1("Eb3", [128, 4, 128])
            lm_b = sb1("lm_b", [128, 8, 128], BF16)
            tr.dma("pool", "d_lm", lm_b[:], lm_d.rearrange("p (l i) -> p l i", l=8), reads=[], writes=["lm_b"])
            Mp = [sb1(f"Mp{i}", [128, 8, 128], BF16) for i in range(2)]
            Np = [sb1(f"Np{i}", [128, 8, 128], BF16) for i in range(2)]
            Y = sb1("Y", [128, 8, 128], BF16)
            Yt = Mp[1]
            Pb = Np[1]
            qkT = sb1("qkT", [128, 8, 128], BF16)
            scT = sb1("scT", [128, 8, 128], BF16)
            nwT = sb1("nwT", [128, 8, 128], BF16)
            vn = sb1("vn", [128, 512], BF16)
            S_f = sb1("S_f", [128, 4, 64])
            S_b = sb1("S_b", [128, 4, 64], BF16)
            Sdec = sb1("Sdec", [128, 4, 64])
            H_f = sb1("H_f", [128, 256])
            H_b = sb1("H_b", [128, 256], BF16)
            Hdec = sb1("Hdec", [128, 256])
            o_t = sb1("o_t", [128, 512])
            o_f = sb1("o_f", [128, 512])
            y_t = sb1("y_t", [128, 512])
            y_f = sb1("y_f", [128, 512])
            mix = sb1("mix", [128, D], BF16)
            mixT = sb1("mixT", [128, KC, 128], BF16)
            print("phase1 sbuf remaining", nc.sbuf_bytes_remaining)

            tr.op("dve", lambda e: e.memset(halo[:], 0.0), [], ["halo"])
            for i_ in range(TPG):
                tr.op("dve", lambda e, b_=RCTs[i_]: e.memset(b_[:], 0.0), [], [("RCT", i_)])
            tr.op("dve", lambda e: e.memset(nwT[:], 0.0), [], [("nwT", 0, 0), ("nwT", 0, 1), ("nwT", 1, 0), ("nwT", 1, 1)])
            tr.op("dve", lambda e: e.memset(S_f[:], 0.0), [], ["S_f"])
            tr.op("dve", lambda e: e.memset(S_b[:], 0.0), [], ["S_b"])
            tr.op("dve", lambda e: e.memset(H_f[:], 0.0), [], ["H_f"])
            tr.op("dve", lambda e: e.memset(H_b[:], 0.0), [], ["H_b"])

            xctr = [0]

            def norm_to_hT(src_d, src_keys, tok0, hT_buf, hT_key, col0, slots, slot_names, hb, hbk, ns_, nsk):
                s = xctr[0] % len(slots)
                xctr[0] += 1
                xb = slots[s]
                xk = (slot_names, s)
                tr.dma("sp", f"{slot_names}{s}", xb[:], src_d[tok0:tok0 + 128, :], reads=src_keys, writes=[xk])
                act(hb[:], xb[:], AF.Square, [xk], [hbk, nsk], accum_out=ns_[:, 0:1])
                act(ns_[:, 1:2], ns_[:, 0:1], AF.Ln, [nsk], [nsk], bias=EPS, scale=1.0 / D)
                act(ns_[:, 2:3], ns_[:, 1:2], AF.Exp, [nsk], [nsk], scale=-0.5)
                return xb, xk

            def norm_finish(xb, xk, hT_buf, hT_key, col0, hb, hbk, ns_, nsk, dst=None, dstk=None, pool=0):
                dst = dst if dst is not None else xb
                dstk = dstk if dstk is not None else xk
                stt("dve", dst[:], xb[:], ns_[:, 2:3], modb[:, 1, :], ALU.mult, ALU.mult, [xk, nsk, "modb"], [dstk])
                tt("dve", hb[:], dst[:], modb[:, 0, :], ALU.add, [dstk, "modb"], [hbk])
                pt, pk = bank(pool)
                ptb = pt[:].bitcast(BF16)
                for k in range(KC):
                    tp(ptb[:, k * 128:(k + 1) * 128], hb[:, k * 128:(k + 1) * 128], ident_b, [hbk, "cst_b"], [pk])
                cp("act", hT_buf[:, :, col0:col0 + 128], ptb.rearrange("p (k t) -> p k t", k=KC), [pk], [hT_key])

            SENT = object()

            def fm_gen(g):
                gp_ = g % 2
                fmk_ = ("fm", gp_)
                F_ = fm[gp_]
                for t in range(TPG):
                    xb, xk = norm_to_hT(x_d, [], g * GT + t * 128, hT, "hT", t * 128, xt, "d_x", hbf, "hbf", nstat, "nstat")
                    norm_finish(xb, xk, hT, "hT", t * 128, hbf, "hbf", nstat, "nstat", pool=1)
                    yield
                for c0 in range(0, 18, 2):
                    pts = []
                    for c in (c0, c0 + 1):
                        pt, pk = bank(1)
                        for k in range(KC):
                            mm(pt[:, 0:GT], w_in[:, k, c * 128:(c + 1) * 128], hT[:, k, :], k == 0, k == KC - 1, ["w_in", "hT"], [pk])
                        pts.append((pt, pk))
                    cp("dve", pre2[:, :, 0:3], halo[:, c0:c0 + 2, :], ["halo"], [("preh", 0), ("preh", 1)])
                    for i, c in enumerate((c0, c0 + 1)):
                        pt, pk = pts[i]
                        pb, pbk, ca, cak = pre[i], ("pre", i), cacc[i], ("cacc", i)
                        cp("act", pb[:, 3:GT + 3], pt[:, 0:GT], [pk], [pbk])
                        tr.op("act", lambda e, ca=ca, pt=pt, c=c: e.activation(out=ca[:], in_=pt[:, 0:GT], func=AF.Copy, scale=cw[:, c * 4 + 3:c * 4 + 4]),
                              [pk, "cw"], [cak])
                    cp("dve", halo[:, c0:c0 + 2, :], pre2[:, :, GT:GT + 3], [("pre", 0), ("pre", 1)], ["halo"])
                    for i, c in enumerate((c0, c0 + 1)):
                        pb, pbk, ca, cak = pre[i], ("pre", i), cacc[i], ("cacc", i)
                        phk = ("preh", i)
                        stt("dve", ca[:], pb[:, 2:GT + 2], cw[:, c * 4 + 2:c * 4 + 3], ca[:], ALU.mult, ALU.add, [pbk, phk, cak, "cw"], [cak])
                        stt("dve", ca[:], pb[:, 1:GT + 1], cw[:, c * 4 + 1:c * 4 + 2], ca[:], ALU.mult, ALU.add, [pbk, phk, cak, "cw"], [cak])
                        stt("dve", ca[:], pb[:, 0:GT], cw[:, c * 4 + 0:c * 4 + 1], ca[:], ALU.mult, ALU.add, [pbk, phk, cak, "cw"], [cak])
                    for i, c in enumerate((c0, c0 + 1)):
                        ca, cak = cacc[i], ("cacc", i)
                        act(F_[:, c, :], ca[:], AF.Silu, [cak, "cbias"], [fmk_], bias=cbias[:, c:c + 1])
                    yield

            def tm_stage(g):
                for t in range(TPG):
                    cs = slice(t * 128, (t + 1) * 128)
                    for zi, off in enumerate((O_ZG, O_ZS)):
                        pt, pk = bank()
                        for k in range(KC):
                            mm(pt[:], hT[:, k, cs], w_in[:, k, off:off + 512], k == 0, k == KC - 1, ["w_in", "hT"], [pk])
                        act(sz[:, t, zi * 512:(zi + 1) * 512], pt[:], AF.Silu, [pk], [("sz", t)])
                    pt, pk = bank()
                    for k in range(KC):
                        mm(pt[:, 0:24], hT[:, k, cs], w_in[:, k, O_GATE:O_GATE + 24], k == 0, k == KC - 1, ["w_in", "hT"], [pk])
                    cp("dve", graw[:, t, :], pt[:, 0:24], [pk], [("graw", t)])

            for _ in fm_gen(0):
                pass
            pend = [iter(())]
            for g in range(KNOB["ng"]):
                gp = g % 2
                fmk = ("fm", gp)
                F = fm[gp]
                tm_stage(g)
                fmg = fm_gen(g + 1) if (g + 1 < KNOB["ng"] and KNOB.get("fm_il", True)) else iter(())

                for t in range(TPG):
                    ti = g * TPG + t
                    cs = slice(t * 128, (t + 1) * 128)
                    if not KNOB["mixer"]:
                        continue
                    gsc, TB, CTb, SC, RC, RCT = gscs[t], TBs[t], CTbs[t], SCs[t], RCs[t], RCTs[t]
                    kg, kd, Ev, xtok, xw, kq_m, C_m, lnss = kgs[t], kds[t], Evs[t], xtoks[t], xws[t], kq_ms[t], C_ms[t], lnsss[t]
                    gr = graw[:, t, :]
                    grk = ("graw", t)
                    act(gsc[:, 0:8], gr[:, 0:8], AF.Exp, [grk], [("gsc0", t)], scale=-1.0)
                    act(gsc[:, 8:16], gsc[:, 0:8], AF.Ln, [("gsc0", t)], [("gsc1", t)], bias=1.0)
                    tt("dve", gsc[:, 16:24], gr[:, 8:16], hp[:, 8:16], ALU.add, [grk, "hp"], [("gsc2", t)])
                    act(gsc[:, 24:32], gsc[:, 16:24], AF.Exp, [("gsc2", t)], [("gsc3", t)])
                    act(gsc[:, 32:40], gsc[:, 24:32], AF.Ln, [("gsc3", t)], [("gsc4", t)], bias=1.0)
                    tt("dve", gsc[:, 40:48], gr[:, 16:24], hp[:, 24:32], ALU.add, [grk, "hp"], [("gsc5", t)])
                    act(gsc[:, 48:56], gsc[:, 40:48], AF.Exp, [("gsc5", t)], [("gsc6", t)])
                    act(gsc[:, 56:64], gsc[:, 48:56], AF.Ln, [("gsc6", t)], [("gsc7", t)], bias=1.0)
                    act(gsc[:, 64:72], gsc[:, 56:64], AF.Ln, [("gsc7", t)], [("gsc8", t)])
                    tt("dve", gsc[:, 72:80], gsc[:, 32:40], hp[:, 0:8], ALU.mult, [("gsc4", t), "hp"], [("ga", t)])
                    tt("dve", gsc[:, 80:88], gsc[:, 56:64], hp[:, 16:24], ALU.mult, [("gsc7", t), "hp"], [("ga", t)])
                    pg, pgk = bank()
                    mm(pg[:, 0:16], triu_f, gsc[:, 72:88], True, True, ["cst_f", ("ga", t)], [pgk])
                    mm(pg[:, 16:32], ones_f, gsc[:, 72:88], True, True, ["cst_f", ("ga", t)], [pgk])
                    cp("dve", TB[:, 16:24], pg[:, 8:16], [pgk], [("TB2", t)])
                    cp("dve", TB[:, 48:64], pg[:, 16:32], [pgk], [("TB67", t)])
                    Gk = ("Gs", t)
                    cp("dve", gsc[:, 88:96], pg[:, 0:8], [pgk], [Gk])
                    G = gsc[:, 88:96]

                    pqk, pqkk = bank()
                    pqkb = pqk[:].bitcast(BF16)
                    for c in range(8):
                        tp(pqkb[:, c * 128:(c + 1) * 128], F[:, c, cs], ident_b, [fmk, "cst_b"], [pqkk])
                    pv, pvk = bank()
                    pvb = pv[:].bitcast(BF16)
                    for c in range(4):
                        tp(pvb[:, c * 128:(c + 1) * 128], F[:, 8 + c, cs], ident_b, [fmk, "cst_b"], [pvk])
                    px, pxk = bank()
                    pxb = px[:].bitcast(BF16)
                    for c in range(5):
                        tp(pxb[:, c * 128:(c + 1) * 128], F[:, 12 + c, cs], ident_b, [fmk, "cst_b"], [pxk])
                    act(sqf[:], pqkb[:, 0:1024], AF.Square, [pqkk], ["op_t"])
                    red("dve", lnss[:], v3(sqf[:], 16), ["op_t"], [("lnss", t)])
                    act(lnss[:], lnss[:], AF.Ln, [("lnss", t)], [("lnss", t)], bias=EPS)
                    lq = lnss[:, 0:8]
                    lk = lnss[:, 8:16]
                    tt("dve", gsc[:, 0:8], lk, gsc[:, 8:16], ALU.add, [("lnss", t), ("gsc1", t)], [("gsc0", t)])
                    stt("dve", TB[:, 8:16], gsc[:, 0:8], -0.5, G, ALU.mult, ALU.add, [("gsc0", t), Gk], [("TB1", t)])
                    stt("dve", TB[:, 0:8], lq, -0.5, G, ALU.mult, ALU.add, [("lnss", t), Gk], [("TB0", t)])
                    ts("dve", TB[:, 0:8], TB[:, 0:8], -math.log(8.0), None, ALU.add, None, [("TB0", t)], [("TB0", t)])
                    stt("dve", CTb[:, 8:16], G, -2.0, TB[:, 8:16], ALU.mult, ALU.add, [Gk, ("TB1", t)], [("CT1", t)])
                    stt("dve", CTb[:, 0:8], lk, -0.5, G, ALU.mult, ALU.subtract, [("lnss", t), Gk], [("CT0", t)])
                    tt("dve", CTb[:, 16:24], gsc[:, 64:72], TB[:, 16:24], ALU.subtract, [("gsc8", t), ("TB2", t)], [("CT2", t)])
                    tt("dve", TB[:, 24:32], CTb[:, 0:8], TB[:, 48:56], ALU.add, [("CT0", t), ("TB67", t)], [("TB3", t)])
                    ts("dve", TB[:, 32:40], gsc[:, 8:16], -0.5, None, ALU.mult, None, [("gsc1", t)], [("TB4", t)])
                    tt("dve", TB[:, 40:48], CTb[:, 16:24], TB[:, 56:64], ALU.add, [("CT2", t), ("TB67", t)], [("TB5", t)])
                    TBk = [("TB0", t), ("TB1", t), ("TB2", t), ("TB3", t), ("TB4", t), ("TB5", t), ("TB67", t)]
                    act(SC[:], TB[:], AF.Exp, TBk, [("SC", t)])
                    cp("dve", RC[:, 0:24], TB[:, 0:24], [("TB0", t), ("TB1", t), ("TB2", t)], [("RC0", t)])
                    tt("dve", RC[:, 24:48], TB[:, 0:24], RC[:, 0:24], ALU.subtract, [("TB0", t), ("TB1", t), ("TB2", t), ("RC0", t)], [("RC1", t)])
                    cp("dve", RC[:, 48:72], CTb[:], [("CT0", t), ("CT1", t), ("CT2", t)], [("RC2", t)])
                    tt("dve", RC[:, 72:96], CTb[:], RC[:, 48:72], ALU.subtract, [("CT0", t), ("CT1", t), ("CT2", t), ("RC2", t)], [("RC3", t)])
                    prc, prck = bank()
                    prcb = prc[:].bitcast(BF16)
                    tp(prcb[0:48, 0:128], RC[:, 0:48], ident_b, [("RC0", t), ("RC1", t), "cst_b"], [prck])
                    tp(prcb[0:48, 128:256], RC[:, 48:96], ident_b, [("RC2", t), ("RC3", t), "cst_b"], [prck])
                    cp("act", RCT[0:48, :], prcb[0:48, 0:256], [prck], [("RCT", t)])
                    tt("dve", v3(kg[:], 8), v3(pqkb[:, 512:1024], 8), bc_h(SC[:, 8:16], 64), ALU.mult, [pqkk, ("SC", t)], [("kg", t)])
                    tt("dve", v3(kd[:], 8), v3(pqkb[:, 512:1024], 8), bc_h(SC[:, 24:32], 64), ALU.mult, [pqkk, ("SC", t)], [("kd", t)])
                    tt("dve", v3(Ev[:], 8), v3(pvb[:, 0:512], 8), bc_h(SC[:, 32:40], 64), ALU.mult, [pvk, ("SC", t)], [("Ev", t)])
                    cp("act", xtok[:], pxb[:, 0:640], [pxk], [("xtok", t)])
                    tt("dve", v3(xw[:], 8), v3(pxb[:, 0:512], 8), bc_h(SC[:, 40:48], 64), ALU.mult, [pxk, ("SC", t)], [("xw", t)])

                    for h2 in range(2):
                        msk = cst_f[:, 640 + h2:641 + h2]
                        tr.op("act", lambda e, h2=h2, msk=msk, F=F, cs=cs, kq_m=kq_m: e.activation(out=kq_m[:, :, h2, :], in_=F[:, 0:8, cs], func=AF.Copy, scale=msk),
                              [fmk, "cst_f"], [("kq_m", t, h2)])
                        ts("dve", C_m[:, h2, :], F[:, 17, cs], msk, None, ALU.mult, None, [fmk, "cst_f"], [("C_m", t, h2)])
                for t in range(TPG):
                    ti = g * TPG + t
                    cs = slice(t * 128, (t + 1) * 128)
                    if not KNOB["mixer"]:
                        s = 0
                        tr.dma("sp", f"d_xr{s}", xr[s][:], x_d[ti * 128:(ti + 1) * 128, :], reads=[], writes=[("xr", s)])
                        tr.dma("sp", f"d_st{s}", x1_d[ti * 128:(ti + 1) * 128, :], xr[s][:], reads=[("xr", s)], writes=[("x1d", ti)])
                        continue
                    SC, RCT = SCs[t], RCTs[t]
                    kg, kd, Ev, xtok, xw, kq_m, C_m = kgs[t], kds[t], Evs[t], xtoks[t], xws[t], kq_ms[t], C_ms[t]
                    KQM = [("kq_m", t, 0), ("kq_m", t, 1)]
                    CM = [("C_m", t, 0), ("C_m", t, 1)]
                    def ssd_chain():
                        pcb, pcbk = bank(1)
                        for g2 in range(2):
                            mm(pcb[:, g2 * 128:(g2 + 1) * 128], F[:, 16, cs], C_m[:, g2, :], True, True, [fmk] + CM, [pcbk])
                        for hh in range(2):
                            hs = slice(hh * 4, (hh + 1) * 4)
                            pe_, pek = bank(1)
                            for h4 in range(4):
                                h = hh * 4 + h4
                                selrh = sel_b[:, (16 + h) * 128:(16 + h + 1) * 128]
                                o = pe_[:, h4 * 128:(h4 + 1) * 128]
                                mm(o, selrh, RCT[:, 0:128], True, False, ["sel_b", ("RCT", t)], [pek])
                                mm(o, RCT[:, 128:256], selrh, False, False, ["sel_b", ("RCT", t)], [pek])
                                mm(o, ident_b, mask_i_b, False, True, ["cst_b"], [pek])
                            act(Eb3[:], v3(pe_[:], 4), AF.Exp, [pek], ["Eb3"])
                            tt("dve", scT[:, hs, :], pcb[:, hh * 128:(hh + 1) * 128].unsqueeze(1).to_broadcast([128, 4, 128]),
                               Eb3[:], ALU.mult, [pcbk, "Eb3"], [("scT", hh)])
                            yield
                        pyd, pydk = bank(1)
                        for h in range(8):
                            o = pyd[:, h * 64:(h + 1) * 64]
                            mm(o, scT[:, h, :], xtok[:, h * 64:(h + 1) * 64], True, False, [("scT", h // 4), ("xtok", t)], [pydk])
                            mm(o, diagD[:, h, :], xtok[:, h * 64:(h + 1) * 64], False, True, ["diagD", ("xtok", t)], [pydk])
                        pyo, pyok = bank(1)
                        phn, phnk = bank(1)
                        for g2 in range(2):
                            rs = slice(64 * g2, 64 * g2 + 64)
                            mm(pyo[:, g2 * 256:(g2 + 1) * 256], C_m[:, g2, :], H_b[:, :], True, True, CM + ["H_b"], [pyok])
                        mm(phn[:], xtok[:, 512:640], xw[:], True, True, [("xtok", t), ("xw", t)], [phnk])
                        for g2 in range(2):
                            rs = slice(64 * g2, 64 * g2 + 64)
                            tt("pool", v3(Hdec[rs], 4), v3(H_f[rs], 4), bc_h(SC[rs, 56 + 4 * g2:60 + 4 * g2], 64), ALU.mult, ["H_f", ("SC", t)], ["Hdec"])
                        yield
                        tt("dve", v3(y_t[:], 8), v3(pyo[:], 8), bc_h(SC[:, 16:24], 64), ALU.mult, [pyok, ("SC", t)], ["y_t"])
                        for g2 in range(2):
                            rs = slice(64 * g2, 64 * g2 + 64)
                            src = phn[rs, g2 * 256:(g2 + 1) * 256]
                            tt("dve", H_b[rs], Hdec[rs], src, ALU.add, ["Hdec", phnk], ["H_b"])
                            tt("dve", H_f[rs], Hdec[rs], src, ALU.add, ["Hdec", phnk], ["H_f"])
                        yield
                        tt("dve", y_f[:], y_t[:], pyd[:], ALU.add, ["y_t", pydk], ["y_f"])
                        if ti in (0, 1):
                            dbg(f"yf_{ti}", y_f[:], "y_f")
                        tt("pool", y_f[:], y_f[:], sz[:, t, 512:1024], ALU.mult, ["y_f", ("sz", t)], ["y_f"])
                        yield
                        for g2 in range(2):
                            act(y_t[:, g2 * 256:(g2 + 1) * 256], y_f[:, g2 * 256:(g2 + 1) * 256], AF.Square, ["y_f"], ["y_t", "ssq"],
                                accum_out=ssq[:, 8 + g2:9 + g2])
                        act(ssq[:, 8:10], ssq[:, 8:10], AF.Ln, ["ssq"], ["ssq"], bias=EPS, scale=1.0 / 256)
                        act(ssq[:, 8:10], ssq[:, 8:10], AF.Exp, ["ssq"], ["ssq"], scale=-0.5)
                        yield
                        tt("dve", v3(y_t[:], 2), v3(y_f[:], 2), ssq[:, 8:10].unsqueeze(2).to_broadcast([128, 2, 256]), ALU.mult, ["y_f", "ssq"], ["y_t"])
                        tt("pool", mix[:, 512:1024], y_t[:], snw[:], ALU.mult, ["y_t", "snw"], ["mix"])

                    ssd = ssd_chain()

                    def side():
                        if next(pend[0], SENT) is SENT:
                            if next(ssd, SENT) is SENT:
                                next(fmg, None)
                    pkk = [bank(), bank()]
                    pkq = [bank(), bank()]
                    for hh in range(2):
                        a_, ak = pkk[hh]
                        b_, bk = pkq[hh]
                        for h4 in range(4):
                            h = hh * 4 + h4
                            kTc = F[:, 4 + h // 2, cs]
                            mm(a_[:, h4 * 128:(h4 + 1) * 128], kTc, kq_m[:, 4 + h // 2, h % 2, :], True, True, [fmk] + KQM, [ak])
                            mm(b_[:, h4 * 128:(h4 + 1) * 128], kTc, kq_m[:, h // 2, h % 2, :], True, True, [fmk] + KQM, [bk])
                    ectr = 0
                    for r, mk in ((1, mask_s_b), (0, mask_i_b)):
                        for hh in range(2):
                            hs = slice(hh * 4, (hh + 1) * 4)
                            pe_, pek = bank()
                            for h4 in range(4):
                                h = hh * 4 + h4
                                selrh = sel_b[:, (r * 8 + h) * 128:(r * 8 + h + 1) * 128]
                                o = pe_[:, h4 * 128:(h4 + 1) * 128]
                                mm(o, selrh, RCT[:, 0:128], True, False, ["sel_b", ("RCT", t)], [pek])
                                mm(o, RCT[:, 128:256], selrh, False, False, ["sel_b", ("RCT", t)], [pek])
                                mm(o, ident_b, mk, False, True, ["cst_b"], [pek])
                            E = Eb[ectr % 2]
                            Ek = ("Eb", ectr % 2)
                            ectr += 1
                            act(E[:], v3(pe_[:], 4), AF.Exp, [pek], [Ek])
                            if r == 1:
                                stt("dve", Mp[0][:, hs, :], v3(pkk[hh][0][:], 4), -1.0, E[:], ALU.mult, ALU.mult, [pkk[hh][1], Ek], [("Mp", 0, hh)])
                                if ti == 0 and hh == 0:
                                    dbg("E1_0", E[:].rearrange("p h i -> p (h i)"), Ek)
                            else:
                                tt("dve", qkT[:, hs, :], v3(pkq[hh][0][:], 4), E[:], ALU.mult, [pkq[hh][1], Ek], [("qkT", hh)])
                            side()
                    if KNOB.get("mstop") == 3:
                        s = 0
                        tr.dma("sp", f"d_xr{s}", xr[s][:], x_d[ti * 128:(ti + 1) * 128, :], reads=[], writes=[("xr", s)])
                        tr.dma("sp", f"d_st{s}", x1_d[ti * 128:(ti + 1) * 128, :], xr[s][:], reads=[("xr", s)], writes=[("x1d", ti)])
                        continue
                    bc4 = lambda a: a.unsqueeze(1).to_broadcast([128, 4, 128])
                    for hh in range(2):
                        hs = slice(hh * 4, (hh + 1) * 4)
                        pn, pnk = bank()
                        pnb = pn[:].bitcast(BF16)
                        for h4 in range(4):
                            tp(pnb[:, h4 * 128:(h4 + 1) * 128], Mp[0][:, hh * 4 + h4, :], ident_b, [("Mp", 0, hh), "cst_b"], [pnk])
                        cp("act", Np[0][:, hs, :], v3(pnb[:, 0:512], 4), [pnk], [("Np", 0, hh)])
                        tt("dve", Y[:, hs, :], Mp[0][:, hs, :], bc4(lm_b[:, 1, :]), ALU.mult, [("Mp", 0, hh), "lm_b"], [("Y", hh)])
                        tt("dve", Y[:, hs, :], Y[:, hs, :], bc4(ident_b), ALU.add, [("Y", hh), "cst_b"], [("Y", hh)])
                        tt("dve", Yt[:, hs, :], Np[0][:, hs, :], bc4(lm_b[:, 0, :]), ALU.mult, [("Np", 0, hh), "lm_b"], [("Yt", hh)])
                        tt("dve", Yt[:, hs, :], Yt[:, hs, :], bc4(ident_b), ALU.add, [("Yt", hh), "cst_b"], [("Yt", hh)])
                        side()
                    for lvl in range(1, 7):
                        pPs = []
                        for hh in range(2):
                            pP, pPk = bank()
                            for h4 in range(4):
                                mm(pP[:, h4 * 128:(h4 + 1) * 128], Np[0][:, hh * 4 + h4, :], Y[:, hh * 4 + h4, :], True, True, [("Np", 0, hh), ("Y", hh)], [pPk])
                            pPs.append((pP, pPk))
                        for hh in range(2):
                            hs = slice(hh * 4, (hh + 1) * 4)
                            tt("dve", Pb[:, hs, :], v3(pPs[hh][0][:], 4), bc4(lm_b[:, 1 + lvl, :]), ALU.mult, [pPs[hh][1], "lm_b"], [("Pb", hh)])
                        side()
                        upd = []
                        for hh in range(2):
                            Ytk, Pk = ("Yt", hh), ("Pb", hh)
                            pY, pYk = bank()
                            for h4 in range(4):
                                h = hh * 4 + h4
                                o = pY[:, h4 * 128:(h4 + 1) * 128]
                                mm(o, ident_b, Y[:, h, :], True, False, ["cst_b", ("Y", hh)], [pYk])
                                mm(o, Yt[:, h, :], Pb[:, h, :], False, True, [Ytk, Pk], [pYk])
                            pYt, pYtk = (None, None)
                            if lvl < 6:
                                pYt, pYtk = bank()
                                for h4 in range(4):
                                    h = hh * 4 + h4
                                    o = pYt[:, h4 * 128:(h4 + 1) * 128]
                                    mm(o, Pb[:, h, :], Yt[:, h, :], True, True, [Ytk, Pk], [pYtk])
                            upd.append((pY, pYk, pYt, pYtk))
                        for hh in range(2):
                            hs = slice(hh * 4, (hh + 1) * 4)
                            pY, pYk, pYt, pYtk = upd[hh]
                            cp("act", Y[:, hs, :], v3(pY[:], 4), [pYk], [("Y", hh)])
                            if lvl < 6:
                                tt("dve", Yt[:, hs, :], Yt[:, hs, :], v3(pYt[:], 4), ALU.add, [("Yt", hh), pYtk], [("Yt", hh)])
                        side()
                    if KNOB.get("mstop") == 5:
                        s = 0
                        tr.dma("sp", f"d_xr{s}", xr[s][:], x_d[ti * 128:(ti + 1) * 128, :], reads=[], writes=[("xr", s)])
                        tr.dma("sp", f"d_st{s}", x1_d[ti * 128:(ti + 1) * 128, :], xr[s][:], reads=[("xr", s)], writes=[("x1d", ti)])
                        continue
                    for hh in range(2):
                        pw, pwk = bank()
                        for h4 in range(4):
                            h = hh * 4 + h4
                            pr = h // 2
                            mm(pw[:, h4 * 128:(h4 + 1) * 128], kg[:, pr * 128:(pr + 1) * 128], Y[:, h, :], True, True, [("kg", t), ("Y", hh)], [pwk])
                        pw4 = v3(pw[:], 4)
                        tr.op("act", lambda e, pw4=pw4, hh=hh: e.mul(out=nwT[0:64, hh * 4:hh * 4 + 4:2, :], in_=pw4[0:64, 0::2, :], mul=-1.0), [pwk], [("nwT", hh, 0)])
                        tr.op("act", lambda e, pw4=pw4, hh=hh: e.mul(out=nwT[64:128, hh * 4 + 1:hh * 4 + 4:2, :], in_=pw4[64:128, 1::2, :], mul=-1.0), [pwk], [("nwT", hh, 1)])
                    if KNOB.get("mstop") == 6:
                        s = 0
                        tr.dma("sp", f"d_xr{s}", xr[s][:], x_d[ti * 128:(ti + 1) * 128, :], reads=[], writes=[("xr", s)])
                        tr.dma("sp", f"d_st{s}", x1_d[ti * 128:(ti + 1) * 128, :], xr[s][:], reads=[("xr", s)], writes=[("x1d", ti)])
                        continue
                    for _ in pend[0]:
                        pass
                    Yks = [("Y", 0), ("Y", 1)]
                    pvn, pvnk = bank()
                    for h in range(8):
                        po = 64 * (h % 2)
                        o = pvn[:, h * 64:(h + 1) * 64]
                        mm(o, Y[:, h, :], Ev[:, h * 64:(h + 1) * 64], True, False, Yks + [("Ev", t)], [pvnk])
                        mm(o, nwT[:, h, :], S_b[:, h // 2, :], False, True, [("nwT", h // 4, h % 2), "S_b"], [pvnk])
                    po1, po1k = bank()
                    for h in range(8):
                        po = 64 * (h % 2)
                        mm(po1[:, h * 64:(h + 1) * 64], kq_m[:, h // 2, h % 2, :], S_b[:, h // 2, :], True, True, KQM + ["S_b"], [po1k])
                    tt("dve", v3(vn[:], 8), v3(pvn[:], 8), bc_h(SC[:, 32:40], 64), ALU.mult, [pvnk, ("SC", t)], ["vn"])
                    side()
                    glg = SC[:, 48:56].rearrange("p (pr two) -> p pr two", two=2)
                    for h2 in range(2):
                        rs = slice(64 * h2, 64 * h2 + 64)
                        tt("pool", Sdec[rs], S_f[rs], glg[rs, :, h2].unsqueeze(2).to_broadcast([64, 4, 64]), ALU.mult, ["S_f", ("SC", t)], ["Sdec"])
                    po2, po2k = bank()
                    psn, psnk = bank()
                    for h in range(8):
                        mm(po2[:, h * 64:(h + 1) * 64], qkT[:, h, :], vn[:, h * 64:(h + 1) * 64], True, True, [("qkT", h // 4), "vn"], [po2k])
                    for pr in range(4):
                        mm(psn[:, pr * 128:(pr + 1) * 128], kd[:, pr * 128:(pr + 1) * 128], vn[:, pr * 128:(pr + 1) * 128], True, True, [("kd", t), "vn"], [psnk])
                    psn4 = v3(psn[:], 4)
                    for h2 in range(2):
                        rs = slice(64 * h2, 64 * h2 + 64)
                        src = psn4[rs, :, 64 * h2:64 * h2 + 64]
                        tt("dve", S_b[rs], Sdec[rs], src, ALU.add, ["Sdec", psnk], ["S_b"])
                        tt("dve", S_f[rs], Sdec[rs], src, ALU.add, ["Sdec", psnk], ["S_f"])
                    tt("dve", v3(o_t[:], 8), v3(po1[:], 8), bc_h(SC[:, 0:8], 64), ALU.mult, [po1k, ("SC", t)], ["o_t"])
                    tt("dve", o_f[:], o_t[:], po2[:], ALU.add, ["o_t", po2k], ["o_f"])
                    side()
                    if ti in (0, 1):
                        dbg(f"of_{ti}", o_f[:], "o_f")
                        dbg(f"SC_{ti}", SC[:], ("SC", t))
                    if KNOB.get("mstop") == 7:
                        s = 0
                        tr.dma("sp", f"d_xr{s}", xr[s][:], x_d[ti * 128:(ti + 1) * 128, :], reads=[], writes=[("xr", s)])
                        tr.dma("sp", f"d_st{s}", x1_d[ti * 128:(ti + 1) * 128, :], xr[s][:], reads=[("xr", s)], writes=[("x1d", ti)])
                        continue
                    act(sqf[:, 0:512], o_f[:], AF.Square, ["o_f"], ["op_t"])
                    red("dve", ssq[:, 0:8], v3(sqf[:, 0:512], 8), ["op_t"], ["ssq"])
                    act(ssq[:, 0:8], ssq[:, 0:8], AF.Ln, ["ssq"], ["ssq"], bias=EPS, scale=1.0 / 64)
                    act(ssq[:, 0:8], ssq[:, 0:8], AF.Exp, ["ssq"], ["ssq"], scale=-0.5)
                    tt("dve", v3(o_t[:], 8), v3(o_f[:], 8), bc_h(ssq[:, 0:8], 64), ALU.mult, ["o_f", "ssq"], ["o_t"])
                    tt("pool", v3(o_t[:], 8), v3(o_t[:], 8), gnw[:].unsqueeze(1).to_broadcast([128, 8, 64]), ALU.mult, ["o_t", "gnw"], ["o_t"])
                    tt("pool", mix[:, 0:512], o_t[:], sz[:, t, 0:512], ALU.mult, ["o_t", ("sz", t)], ["mix"])

                    if KNOB.get("mstop") == 8:
                        s = 0
                        tr.dma("sp", f"d_xr{s}", xr[s][:], x_d[ti * 128:(ti + 1) * 128, :], reads=[], writes=[("xr", s)])
                        tr.dma("sp", f"d_st{s}", x1_d[ti * 128:(ti + 1) * 128, :], xr[s][:], reads=[("xr", s)], writes=[("x1d", ti)])
                        continue
                    if KNOB.get("mstop") == 9:
                        s = 0
                        tr.dma("sp", f"d_xr{s}", xr[s][:], x_d[ti * 128:(ti + 1) * 128, :], reads=[], writes=[("xr", s)])
                        tr.dma("sp", f"d_st{s}", x1_d[ti * 128:(ti + 1) * 128, :], xr[s][:], reads=[("xr", s)], writes=[("x1d", ti)])
                        continue
                    for _ in ssd:
                        pass
                    def tail_chain(ti=ti):
                        s_ = 0
                        pmt, pmtk = bank(1)
                        pmtb = pmt[:].bitcast(BF16)
                        for c in range(KC):
                            tp(pmtb[:, c * 128:(c + 1) * 128], mix[:, c * 128:(c + 1) * 128], ident_b, ["mix", "cst_b"], [pmtk])
                        cp("act", mixT[:], pmtb.rearrange("p (k t) -> p k t", k=KC), [pmtk], ["mixT"])
                        tr.dma("sp", f"d_xr{s_}", xr[s_][:], x_d[ti * 128:(ti + 1) * 128, :], reads=[], writes=[("xr", s_)])
                        yield
                        for n in range(2):
                            pop, popk = bank(1)
                            for c in range(KC):
                                mm(pop[:], mixT[:, c, :], w_out[:, c, n * 512:(n + 1) * 512], c == 0, c == KC - 1, ["mixT", "w_out"], [popk])
                            ns = slice(n * 512, (n + 1) * 512)
                            tt("dve", op_t[:, ns], pop[:], modb[:, 2, ns], ALU.mult, [popk, "modb"], ["op_t"])
                            tt("pool", xr[s_][:, ns], op_t[:, ns], xr[s_][:, ns], ALU.add, ["op_t", ("xr", s_)], [("xr", s_)])
                            if n == 0:
                                yield
                        tr.dma("sp", f"d_st{s_}", x1_d[ti * 128:(ti + 1) * 128, :], xr[s_][:], reads=[("xr", s_)], writes=[("x1d", ti)])
                        if ti in (0, 1):
                            dbg(f"x1_{ti}", xr[s_][:], ("xr", s_))

                    for _ in pend[0]:
                        pass
                    pend[0] = tail_chain()
                if not KNOB.get("fm_il", True) and g + 1 < KNOB["ng"]:
                    fmg = fm_gen(g + 1)
                for _ in fmg:
                    pass
            for _ in pend[0]:
                pass
            with nc.Block() as block:
                tr.emit(block)

        p2 = ExitStack()
        with p2:
            def sb2(name, shape, dt=F32):
                return sb(name, shape, dt, stack=p2)

            W1 = sb2("W1", [128, KC, DFF], BF16)
            W2 = sb2("W2", [128, 32, D], BF16)
            ms = ExitStack()
            with ms:
                adab1 = [sb(f"adabb{i}", [128, MW], F32, ms) for i in range(2)]
                adaw1 = [sb(f"adawb{i}", [128, KC, MW], BF16, ms) for i in range(2)]
                lnw1 = sb("lnw1", [128, D], F32, ms)
                cB1 = sb("cB1", [128, KC, 128], BF16, ms)
                tr.dma("sp", "d_k_lnw1", lnw1[:], lnw_d[:, D:2 * D], reads=[], writes=["lnw1"])
                cp("dve", cB1[:], c_sb[:].unsqueeze(2).to_broadcast([128, KC, 128]), ["c_sb"], ["cB1"])
                adst1 = [sb(f"adstb{i}", [128, KC, MW], F32, ms) for i in range(2)]
                compute_mod(3, adaw1, adab1, cB1, "cB1", lnw1, "lnw1", adst1)
                for pc in range(4):
                    for k in range(KC):
                        tr.dma("pool", f"d_w1_{pc}", W1[:, k, pc * 1024:(pc + 1) * 1024], w1_d[k * 128:(k + 1) * 128, pc * 1024:(pc + 1) * 1024],
                               reads=[], writes=[("W1", pc)])
                for k in range(32):
                    tr.dma("pool", "d_w2", W2[:, k, :], w2_d[k * 128:(k + 1) * 128, :], reads=[], writes=["W2"])
                with nc.Block() as block:
                    tr.emit(block)
            lnwf = sb2("lnwf", [128, D])
            tr.dma("sp", "d_k_lnwf", lnwf[:], lnw_d[:, 2 * D:3 * D], reads=[], writes=["lnwf"])
            x1k = [sb2(f"x1k{i}", [128, D]) for i in range(4)]
            hbf2 = sb2("hbf2", [128, D], BF16)
            h2T = [sb2(f"h2T{i}", [128, KC, G2T], BF16) for i in range(2)]
            nst2 = sb2("nst2", [128, 8])
            rl = [sb2(f"rl{i}", [128, G2T]) for i in range(2)]
            aT = sb2("aT", [128, 32, G2T], BF16)
            f_t = sb2("f_t", [128, D])
            hn2 = sb2("hn2", [128, D])
            print("phase2 sbuf remaining", nc.sbuf_bytes_remaining)

            def p2_norm(g):
                tiles = []
                hb = h2T[g % 2]
                hk = ("h2T", g % 2)
                for t in range(2):
                    ti = g * 2 + t
                    xb, xk = norm_to_hT(x1_d, [("x1d", ti)], ti * 128, hb, hk, t * 128, x1k, "d_p2x", hbf2, "hbf2", nst2, "nst2")
                    norm_finish(xb, xk, hb, hk, t * 128, hbf2, "hbf2", nst2, "nst2", dst=hn2, dstk="hn2")
                    tiles.append((xb, xk))
                return tiles

            def p2_ffn1(g):
                hb = h2T[g % 2]
                hk = ("h2T", g % 2)
                for c in range(32):
                    pt, pk = bank()
                    for k in range(KC):
                        mm(pt[:, 0:G2T], W1[:, k, c * 128:(c + 1) * 128], hb[:, k, :], k == 0, k == KC - 1, [("W1", c // 8), hk], [pk])
                    r_ = rl[c % 2]
                    rk = ("rl", c % 2)
                    act(r_[:], pt[:, 0:G2T], AF.Relu, [pk], [rk])
                    tt("pool" if c % 2 else "dve", aT[:, c, :], r_[:], r_[:], ALU.mult, [rk], [("aT", c)])

            def p2_ffn2(g, tiles):
                for t in range(2):
                    ti = g * 2 + t
                    xb, xk = tiles[t]
                    for n in range(2):
                        pt, pk = bank()
                        for c in range(32):
                            mm(pt[:], aT[:, c, t * 128:(t + 1) * 128], W2[:, c, n * 512:(n + 1) * 512], c == 0, c == 31, [("aT", c), "W2"], [pk])
                        ns = slice(n * 512, (n + 1) * 512)
                        tt("dve", f_t[:, ns], pt[:], modb[:, 2, ns], ALU.mult, [pk, "modb"], ["f_t"])
                        tt("pool", xb[:, ns], f_t[:, ns], xb[:, ns], ALU.add, ["f_t", xk], [xk])
                    act(hbf2[:], xb[:], AF.Square, [xk], ["hbf2", "nst2b"], accum_out=nst2[:, 4:5])
                    act(nst2[:, 5:6], nst2[:, 4:5], AF.Ln, ["nst2b"], ["nst2b"], bias=EPS, scale=1.0 / D)
                    act(nst2[:, 6:7], nst2[:, 5:6], AF.Exp, ["nst2b"], ["nst2b"], scale=-0.5)
                    stt("dve", xb[:], xb[:], nst2[:, 6:7], lnwf[:], ALU.mult, ALU.mult, [xk, "nst2b", "lnwf"], [xk])
                    slot = xk[1]
                    tr.dma("sp", f"d_po{slot}", out_d[ti * 128:(ti + 1) * 128, :], xb[:], reads=[xk], writes=[("outd", slot)])

            n2 = KNOB["ng2"]
            nxt_tiles = p2_norm(0) if n2 > 0 else None
            for g in range(n2):
                tiles = nxt_tiles
                p2_ffn1(g)
                if g + 1 < n2:
                    nxt_tiles = p2_norm(g + 1)
                p2_ffn2(g, tiles)
            tr.final_wait("sp", [("outd", i) for i in range(4)] + [("dbgout", n) for n in dbg_d])
            print("sem counts", {k: v for k, v in tr.cnt.items()})
            with nc.Block() as block:
                tr.emit(block)
    return nc


def host_constants():
    ident = np.eye(128, dtype=np.float32)
    k = np.arange(128)
    triu = (k[:, None] <= k[None, :]).astype(np.float32)
    ones = np.ones((128, 128), np.float32)
    mask_s = np.where(k[None, :] > k[:, None], 0.0, NEG).astype(np.float32)
    mask_i = np.where(k[None, :] >= k[:, None], 0.0, NEG).astype(np.float32)
    m0 = (k < 64).astype(np.float32)[:, None]
    cst = np.concatenate([ident, triu, ones, mask_s, mask_i, m0, 1.0 - m0], axis=1)
    sel = np.zeros((128, 24, 128), np.float32)
    for r in range(3):
        for h in range(8):
            sel[r * 8 + h, r * 8 + h, :] = 1.0
            sel[24 + r * 8 + h, r * 8 + h, :] = 1.0
    def lmask(b):
        i = k[:, None]
        j = k[None, :]
        return ((i // (2 * b) == j // (2 * b)) & ((i // b) % 2 == 1) & ((j // b) % 2 == 0)).astype(np.float32)
    lms = [lmask(1)] + [lmask(b).T for b in (1, 2, 4, 8, 16, 32, 64)]
    lm = np.concatenate(lms, axis=1)
    return np.ascontiguousarray(cst), np.ascontiguousarray(sel.reshape(128, 24 * 128)), np.ascontiguousarray(lm)


def prep_inputs(inputs):
    f = lambda a: np.ascontiguousarray(np.asarray(a, dtype=np.float32))
    w_in = f(inputs["w_in"])[0]
    perm = np.concatenate([np.arange(0, 1536), np.arange(2064, 2832), np.arange(1536, 2048), np.arange(2832, 3344),
                           np.arange(2048, 2056), np.arange(2056, 2064), np.arange(3344, 3352)])
    w_in_r = np.ascontiguousarray(w_in[:, perm])
    gcw = f(inputs["gdn_conv_w"])[0]
    scw = f(inputs["ssm_conv_w"])[0]
    allw = np.concatenate([gcw, scw], axis=1)
    cw = np.ascontiguousarray(allw.reshape(4, 18, 128).transpose(2, 1, 0).reshape(128, 72))
    cb_full = np.concatenate([np.zeros(1536, np.float32), f(inputs["ssm_conv_b"])[0]])
    cb = np.ascontiguousarray(cb_full.reshape(18, 128).T)
    hp_row = np.concatenate([f(inputs["gdn_A_log"])[0], f(inputs["gdn_dt_bias"])[0], f(inputs["ssm_A_log"])[0],
                             f(inputs["ssm_dt_bias"])[0], f(inputs["ssm_D"])[0]])
    bc = lambda row: np.ascontiguousarray(np.broadcast_to(row[None, :], (128, row.shape[0])))
    lnw_row = np.concatenate([f(inputs["ln1_w"])[0], f(inputs["ln2_w"])[0], f(inputs["final_norm_w"])])
    cst, sel, lm = host_constants()
    shared = {
        "w_in_r": w_in_r, "w_out": f(inputs["w_out"])[0], "w_ff1": f(inputs["w_ff1"])[0], "w_ff2": f(inputs["w_ff2"])[0],
        "ada_w": f(inputs["ada_w"])[0], "ada_b_b": bc(f(inputs["ada_b"])[0]), "lnw_b": bc(lnw_row), "cw": cw, "cb": cb,
        "hp": bc(hp_row), "gnw": bc(f(inputs["gdn_norm_w"])[0]), "snw": bc(f(inputs["ssm_norm_w"])[0]),
        "cst": cst, "sel": sel, "lm": lm,
    }
    x = f(inputs["x"])
    c = f(inputs["c"])
    in_maps = []
    for b in range(NCORE):
        m = dict(shared)
        m["x"] = np.ascontiguousarray(x[b])
        m["c_l"] = np.ascontiguousarray(c[b].reshape(KC, 128).T)
        in_maps.append(m)
    return in_maps


def kernel(**inputs):
    in_maps = prep_inputs(inputs)
    nc = build_program(DEBUG)
    res = run_bass_kernel_spmd(nc, in_maps, core_ids=list(range(NCORE)))
    out = np.stack([np.asarray(r["out"], dtype=np.float32) for r in res.results], axis=0)
    if DEBUG:
        kernel.debug = [{k: np.asarray(v) for k, v in r.items() if k.startswith("dbg_")} for r in res.results]
    return out
```

```python
import math
from contextlib import ExitStack
import numpy as np
import concourse.bass as bass
import concourse.mybir as mybir
from concourse.bass_utils import run_bass_kernel_spmd

F32 = mybir.dt.float32
BF16 = mybir.dt.bfloat16
AF = mybir.ActivationFunctionType
ALU = mybir.AluOpType
AX = mybir.AxisListType

L = 4096
D = 1024
KC = 8
NCORE = 8
GT = 256
NG = L // GT
TPG = GT // 128
DFF = 4096
G2T = 256
NG2 = L // G2T
EPS = 1e-6
NIN = 3352
O_QKV, O_XBC, O_ZG, O_ZS, O_GATE = 0, 1536, 2304, 2816, 3328
NEG = -30000.0

DEBUG = {}
KNOB = {"ng": NG, "mixer": True, "ng2": NG2}


class Tracker:
    ENGS = ("pe", "act", "dve", "pool", "sp")

    def __init__(self, sems):
        self.sems = sems
        self.lists = {e: [] for e in self.ENGS}
        self.cnt = {s: 0 for s in sems}
        self.waited = {e: {} for e in self.ENGS}
        self.lastw = {}
        self.reads = {}

    def _deps(self, eng, incsem, reads, writes):
        need = {}

        def add(s, v):
            if v > need.get(s, 0):
                need[s] = v

        for k in reads:
            lw = self.lastw.get(k)
            if lw is not None:
                if lw[0] == "pe" and eng == "pe":
                    continue
                add(*lw)
        skip_same = (incsem == "pe") or incsem.startswith("d_")
        for k in writes:
            lw = self.lastw.get(k)
            if lw is not None and (lw[0] != incsem or not skip_same):
                add(*lw)
            for s, v in self.reads.get(k, {}).items():
                if s != incsem or not skip_same:
                    add(s, v)
        waits = []
        for s, v in need.items():
            if self.waited[eng].get(s, 0) < v:
                self.waited[eng][s] = v
                waits.append((s, v))
        return waits

    def op(self, eng, fn, reads=(), writes=(), incsem=None, incval=1):
        incsem = incsem or eng
        waits = self._deps(eng, incsem, reads, writes)
        self.cnt[incsem] += incval
        v = self.cnt[incsem]
        self.lists[eng].append((waits, fn, incsem, incval))
        for k in reads:
            self.reads.setdefault(k, {})[incsem] = v
        for k in writes:
            self.lastw[k] = (incsem, v)
            self.reads[k] = {}

    def dma(self, queue, slot, out, in_, reads=(), writes=(), **kw):
        self.op(queue, lambda e: e.dma_start(out=out, in_=in_, **kw), reads, writes, incsem=slot, incval=16)

    def final_wait(self, eng, keys):
        waits = self._deps(eng, "__none__", keys, keys)
        self.lists[eng].append((waits, None, None, 0))

    def emit(self, block):
        def run(name):
            def f(eng):
                for waits, fn, incsem, incval in self.lists[name]:
                    for s, v in waits:
                        eng.wait_ge(self.sems[s], v)
                    if fn is not None:
                        fn(eng).then_inc(self.sems[incsem], incval)
                self.lists[name] = []
            return f
        block.tensor(run("pe"))
        block.scalar(run("act"))
        block.vector(run("dve"))
        block.gpsimd(run("pool"))
        block.sync(run("sp"))


def build_program(debug=None):
    debug = debug or {}
    nc = bass.Bass("TRN2", target_bir_lowering=False)

    def din(name, shape):
        return nc.dram_tensor(name, list(shape), F32, kind="ExternalInput").ap()

    x_d = din("x", [L, D])
    c_d = din("c_l", [128, KC])
    win_d = din("w_in_r", [D, NIN])
    wout_d = din("w_out", [D, D])
    w1_d = din("w_ff1", [D, DFF])
    w2_d = din("w_ff2", [DFF, D])
    adaw_d = din("ada_w", [D, 6 * D])
    adab_d = din("ada_b_b", [128, 6 * D])
    lnw_d = din("lnw_b", [128, 3 * D])
    cw_d = din("cw", [128, 18 * 4])
    cb_d = din("cb", [128, 18])
    hp_d = din("hp", [128, 40])
    gnw_d = din("gnw", [128, 64])
    snw_d = din("snw", [128, 512])
    cst_d = din("cst", [128, 5 * 128 + 2])
    sel_d = din("sel", [128, 24 * 128])
    lm_d = din("lm", [128, 8 * 128])
    out_d = nc.dram_tensor("out", [L, D], F32, kind="ExternalOutput").ap()
    x1_d = nc.dram_tensor("x1_scr", [L, D], F32, kind="Internal").ap()
    dbg_d = {}
    for name, shape in debug.items():
        dbg_d[name] = nc.dram_tensor("dbg_" + name, list(shape), F32, kind="ExternalOutput").ap()

    es = ExitStack()
    with es:
        sem_names = ["pe", "act", "dve", "pool", "d_w", "d_c", "d_ada0", "d_ada1", "d_x0", "d_x1", "d_xr0", "d_xr1",
                     "d_st0", "d_st1", "d_w2", "d_p2x0", "d_p2x1", "d_p2x2", "d_p2x3", "d_dbg", "d_adb0", "d_adb1", "d_win", "d_wout", "d_sel", "d_lm", "d_w1", "d_k_cst_f", "d_k_lnw", "d_k_c_sb",
                     "d_k_hp", "d_k_gnw", "d_k_snw", "d_k_cw", "d_k_cbias", "d_po0", "d_po1", "d_po2", "d_po3", "d_k_lnw1", "d_k_lnwf", "d_w1_0", "d_w1_1", "d_w1_2", "d_w1_3"]
        sems = {n: es.enter_context(nc.semaphore("s_" + n)) for n in sem_names}
        tr = Tracker(sems)

        def sb(name, shape, dt=F32, stack=es):
            return stack.enter_context(nc.sbuf_tensor("sb_" + name, list(shape), dt))

        ps = [es.enter_context(nc.psum_tensor(f"ps{i}", [128, 512], F32)) for i in range(8)]
        bank_ctr = [0, 0]
        NB_MAIN = 5

        def bank(pool=0):
            if pool == 0:
                i = bank_ctr[0] % NB_MAIN
            else:
                i = NB_MAIN + bank_ctr[1] % (8 - NB_MAIN)
            bank_ctr[pool] += 1
            k = ("ps", i)
            if k in tr.lastw and not tr.reads.get(k):
                raise RuntimeError(f"PSUM bank {i} re-allocated before its consumer was emitted")
            return ps[i], k

        def mm(out, lhsT, rhs, start, stop, r, w):
            tr.op("pe", lambda e: e.matmul(out, lhsT, rhs, start=start, stop=stop, skip_group_check=True), r, w)

        def tp(out, in_, ident, r, w):
            tr.op("pe", lambda e: e.transpose(out, in_, ident), r, w)

        def act(out, in_, func, r, w, bias=None, scale=None, accum_out=None, eng="act"):
            kw = {}
            if bias is not None:
                kw["bias"] = bias
            if scale is not None:
                kw["scale"] = scale
            if accum_out is not None:
                kw["accum_out"] = accum_out
            tr.op("act", lambda e: e.activation(out=out, in_=in_, func=func, **kw), r, w)

        def tt(eng, out, in0, in1, op, r, w):
            tr.op(eng, lambda e: e.tensor_tensor(out=out, in0=in0, in1=in1, op=op), r, w)

        def ts(eng, out, in0, s1, s2, op0, op1, r, w):
            if op1 is None:
                tr.op(eng, lambda e: e.tensor_scalar(out=out, in0=in0, scalar1=s1, scalar2=None, op0=op0), r, w)
            else:
                tr.op(eng, lambda e: e.tensor_scalar(out=out, in0=in0, scalar1=s1, scalar2=s2, op0=op0, op1=op1), r, w)

        def stt(eng, out, in0, scalar, in1, op0, op1, r, w):
            tr.op(eng, lambda e: e.scalar_tensor_tensor(out=out, in0=in0, scalar=scalar, in1=in1, op0=op0, op1=op1), r, w)

        def cp(eng, out, in_, r, w):
            if eng == "act":
                tr.op("act", lambda e: e.copy(out=out, in_=in_), r, w)
            else:
                tr.op(eng, lambda e: e.tensor_copy(out=out, in_=in_), r, w)

        def red(eng, out, in_, r, w):
            tr.op(eng, lambda e: e.tensor_reduce(out=out, in_=in_, axis=AX.X, op=ALU.add), r, w)

        def dbg(name, ap, key):
            if name in dbg_d:
                tr.dma("sp", "d_dbg", dbg_d[name], ap, reads=[key], writes=[("dbgout", name)])

        def bc_h(ap2, n):
            P = ap2.shape[0]
            return ap2.unsqueeze(2).to_broadcast([P, ap2.shape[1], n])

        def v3(ap, h):
            return ap.rearrange("p (h d) -> p h d", h=h)

        cst_f = sb("cst_f", [128, 5 * 128 + 2])
        cst_b = sb("cst_b", [128, 5 * 128 + 2], BF16)
        c_sb = sb("c_sb", [128, KC])
        modb = sb("modb", [128, 3, D])

        ident_f = cst_f[:, 0:128]
        triu_f = cst_f[:, 128:256]
        ones_f = cst_f[:, 256:384]
        ident_b = cst_b[:, 0:128]
        mask_s_b = cst_b[:, 384:512]
        mask_i_b = cst_b[:, 512:640]

        for (dst, src, key) in [(cst_f, cst_d, "cst_f"), (c_sb, c_d, "c_sb")]:
            tr.dma("sp", "d_k_" + key, dst[:], src, reads=[], writes=[key])
        cp("dve", cst_b[:], cst_f[:], ["cst_f"], ["cst_b"])
        act(c_sb[:], c_sb[:], AF.Silu, ["c_sb"], ["c_sb"])

        ada_ctr = [0]
        MW = 256

        def compute_mod(first_chunk, adaw, adab, cB, cBk, lnw, lnwk, stage):
            for j in range(3):
                for half in range(D // MW):
                    col0 = (first_chunk + j) * D + half * MW
                    s = ada_ctr[0] % 2
                    ada_ctr[0] += 1
                    tr.dma("sp", f"d_ada{s}", stage[s][:], adaw_d[:, col0:col0 + MW].rearrange("(k p) n -> p k n", p=128),
                           reads=[], writes=[("adst", s)])
                    cp("act", adaw[s][:], stage[s][:], [("adst", s)], [("adaw", s)])
                    tr.dma("sp", f"d_adb{s}", adab[s][:], adab_d[:, col0:col0 + MW], reads=[], writes=[("adab", s)])
                    pt, pk = bank()
                    for k in range(KC):
                        mm(pt[:, 0:MW], cB[:, k, :], adaw[s][:, k, :], k == 0, k == KC - 1, [cBk, ("adaw", s)], [pk])
                    dst = modb[:, j, half * MW:(half + 1) * MW]
                    tt("dve", dst, pt[:, 0:MW], adab[s][:], ALU.add, [pk, ("adab", s)], ["modb"])
                    if j == 1:
                        stt("dve", dst, dst, 1.0, lnw[:, half * MW:(half + 1) * MW],
                            ALU.add, ALU.mult, ["modb", lnwk], ["modb"])

        ms = ExitStack()
        with ms:
            adab0 = [sb(f"adab{i}", [128, MW], F32, ms) for i in range(2)]
            adaw0 = [sb(f"adaw{i}", [128, KC, MW], BF16, ms) for i in range(2)]
            lnw0 = sb("lnw0", [128, D], F32, ms)
            cB0 = sb("cB0", [128, KC, 128], BF16, ms)
            tr.dma("sp", "d_k_lnw", lnw0[:], lnw_d[:, 0:D], reads=[], writes=["lnw0"])
            cp("dve", cB0[:], c_sb[:].unsqueeze(2).to_broadcast([128, KC, 128]), ["c_sb"], ["cB0"])
            adst0 = [sb(f"adst{i}", [128, KC, MW], F32, ms) for i in range(2)]
            compute_mod(0, adaw0, adab0, cB0, "cB0", lnw0, "lnw0", adst0)
            with nc.Block() as block:
                tr.emit(block)

        p1 = ExitStack()
        with p1:
            def sb1(name, shape, dt=F32):
                return sb(name, shape, dt, stack=p1)

            w_in = sb1("w_in", [128, KC, NIN], BF16)
            w_out = sb1("w_out", [128, KC, D], BF16)
            half_n = NIN // 2
            for k in range(KC):
                for pc in range(2):
                    tr.dma("pool", "d_win", w_in[:, k, pc * half_n:(pc + 1) * half_n],
                           win_d[k * 128:(k + 1) * 128, pc * half_n:(pc + 1) * half_n], reads=[], writes=["w_in"])
            for k in range(KC):
                tr.dma("pool", "d_wout", w_out[:, k, :], wout_d[k * 128:(k + 1) * 128, :], reads=[], writes=["w_out"])

            sel_b = sb1("sel_b", [128, 24 * 128], BF16)
            hp = sb1("hp", [128, 40])
            gnw = sb1("gnw", [128, 64])
            snw = sb1("snw", [128, 512])
            cw = sb1("cw", [128, 72])
            cbias = sb1("cbias", [128, 18])
            diagD = sb1("diagD", [128, 8, 128], BF16)
            for (dst, src, key) in [(hp, hp_d, "hp"), (gnw, gnw_d, "gnw"), (snw, snw_d, "snw"), (cw, cw_d, "cw"), (cbias, cb_d, "cbias")]:
                tr.dma("sp", "d_k_" + key, dst[:], src, reads=[], writes=[key])
            tr.dma("pool", "d_sel", sel_b[:, 0:1536], sel_d[:, 0:1536], reads=[], writes=["sel_b"])
            tr.dma("pool", "d_sel", sel_b[:, 1536:3072], sel_d[:, 1536:3072], reads=[], writes=["sel_b"])
            act(hp[:, 0:8], hp[:, 0:8], AF.Exp, ["hp"], ["hp"])
            act(hp[:, 16:24], hp[:, 16:24], AF.Exp, ["hp"], ["hp"])
            ts("dve", hp[:, 0:8], hp[:, 0:8], -1.0, None, ALU.mult, None, ["hp"], ["hp"])
            ts("dve", hp[:, 16:24], hp[:, 16:24], -1.0, None, ALU.mult, None, ["hp"], ["hp"])
            for h in range(8):
                ts("dve", diagD[:, h, :], ident_f, hp[:, 32 + h:33 + h], None, ALU.mult, None, ["hp", "cst_f"], ["diagD"])

            xt = [sb1(f"xt{i}", [128, D]) for i in range(1)]
            xr = [sb1(f"xr{i}", [128, D]) for i in range(1)]
            hbf = sb1("hbf", [128, D], BF16)
            hT = sb1("hT", [128, KC, GT], BF16)
            nstat = sb1("nstat", [128, 8])
            pre2 = sb1("pre2", [128, 2, GT + 3])
            pre = [pre2[:, 0, :], pre2[:, 1, :]]
            cacc = [sb1(f"cacc{i}", [128, GT]) for i in range(2)]
            halo = sb1("halo", [128, 18, 3])
            fm = [sb1(f"fm{i}", [128, 18, GT], BF16) for i in range(2)]
            sz = sb1("sz", [128, TPG, 1024], BF16)
            graw = sb1("graw", [128, TPG, 24])
            gscs = [sb1(f"gsc{i}", [128, 96]) for i in range(TPG)]
            TBs = [sb1(f"TB{i}", [128, 64]) for i in range(TPG)]
            CTbs = [sb1(f"CTb{i}", [128, 24]) for i in range(TPG)]
            SCs = [sb1(f"SC{i}", [128, 64]) for i in range(TPG)]
            RCs = [sb1(f"RC{i}", [128, 96], BF16) for i in range(TPG)]
            RCTs = [sb1(f"RCT{i}", [128, 256], BF16) for i in range(TPG)]
            kq_ms = [sb1(f"kq_m{i}", [128, 8, 2, 128], BF16) for i in range(TPG)]
            C_ms = [sb1(f"C_m{i}", [128, 2, 128], BF16) for i in range(TPG)]
            lnsss = [sb1(f"lnss{i}", [128, 16]) for i in range(TPG)]
            op_t = sb1("op_t", [128, D])
            sqf = op_t
            ssq = sb1("ssq", [128, 16])
            kgs = [sb1(f"kg{i}", [128, 512], BF16) for i in range(TPG)]
            kds = [sb1(f"kd{i}", [128, 512], BF16) for i in range(TPG)]
            Evs = [sb1(f"Ev{i}", [128, 512], BF16) for i in range(TPG)]
            xtoks = [sb1(f"xtok{i}", [128, 640], BF16) for i in range(TPG)]
            xws = [sb1(f"xw{i}", [128, 512], BF16) for i in range(TPG)]
            Eb = [sb1(f"Eb{i}", [128, 4, 128]) for i in range(2)]
            Eb3 = sb1("Eb3", [128, 4, 128])
            lm_b = sb1("lm_b", [128, 8, 128], BF16)
            tr.dma("pool", "d_lm", lm_b[:], lm_d.rearrange("p (l i) -> p l i", l=8), reads=[], writes=["lm_b"])
            Mp = [sb1(f"Mp{i}", [128, 8, 128], BF16) for i in range(2)]
            Np = [sb1(f"Np{i}", [128, 8, 128], BF16) for i in range(2)]
            Y = sb1("Y", [128, 8, 128], BF16)
            Yt = Mp[1]
            Pb = Np[1]
            qkT = sb1("qkT", [128, 8, 128], BF16)
            scT = sb1("scT", [128, 8, 128], BF16)
            nwT = sb1("nwT", [128, 8, 128], BF16)
            vn = sb1("vn", [128, 512], BF16)
            S_f = sb1("S_f", [128, 4, 64])
            S_b = sb1("S_b", [128, 4, 64], BF16)
            Sdec = sb1("Sdec", [128, 4, 64])
            H_f = sb1("H_f", [128, 256])
            H_b = sb1("H_b", [128, 256], BF16)
            Hdec = sb1("Hdec", [128, 256])
            o_t = sb1("o_t", [128, 512])
            o_f = sb1("o_f", [128, 512])
            y_t = sb1("y_t", [128, 512])
            y_f = sb1("y_f", [128, 512])
            mix = sb1("mix", [128, D], BF16)
            mixT = sb1("mixT", [128, KC, 128], BF16)
            print("phase1 sbuf remaining", nc.sbuf_bytes_remaining)

            tr.op("dve", lambda e: e.memset(halo[:], 0.0), [], ["halo"])
            for i_ in range(TPG):
                tr.op("dve", lambda e, b_=RCTs[i_]: e.memset(b_[:], 0.0), [], [("RCT", i_)])
            tr.op("dve", lambda e: e.memset(nwT[:], 0.0), [], [("nwT", 0, 0), ("nwT", 0, 1), ("nwT", 1, 0), ("nwT", 1, 1)])
            tr.op("dve", lambda e: e.memset(S_f[:], 0.0), [], [("S_f", 0), ("S_f", 1)])
            tr.op("dve", lambda e: e.memset(S_b[:], 0.0), [], [("S_b", 0), ("S_b", 1)])
            tr.op("dve", lambda e: e.memset(H_f[:], 0.0), [], [("H_f", 0), ("H_f", 1)])
            tr.op("dve", lambda e: e.memset(H_b[:], 0.0), [], [("H_b", 0), ("H_b", 1)])

            xctr = [0]

            def norm_to_hT(src_d, src_keys, tok0, hT_buf, hT_key, col0, slots, slot_names, hb, hbk, ns_, nsk):
                s = xctr[0] % len(slots)
                xctr[0] += 1
                xb = slots[s]
                xk = (slot_names, s)
                tr.dma("sp", f"{slot_names}{s}", xb[:], src_d[tok0:tok0 + 128, :], reads=src_keys, writes=[xk])
                act(hb[:], xb[:], AF.Square, [xk], [hbk, nsk], accum_out=ns_[:, 0:1])
                act(ns_[:, 1:2], ns_[:, 0:1], AF.Ln, [nsk], [nsk], bias=EPS, scale=1.0 / D)
                act(ns_[:, 2:3], ns_[:, 1:2], AF.Exp, [nsk], [nsk], scale=-0.5)
                return xb, xk

            def norm_finish(xb, xk, hT_buf, hT_key, col0, hb, hbk, ns_, nsk, dst=None, dstk=None, pool=0):
                dst = dst if dst is not None else xb
                dstk = dstk if dstk is not None else xk
                stt("dve", dst[:], xb[:], ns_[:, 2:3], modb[:, 1, :], ALU.mult, ALU.mult, [xk, nsk, "modb"], [dstk])
                tt("dve", hb[:], dst[:], modb[:, 0, :], ALU.add, [dstk, "modb"], [hbk])
                pt, pk = bank(pool)
                ptb = pt[:].bitcast(BF16)
                for k in range(KC):
                    tp(ptb[:, k * 128:(k + 1) * 128], hb[:, k * 128:(k + 1) * 128], ident_b, [hbk, "cst_b"], [pk])
                cp("act", hT_buf[:, :, col0:col0 + 128], ptb.rearrange("p (k t) -> p k t", k=KC), [pk], [hT_key])

            SENT = object()

            def fm_gen(g):
                gp_ = g % 2
                fmk_ = ("fm", gp_)
                F_ = fm[gp_]
                for t in range(TPG):
                    xb, xk = norm_to_hT(x_d, [], g * GT + t * 128, hT, "hT", t * 128, xt, "d_x", hbf, "hbf", nstat, "nstat")
                    norm_finish(xb, xk, hT, "hT", t * 128, hbf, "hbf", nstat, "nstat", pool=1)
                    yield
                for c0 in range(0, 18, 2):
                    pts = []
                    for c in (c0, c0 + 1):
                        pt, pk = bank(1)
                        for k in range(KC):
                            mm(pt[:, 0:GT], w_in[:, k, c * 128:(c + 1) * 128], hT[:, k, :], k == 0, k == KC - 1, ["w_in", "hT"], [pk])
                        pts.append((pt, pk))
                    cp("dve", pre2[:, :, 0:3], halo[:, c0:c0 + 2, :], ["halo"], [("preh", 0), ("preh", 1)])
                    for i, c in enumerate((c0, c0 + 1)):
                        pt, pk = pts[i]
                        pb, pbk, ca, cak = pre[i], ("pre", i), cacc[i], ("cacc", i)
                        cp("act", pb[:, 3:GT + 3], pt[:, 0:GT], [pk], [pbk])
                        tr.op("act", lambda e, ca=ca, pt=pt, c=c: e.activation(out=ca[:], in_=pt[:, 0:GT], func=AF.Copy, scale=cw[:, c * 4 + 3:c * 4 + 4]),
                              [pk, "cw"], [cak])
                    cp("dve", halo[:, c0:c0 + 2, :], pre2[:, :, GT:GT + 3], [("pre", 0), ("pre", 1)], ["halo"])
                    for i, c in enumerate((c0, c0 + 1)):
                        pb, pbk, ca, cak = pre[i], ("pre", i), cacc[i], ("cacc", i)
                        phk = ("preh", i)
                        stt("dve", ca[:], pb[:, 2:GT + 2], cw[:, c * 4 + 2:c * 4 + 3], ca[:], ALU.mult, ALU.add, [pbk, phk, cak, "cw"], [cak])
                        stt("dve", ca[:], pb[:, 1:GT + 1], cw[:, c * 4 + 1:c * 4 + 2], ca[:], ALU.mult, ALU.add, [pbk, phk, cak, "cw"], [cak])
                        stt("dve", ca[:], pb[:, 0:GT], cw[:, c * 4 + 0:c * 4 + 1], ca[:], ALU.mult, ALU.add, [pbk, phk, cak, "cw"], [cak])
                    for i, c in enumerate((c0, c0 + 1)):
                        ca, cak = cacc[i], ("cacc", i)
                        act(F_[:, c, :], ca[:], AF.Silu, [cak, "cbias"], [fmk_], bias=cbias[:, c:c + 1])
                    yield

            def tm_stage(g):
                for t in range(TPG):
                    cs = slice(t * 128, (t + 1) * 128)
                    for zi, off in enumerate((O_ZG, O_ZS)):
                        pt, pk = bank()
                        for k in range(KC):
                            mm(pt[:], hT[:, k, cs], w_in[:, k, off:off + 512], k == 0, k == KC - 1, ["w_in", "hT"], [pk])
                        act(sz[:, t, zi * 512:(zi + 1) * 512], pt[:], AF.Silu, [pk], [("sz", t)])
                    pt, pk = bank()
                    for k in range(KC):
                        mm(pt[:, 0:24], hT[:, k, cs], w_in[:, k, O_GATE:O_GATE + 24], k == 0, k == KC - 1, ["w_in", "hT"], [pk])
                    cp("dve", graw[:, t, :], pt[:, 0:24], [pk], [("graw", t)])

            for _ in fm_gen(0):
                pass
            pend = [iter(())]
            for g in range(KNOB["ng"]):
                gp = g % 2
                fmk = ("fm", gp)
                F = fm[gp]
                tm_stage(g)
                fmg = fm_gen(g + 1) if (g + 1 < KNOB["ng"] and KNOB.get("fm_il", True)) else iter(())

                for t in range(TPG):
                    ti = g * TPG + t
                    cs = slice(t * 128, (t + 1) * 128)
                    if not KNOB["mixer"]:
                        continue
                    gsc, TB, CTb, SC, RC, RCT = gscs[t], TBs[t], CTbs[t], SCs[t], RCs[t], RCTs[t]
                    kg, kd, Ev, xtok, xw, kq_m, C_m, lnss = kgs[t], kds[t], Evs[t], xtoks[t], xws[t], kq_ms[t], C_ms[t], lnsss[t]
                    gr = graw[:, t, :]
                    grk = ("graw", t)
                    act(gsc[:, 0:8], gr[:, 0:8], AF.Exp, [grk], [("gsc0", t)], scale=-1.0)
                    act(gsc[:, 8:16], gsc[:, 0:8], AF.Ln, [("gsc0", t)], [("gsc1", t)], bias=1.0)
                    tt("dve", gsc[:, 16:24], gr[:, 8:16], hp[:, 8:16], ALU.add, [grk, "hp"], [("gsc2", t)])
                    act(gsc[:, 24:32], gsc[:, 16:24], AF.Exp, [("gsc2", t)], [("gsc3", t)])
                    act(gsc[:, 32:40], gsc[:, 24:32], AF.Ln, [("gsc3", t)], [("gsc4", t)], bias=1.0)
                    tt("dve", gsc[:, 40:48], gr[:, 16:24], hp[:, 24:32], ALU.add, [grk, "hp"], [("gsc5", t)])
                    act(gsc[:, 48:56], gsc[:, 40:48], AF.Exp, [("gsc5", t)], [("gsc6", t)])
                    act(gsc[:, 56:64], gsc[:, 48:56], AF.Ln, [("gsc6", t)], [("gsc7", t)], bias=1.0)
                    act(gsc[:, 64:72], gsc[:, 56:64], AF.Ln, [("gsc7", t)], [("gsc8", t)])
                    tt("dve", gsc[:, 72:80], gsc[:, 32:40], hp[:, 0:8], ALU.mult, [("gsc4", t), "hp"], [("ga", t)])
                    tt("dve", gsc[:, 80:88], gsc[:, 56:64], hp[:, 16:24], ALU.mult, [("gsc7", t), "hp"], [("ga", t)])
                    pg, pgk = bank()
                    mm(pg[:, 0:16], triu_f, gsc[:, 72:88], True, True, ["cst_f", ("ga", t)], [pgk])
                    mm(pg[:, 16:32], ones_f, gsc[:, 72:88], True, True, ["cst_f", ("ga", t)], [pgk])
                    cp("dve", TB[:, 16:24], pg[:, 8:16], [pgk], [("TB2", t)])
                    cp("dve", TB[:, 48:64], pg[:, 16:32], [pgk], [("TB67", t)])
                    Gk = ("Gs", t)
                    cp("dve", gsc[:, 88:96], pg[:, 0:8], [pgk], [Gk])
                    G = gsc[:, 88:96]

                    pqk, pqkk = bank()
                    pqkb = pqk[:].bitcast(BF16)
                    for c in range(8):
                        tp(pqkb[:, c * 128:(c + 1) * 128], F[:, c, cs], ident_b, [fmk, "cst_b"], [pqkk])
                    pv, pvk = bank()
                    pvb = pv[:].bitcast(BF16)
                    for c in range(4):
                        tp(pvb[:, c * 128:(c + 1) * 128], F[:, 8 + c, cs], ident_b, [fmk, "cst_b"], [pvk])
                    px, pxk = bank()
                    pxb = px[:].bitcast(BF16)
                    for c in range(5):
                        tp(pxb[:, c * 128:(c + 1) * 128], F[:, 12 + c, cs], ident_b, [fmk, "cst_b"], [pxk])
                    act(sqf[:], pqkb[:, 0:1024], AF.Square, [pqkk], ["op_t"])
                    red("dve", lnss[:], v3(sqf[:], 16), ["op_t"], [("lnss", t)])
                    act(lnss[:], lnss[:], AF.Ln, [("lnss", t)], [("lnss", t)], bias=EPS)
                    lq = lnss[:, 0:8]
                    lk = lnss[:, 8:16]
                    tt("dve", gsc[:, 0:8], lk, gsc[:, 8:16], ALU.add, [("lnss", t), ("gsc1", t)], [("gsc0", t)])
                    stt("dve", TB[:, 8:16], gsc[:, 0:8], -0.5, G, ALU.mult, ALU.add, [("gsc0", t), Gk], [("TB1", t)])
                    stt("dve", TB[:, 0:8], lq, -0.5, G, ALU.mult, ALU.add, [("lnss", t), Gk], [("TB0", t)])
                    ts("dve", TB[:, 0:8], TB[:, 0:8], -math.log(8.0), None, ALU.add, None, [("TB0", t)], [("TB0", t)])
                    stt("dve", CTb[:, 8:16], G, -2.0, TB[:, 8:16], ALU.mult, ALU.add, [Gk, ("TB1", t)], [("CT1", t)])
                    stt("dve", CTb[:, 0:8], lk, -0.5, G, ALU.mult, ALU.subtract, [("lnss", t), Gk], [("CT0", t)])
                    tt("dve", CTb[:, 16:24], gsc[:, 64:72], TB[:, 16:24], ALU.subtract, [("gsc8", t), ("TB2", t)], [("CT2", t)])
                    tt("dve", TB[:, 24:32], CTb[:, 0:8], TB[:, 48:56], ALU.add, [("CT0", t), ("TB67", t)], [("TB3", t)])
                    ts("dve", TB[:, 32:40], gsc[:, 8:16], -0.5, None, ALU.mult, None, [("gsc1", t)], [("TB4", t)])
                    tt("dve", TB[:, 40:48], CTb[:, 16:24], TB[:, 56:64], ALU.add, [("CT2", t), ("TB67", t)], [("TB5", t)])
                    TBk = [("TB0", t), ("TB1", t), ("TB2", t), ("TB3", t), ("TB4", t), ("TB5", t), ("TB67", t)]
                    act(SC[:], TB[:], AF.Exp, TBk, [("SC", t)])
                    cp("dve", RC[:, 0:24], TB[:, 0:24], [("TB0", t), ("TB1", t), ("TB2", t)], [("RC0", t)])
                    tt("dve", RC[:, 24:48], TB[:, 0:24], RC[:, 0:24], ALU.subtract, [("TB0", t), ("TB1", t), ("TB2", t), ("RC0", t)], [("RC1", t)])
                    cp("dve", RC[:, 48:72], CTb[:], [("CT0", t), ("CT1", t), ("CT2", t)], [("RC2", t)])
                    tt("dve", RC[:, 72:96], CTb[:], RC[:, 48:72], ALU.subtract, [("CT0", t), ("CT1", t), ("CT2", t), ("RC2", t)], [("RC3", t)])
                    prc, prck = bank()
                    prcb = prc[:].bitcast(BF16)
                    tp(prcb[0:48, 0:128], RC[:, 0:48], ident_b, [("RC0", t), ("RC1", t), "cst_b"], [prck])
                    tp(prcb[0:48, 128:256], RC[:, 48:96], ident_b, [("RC2", t), ("RC3", t), "cst_b"], [prck])
                    cp("act", RCT[0:48, :], prcb[0:48, 0:256], [prck], [("RCT", t)])
                    tt("dve", v3(kg[:], 8), v3(pqkb[:, 512:1024], 8), bc_h(SC[:, 8:16], 64), ALU.mult, [pqkk, ("SC", t)], [("kg", t)])
                    tt("dve", v3(kd[:], 8), v3(pqkb[:, 512:1024], 8), bc_h(SC[:, 24:32], 64), ALU.mult, [pqkk, ("SC", t)], [("kd", t)])
                    tt("dve", v3(Ev[:], 8), v3(pvb[:, 0:512], 8), bc_h(SC[:, 32:40], 64), ALU.mult, [pvk, ("SC", t)], [("Ev", t)])
                    cp("act", xtok[:], pxb[:, 0:640], [pxk], [("xtok", t)])
                    tt("dve", v3(xw[:], 8), v3(pxb[:, 0:512], 8), bc_h(SC[:, 40:48], 64), ALU.mult, [pxk, ("SC", t)], [("xw", t)])

                    for h2 in range(2):
                        msk = cst_f[:, 640 + h2:641 + h2]
                        tr.op("act", lambda e, h2=h2, msk=msk, F=F, cs=cs, kq_m=kq_m: e.activation(out=kq_m[:, :, h2, :], in_=F[:, 0:8, cs], func=AF.Copy, scale=msk),
                              [fmk, "cst_f"], [("kq_m", t, h2)])
                        ts("dve", C_m[:, h2, :], F[:, 17, cs], msk, None, ALU.mult, None, [fmk, "cst_f"], [("C_m", t, h2)])
                for t in range(TPG):
                    ti = g * TPG + t
                    cs = slice(t * 128, (t + 1) * 128)
                    if not KNOB["mixer"]:
                        s = 0
                        tr.dma("sp", f"d_xr{s}", xr[s][:], x_d[ti * 128:(ti + 1) * 128, :], reads=[], writes=[("xr", s)])
                        tr.dma("sp", f"d_st{s}", x1_d[ti * 128:(ti + 1) * 128, :], xr[s][:], reads=[("xr", s)], writes=[("x1d", ti)])
                        continue
                    SC, RCT = SCs[t], RCTs[t]
                    kg, kd, Ev, xtok, xw, kq_m, C_m = kgs[t], kds[t], Evs[t], xtoks[t], xws[t], kq_ms[t], C_ms[t]
                    KQM = [("kq_m", t, 0), ("kq_m", t, 1)]
                    CM = [("C_m", t, 0), ("C_m", t, 1)]
                    def ssd_chain():
                        pcb, pcbk = bank(1)
                        for g2 in range(2):
                            mm(pcb[:, g2 * 128:(g2 + 1) * 128], F[:, 16, cs], C_m[:, g2, :], True, True, [fmk] + CM, [pcbk])
                        for hh in range(2):
                            hs = slice(hh * 4, (hh + 1) * 4)
                            pe_, pek = bank(1)
                            for h4 in range(4):
                                h = hh * 4 + h4
                                selrh = sel_b[:, (16 + h) * 128:(16 + h + 1) * 128]
                                o = pe_[:, h4 * 128:(h4 + 1) * 128]
                                mm(o, selrh, RCT[:, 0:128], True, False, ["sel_b", ("RCT", t)], [pek])
                                mm(o, RCT[:, 128:256], selrh, False, False, ["sel_b", ("RCT", t)], [pek])
                                mm(o, ident_b, mask_i_b, False, True, ["cst_b"], [pek])
                            act(Eb3[:], v3(pe_[:], 4), AF.Exp, [pek], ["Eb3"])
                            tt("dve", scT[:, hs, :], pcb[:, hh * 128:(hh + 1) * 128].unsqueeze(1).to_broadcast([128, 4, 128]),
                               Eb3[:], ALU.mult, [pcbk, "Eb3"], [("scT", hh)])
                            yield
                        pyd, pydk = bank(1)
                        for h in range(8):
                            o = pyd[:, h * 64:(h + 1) * 64]
                            mm(o, scT[:, h, :], xtok[:, h * 64:(h + 1) * 64], True, False, [("scT", h // 4), ("xtok", t)], [pydk])
                            mm(o, diagD[:, h, :], xtok[:, h * 64:(h + 1) * 64], False, True, ["diagD", ("xtok", t)], [pydk])
                        pyo, pyok = bank(1)
                        phn, phnk = bank(1)
                        for g2 in range(2):
                            rs = slice(64 * g2, 64 * g2 + 64)
                            mm(pyo[:, g2 * 256:(g2 + 1) * 256], C_m[:, g2, :], H_b[:, :], True, True, CM + [("H_b", 0), ("H_b", 1)], [pyok])
                        mm(phn[:], xtok[:, 512:640], xw[:], True, True, [("xtok", t), ("xw", t)], [phnk])
                        for g2 in range(2):
                            rs = slice(64 * g2, 64 * g2 + 64)
                            tt("pool", v3(Hdec[rs], 4), v3(H_f[rs], 4), bc_h(SC[rs, 56 + 4 * g2:60 + 4 * g2], 64), ALU.mult, [("H_f", g2), ("SC", t)], [("Hdec", g2)])
                        yield
                        tt("dve", v3(y_t[:], 8), v3(pyo[:], 8), bc_h(SC[:, 16:24], 64), ALU.mult, [pyok, ("SC", t)], ["y_t"])
                        for g2 in range(2):
                            rs = slice(64 * g2, 64 * g2 + 64)
                            src = phn[rs, g2 * 256:(g2 + 1) * 256]
                            tt("dve", H_b[rs], Hdec[rs], src, ALU.add, [("Hdec", g2), phnk], [("H_b", g2)])
                            tt("dve", H_f[rs], Hdec[rs], src, ALU.add, [("Hdec", g2), phnk], [("H_f", g2)])
                        yield
                        tt("dve", y_f[:], y_t[:], pyd[:], ALU.add, ["y_t", pydk], ["y_f"])
                        if ti in (0, 1):
                            dbg(f"yf_{ti}", y_f[:], "y_f")
                        tt("pool", y_f[:], y_f[:], sz[:, t, 512:1024], ALU.mult, ["y_f", ("sz", t)], ["y_f"])
                        yield
                        for g2 in range(2):
                            act(y_t[:, g2 * 256:(g2 + 1) * 256], y_f[:, g2 * 256:(g2 + 1) * 256], AF.Square, ["y_f"], ["y_t", "ssq"],
                                accum_out=ssq[:, 8 + g2:9 + g2])
                        act(ssq[:, 8:10], ssq[:, 8:10], AF.Ln, ["ssq"], ["ssq"], bias=EPS, scale=1.0 / 256)
                        act(ssq[:, 8:10], ssq[:, 8:10], AF.Exp, ["ssq"], ["ssq"], scale=-0.5)
                        yield
                        tt("dve", v3(y_t[:], 2), v3(y_f[:], 2), ssq[:, 8:10].unsqueeze(2).to_broadcast([128, 2, 256]), ALU.mult, ["y_f", "ssq"], ["y_t"])
                        tt("pool", mix[:, 512:1024], y_t[:], snw[:], ALU.mult, ["y_t", "snw"], ["mix"])

                    ssd = ssd_chain()

                    def side():
                        if next(pend[0], SENT) is SENT:
                            if next(ssd, SENT) is SENT:
                                next(fmg, None)
                    pkk = [bank(), bank()]
                    pkq = [bank(), bank()]
                    for hh in range(2):
                        a_, ak = pkk[hh]
                        b_, bk = pkq[hh]
                        for h4 in range(4):
                            h = hh * 4 + h4
                            kTc = F[:, 4 + h // 2, cs]
                            mm(a_[:, h4 * 128:(h4 + 1) * 128], kTc, kq_m[:, 4 + h // 2, h % 2, :], True, True, [fmk] + KQM, [ak])
                            mm(b_[:, h4 * 128:(h4 + 1) * 128], kTc, kq_m[:, h // 2, h % 2, :], True, True, [fmk] + KQM, [bk])
                    ectr = 0
                    for r, mk in ((1, mask_s_b), (0, mask_i_b)):
                        for hh in range(2):
                            hs = slice(hh * 4, (hh + 1) * 4)
                            pe_, pek = bank()
                            for h4 in range(4):
                                h = hh * 4 + h4
                                selrh = sel_b[:, (r * 8 + h) * 128:(r * 8 + h + 1) * 128]
                                o = pe_[:, h4 * 128:(h4 + 1) * 128]
                                mm(o, selrh, RCT[:, 0:128], True, False, ["sel_b", ("RCT", t)], [pek])
                                mm(o, RCT[:, 128:256], selrh, False, False, ["sel_b", ("RCT", t)], [pek])
                                mm(o, ident_b, mk, False, True, ["cst_b"], [pek])
                            E = Eb[ectr % 2]
                            Ek = ("Eb", ectr % 2)
                            ectr += 1
                            act(E[:], v3(pe_[:], 4), AF.Exp, [pek], [Ek])
                            if r == 1:
                                stt("dve", Mp[0][:, hs, :], v3(pkk[hh][0][:], 4), -1.0, E[:], ALU.mult, ALU.mult, [pkk[hh][1], Ek], [("Mp", 0, hh)])
                                if ti == 0 and hh == 0:
                                    dbg("E1_0", E[:].rearrange("p h i -> p (h i)"), Ek)
                            else:
                                tt("dve", qkT[:, hs, :], v3(pkq[hh][0][:], 4), E[:], ALU.mult, [pkq[hh][1], Ek], [("qkT", hh)])
                            side()
                    if KNOB.get("mstop") == 3:
                        s = 0
                        tr.dma("sp", f"d_xr{s}", xr[s][:], x_d[ti * 128:(ti + 1) * 128, :], reads=[], writes=[("xr", s)])
                        tr.dma("sp", f"d_st{s}", x1_d[ti * 128:(ti + 1) * 128, :], xr[s][:], reads=[("xr", s)], writes=[("x1d", ti)])
                        continue
                    bc4 = lambda a: a.unsqueeze(1).to_broadcast([128, 4, 128])
                    for hh in range(2):
                        hs = slice(hh * 4, (hh + 1) * 4)
                        pn, pnk = bank()
                        pnb = pn[:].bitcast(BF16)
                        for h4 in range(4):
                            tp(pnb[:, h4 * 128:(h4 + 1) * 128], Mp[0][:, hh * 4 + h4, :], ident_b, [("Mp", 0, hh), "cst_b"], [pnk])
                        cp("act", Np[0][:, hs, :], v3(pnb[:, 0:512], 4), [pnk], [("Np", 0, hh)])
                        tt("dve", Y[:, hs, :], Mp[0][:, hs, :], bc4(lm_b[:, 1, :]), ALU.mult, [("Mp", 0, hh), "lm_b"], [("Y", hh)])
                        tt("dve", Y[:, hs, :], Y[:, hs, :], bc4(ident_b), ALU.add, [("Y", hh), "cst_b"], [("Y", hh)])
                        tt("dve", Yt[:, hs, :], Np[0][:, hs, :], bc4(lm_b[:, 0, :]), ALU.mult, [("Np", 0, hh), "lm_b"], [("Yt", hh)])
                        tt("dve", Yt[:, hs, :], Yt[:, hs, :], bc4(ident_b), ALU.add, [("Yt", hh), "cst_b"], [("Yt", hh)])
                        side()
                    for lvl in range(1, 7):
                        pPs = []
                        for hh in range(2):
                            pP, pPk = bank()
                            for h4 in range(4):
                                mm(pP[:, h4 * 128:(h4 + 1) * 128], Np[0][:, hh * 4 + h4, :], Y[:, hh * 4 + h4, :], True, True, [("Np", 0, hh), ("Y", hh)], [pPk])
                            pPs.append((pP, pPk))
                        for hh in range(2):
                            hs = slice(hh * 4, (hh + 1) * 4)
                            tt("dve", Pb[:, hs, :], v3(pPs[hh][0][:], 4), bc4(lm_b[:, 1 + lvl, :]), ALU.mult, [pPs[hh][1], "lm_b"], [("Pb", hh)])
                        side()
                        upd = []
                        for hh in range(2):
                            Ytk, Pk = ("Yt", hh), ("Pb", hh)
                            pY, pYk = bank()
                            for h4 in range(4):
                                h = hh * 4 + h4
                                o = pY[:, h4 * 128:(h4 + 1) * 128]
                                mm(o, ident_b, Y[:, h, :], True, False, ["cst_b", ("Y", hh)], [pYk])
                                mm(o, Yt[:, h, :], Pb[:, h, :], False, True, [Ytk, Pk], [pYk])
                            pYt, pYtk = (None, None)
                            if lvl < 6:
                                pYt, pYtk = bank()
                                for h4 in range(4):
                                    h = hh * 4 + h4
                                    o = pYt[:, h4 * 128:(h4 + 1) * 128]
                                    mm(o, Pb[:, h, :], Yt[:, h, :], True, True, [Ytk, Pk], [pYtk])
                            upd.append((pY, pYk, pYt, pYtk))
                        for hh in range(2):
                            hs = slice(hh * 4, (hh + 1) * 4)
                            pY, pYk, pYt, pYtk = upd[hh]
                            cp("act", Y[:, hs, :], v3(pY[:], 4), [pYk], [("Y", hh)])
                            if lvl < 6:
                                tt("dve", Yt[:, hs, :], Yt[:, hs, :], v3(pYt[:], 4), ALU.add, [("Yt", hh), pYtk], [("Yt", hh)])
                        side()
                    if KNOB.get("mstop") == 5:
                        s = 0
                        tr.dma("sp", f"d_xr{s}", xr[s][:], x_d[ti * 128:(ti + 1) * 128, :], reads=[], writes=[("xr", s)])
                        tr.dma("sp", f"d_st{s}", x1_d[ti * 128:(ti + 1) * 128, :], xr[s][:], reads=[("xr", s)], writes=[("x1d", ti)])
                        continue
                    for hh in range(2):
                        pw, pwk = bank()
                        for h4 in range(4):
                            h = hh * 4 + h4
                            pr = h // 2
                            mm(pw[:, h4 * 128:(h4 + 1) * 128], kg[:, pr * 128:(pr + 1) * 128], Y[:, h, :], True, True, [("kg", t), ("Y", hh)], [pwk])
                        pw4 = v3(pw[:], 4)
                        tr.op("act", lambda e, pw4=pw4, hh=hh: e.mul(out=nwT[0:64, hh * 4:hh * 4 + 4:2, :], in_=pw4[0:64, 0::2, :], mul=-1.0), [pwk], [("nwT", hh, 0)])
                        tr.op("act", lambda e, pw4=pw4, hh=hh: e.mul(out=nwT[64:128, hh * 4 + 1:hh * 4 + 4:2, :], in_=pw4[64:128, 1::2, :], mul=-1.0), [pwk], [("nwT", hh, 1)])
                    if KNOB.get("mstop") == 6:
                        s = 0
                        tr.dma("sp", f"d_xr{s}", xr[s][:], x_d[ti * 128:(ti + 1) * 128, :], reads=[], writes=[("xr", s)])
                        tr.dma("sp", f"d_st{s}", x1_d[ti * 128:(ti + 1) * 128, :], xr[s][:], reads=[("xr", s)], writes=[("x1d", ti)])
                        continue
                    for _ in pend[0]:
                        pass
                    Yks = [("Y", 0), ("Y", 1)]
                    pvn, pvnk = bank()
                    for h in range(8):
                        po = 64 * (h % 2)
                        o = pvn[:, h * 64:(h + 1) * 64]
                        mm(o, Y[:, h, :], Ev[:, h * 64:(h + 1) * 64], True, False, Yks + [("Ev", t)], [pvnk])
                        mm(o, nwT[:, h, :], S_b[:, h // 2, :], False, True, [("nwT", h // 4, h % 2), ("S_b", 0), ("S_b", 1)], [pvnk])
                    po1, po1k = bank()
                    for h in range(8):
                        po = 64 * (h % 2)
                        mm(po1[:, h * 64:(h + 1) * 64], kq_m[:, h // 2, h % 2, :], S_b[:, h // 2, :], True, True, KQM + [("S_b", 0), ("S_b", 1)], [po1k])
                    tt("dve", v3(vn[:], 8), v3(pvn[:], 8), bc_h(SC[:, 32:40], 64), ALU.mult, [pvnk, ("SC", t)], ["vn"])
                    side()
                    glg = SC[:, 48:56].rearrange("p (pr two) -> p pr two", two=2)
                    for h2 in range(2):
                        rs = slice(64 * h2, 64 * h2 + 64)
                        tt("pool", Sdec[rs], S_f[rs], glg[rs, :, h2].unsqueeze(2).to_broadcast([64, 4, 64]), ALU.mult, [("S_f", h2), ("SC", t)], [("Sdec", h2)])
                    po2, po2k = bank()
                    psn, psnk = bank()
                    for h in range(8):
                        mm(po2[:, h * 64:(h + 1) * 64], qkT[:, h, :], vn[:, h * 64:(h + 1) * 64], True, True, [("qkT", h // 4), "vn"], [po2k])
                    for pr in range(4):
                        mm(psn[:, pr * 128:(pr + 1) * 128], kd[:, pr * 128:(pr + 1) * 128], vn[:, pr * 128:(pr + 1) * 128], True, True, [("kd", t), "vn"], [psnk])
                    psn4 = v3(psn[:], 4)
                    for h2 in range(2):
                        rs = slice(64 * h2, 64 * h2 + 64)
                        src = psn4[rs, :, 64 * h2:64 * h2 + 64]
                        tt("dve", S_b[rs], Sdec[rs], src, ALU.add, [("Sdec", h2), psnk], [("S_b", h2)])
                        tt("dve", S_f[rs], Sdec[rs], src, ALU.add, [("Sdec", h2), psnk], [("S_f", h2)])
                    tt("dve", v3(o_t[:], 8), v3(po1[:], 8), bc_h(SC[:, 0:8], 64), ALU.mult, [po1k, ("SC", t)], ["o_t"])
                    tt("dve", o_f[:], o_t[:], po2[:], ALU.add, ["o_t", po2k], ["o_f"])
                    side()
                    if ti in (0, 1):
                        dbg(f"of_{ti}", o_f[:], "o_f")
                        dbg(f"SC_{ti}", SC[:], ("SC", t))
                    if KNOB.get("mstop") == 7:
                        s = 0
                        tr.dma("sp", f"d_xr{s}", xr[s][:], x_d[ti * 128:(ti + 1) * 128, :], reads=[], writes=[("xr", s)])
                        tr.dma("sp", f"d_st{s}", x1_d[ti * 128:(ti + 1) * 128, :], xr[s][:], reads=[("xr", s)], writes=[("x1d", ti)])
                        continue
                    act(sqf[:, 0:512], o_f[:], AF.Square, ["o_f"], ["op_t"])
                    red("dve", ssq[:, 0:8], v3(sqf[:, 0:512], 8), ["op_t"], ["ssq"])
                    act(ssq[:, 0:8], ssq[:, 0:8], AF.Ln, ["ssq"], ["ssq"], bias=EPS, scale=1.0 / 64)
                    act(ssq[:, 0:8], ssq[:, 0:8], AF.Exp, ["ssq"], ["ssq"], scale=-0.5)
                    tt("dve", v3(o_t[:], 8), v3(o_f[:], 8), bc_h(ssq[:, 0:8], 64), ALU.mult, ["o_f", "ssq"], ["o_t"])
                    tt("pool", v3(o_t[:], 8), v3(o_t[:], 8), gnw[:].unsqueeze(1).to_broadcast([128, 8, 64]), ALU.mult, ["o_t", "gnw"], ["o_t"])
                    tt("pool", mix[:, 0:512], o_t[:], sz[:, t, 0:512], ALU.mult, ["o_t", ("sz", t)], ["mix"])

                    if KNOB.get("mstop") == 8:
                        s = 0
                        tr.dma("sp", f"d_xr{s}", xr[s][:], x_d[ti * 128:(ti + 1) * 128, :], reads=[], writes=[("xr", s)])
                        tr.dma("sp", f"d_st{s}", x1_d[ti * 128:(ti + 1) * 128, :], xr[s][:], reads=[("xr", s)], writes=[("x1d", ti)])
                        continue
                    if KNOB.get("mstop") == 9:
                        s = 0
                        tr.dma("sp", f"d_xr{s}", xr[s][:], x_d[ti * 128:(ti + 1) * 128, :], reads=[], writes=[("xr", s)])
                        tr.dma("sp", f"d_st{s}", x1_d[ti * 128:(ti + 1) * 128, :], xr[s][:], reads=[("xr", s)], writes=[("x1d", ti)])
                        continue
                    for _ in ssd:
                        pass
                    def tail_chain(ti=ti):
                        s_ = 0
                        pmt, pmtk = bank(1)
                        pmtb = pmt[:].bitcast(BF16)
                        for c in range(KC):
                            tp(pmtb[:, c * 128:(c + 1) * 128], mix[:, c * 128:(c + 1) * 128], ident_b, ["mix", "cst_b"], [pmtk])
                        cp("act", mixT[:], pmtb.rearrange("p (k t) -> p k t", k=KC), [pmtk], ["mixT"])
                        tr.dma("sp", f"d_xr{s_}", xr[s_][:], x_d[ti * 128:(ti + 1) * 128, :], reads=[], writes=[("xr", s_)])
                        yield
                        for n in range(2):
                            pop, popk = bank(1)
                            for c in range(KC):
                                mm(pop[:], mixT[:, c, :], w_out[:, c, n * 512:(n + 1) * 512], c == 0, c == KC - 1, ["mixT", "w_out"], [popk])
                            ns = slice(n * 512, (n + 1) * 512)
                            tt("dve", op_t[:, ns], pop[:], modb[:, 2, ns], ALU.mult, [popk, "modb"], ["op_t"])
                            tt("pool", xr[s_][:, ns], op_t[:, ns], xr[s_][:, ns], ALU.add, ["op_t", ("xr", s_)], [("xr", s_)])
                            if n == 0:
                                yield
                        tr.dma("sp", f"d_st{s_}", x1_d[ti * 128:(ti + 1) * 128, :], xr[s_][:], reads=[("xr", s_)], writes=[("x1d", ti)])
                        if ti in (0, 1):
                            dbg(f"x1_{ti}", xr[s_][:], ("xr", s_))

                    for _ in pend[0]:
                        pass
                    pend[0] = tail_chain()
                if not KNOB.get("fm_il", True) and g + 1 < KNOB["ng"]:
                    fmg = fm_gen(g + 1)
                for _ in fmg:
                    pass
            for _ in pend[0]:
                pass
            with nc.Block() as block:
                tr.emit(block)

        p2 = ExitStack()
        with p2:
            def sb2(name, shape, dt=F32):
                return sb(name, shape, dt, stack=p2)

            W1 = sb2("W1", [128, KC, DFF], BF16)
            W2 = sb2("W2", [128, 32, D], BF16)
            ms = ExitStack()
            with ms:
                adab1 = [sb(f"adabb{i}", [128, MW], F32, ms) for i in range(2)]
                adaw1 = [sb(f"adawb{i}", [128, KC, MW], BF16, ms) for i in range(2)]
                lnw1 = sb("lnw1", [128, D], F32, ms)
                cB1 = sb("cB1", [128, KC, 128], BF16, ms)
                tr.dma("sp", "d_k_lnw1", lnw1[:], lnw_d[:, D:2 * D], reads=[], writes=["lnw1"])
                cp("dve", cB1[:], c_sb[:].unsqueeze(2).to_broadcast([128, KC, 128]), ["c_sb"], ["cB1"])
                adst1 = [sb(f"adstb{i}", [128, KC, MW], F32, ms) for i in range(2)]
                compute_mod(3, adaw1, adab1, cB1, "cB1", lnw1, "lnw1", adst1)
                for pc in range(4):
                    for k in range(KC):
                        tr.dma("pool", f"d_w1_{pc}", W1[:, k, pc * 1024:(pc + 1) * 1024], w1_d[k * 128:(k + 1) * 128, pc * 1024:(pc + 1) * 1024],
                               reads=[], writes=[("W1", pc)])
                for k in range(32):
                    tr.dma("pool", "d_w2", W2[:, k, :], w2_d[k * 128:(k + 1) * 128, :], reads=[], writes=["W2"])
                with nc.Block() as block:
                    tr.emit(block)
            lnwf = sb2("lnwf", [128, D])
            tr.dma("sp", "d_k_lnwf", lnwf[:], lnw_d[:, 2 * D:3 * D], reads=[], writes=["lnwf"])
            x1k = [sb2(f"x1k{i}", [128, D]) for i in range(4)]
            hbf2 = sb2("hbf2", [128, D], BF16)
            h2T = [sb2(f"h2T{i}", [128, KC, G2T], BF16) for i in range(2)]
            nst2 = sb2("nst2", [128, 8])
            rl = [sb2(f"rl{i}", [128, G2T]) for i in range(2)]
            aT = sb2("aT", [128, 32, G2T], BF16)
            f_t = sb2("f_t", [128, D])
            hn2 = sb2("hn2", [128, D])
            print("phase2 sbuf remaining", nc.sbuf_bytes_remaining)

            def p2_norm(g):
                tiles = []
                hb = h2T[g % 2]
                hk = ("h2T", g % 2)
                for t in range(2):
                    ti = g * 2 + t
                    xb, xk = norm_to_hT(x1_d, [("x1d", ti)], ti * 128, hb, hk, t * 128, x1k, "d_p2x", hbf2, "hbf2", nst2, "nst2")
                    norm_finish(xb, xk, hb, hk, t * 128, hbf2, "hbf2", nst2, "nst2", dst=hn2, dstk="hn2")
                    tiles.append((xb, xk))
                return tiles

            def p2_ffn1(g):
                hb = h2T[g % 2]
                hk = ("h2T", g % 2)
                for c in range(32):
                    pt, pk = bank()
                    for k in range(KC):
                        mm(pt[:, 0:G2T], W1[:, k, c * 128:(c + 1) * 128], hb[:, k, :], k == 0, k == KC - 1, [("W1", c // 8), hk], [pk])
                    r_ = rl[c % 2]
                    rk = ("rl", c % 2)
                    act(r_[:], pt[:, 0:G2T], AF.Relu, [pk], [rk])
                    tt("pool" if c % 2 else "dve", aT[:, c, :], r_[:], r_[:], ALU.mult, [rk], [("aT", c)])

            def p2_ffn2(g, tiles):
                for t in range(2):
                    ti = g * 2 + t
                    xb, xk = tiles[t]
                    for n in range(2):
                        pt, pk = bank()
                        for c in range(32):
                            mm(pt[:], aT[:, c, t * 128:(t + 1) * 128], W2[:, c, n * 512:(n + 1) * 512], c == 0, c == 31, [("aT", c), "W2"], [pk])
                        ns = slice(n * 512, (n + 1) * 512)
                        tt("dve", f_t[:, ns], pt[:], modb[:, 2, ns], ALU.mult, [pk, "modb"], ["f_t"])
                        tt("pool", xb[:, ns], f_t[:, ns], xb[:, ns], ALU.add, ["f_t", xk], [xk])
                    act(hbf2[:], xb[:], AF.Square, [xk], ["hbf2", "nst2b"], accum_out=nst2[:, 4:5])
                    act(nst2[:, 5:6], nst2[:, 4:5], AF.Ln, ["nst2b"], ["nst2b"], bias=EPS, scale=1.0 / D)
                    act(nst2[:, 6:7], nst2[:, 5:6], AF.Exp, ["nst2b"], ["nst2b"], scale=-0.5)
                    stt("dve", xb[:], xb[:], nst2[:, 6:7], lnwf[:], ALU.mult, ALU.mult, [xk, "nst2b", "lnwf"], [xk])
                    slot = xk[1]
                    tr.dma("sp", f"d_po{slot}", out_d[ti * 128:(ti + 1) * 128, :], xb[:], reads=[xk], writes=[("outd", slot)])

            n2 = KNOB["ng2"]
            nxt_tiles = p2_norm(0) if n2 > 0 else None
            for g in range(n2):
                tiles = nxt_tiles
                p2_ffn1(g)
                if g + 1 < n2:
                    nxt_tiles = p2_norm(g + 1)
                p2_ffn2(g, tiles)
            tr.final_wait("sp", [("outd", i) for i in range(4)] + [("dbgout", n) for n in dbg_d])
            print("sem counts", {k: v for k, v in tr.cnt.items()})
            with nc.Block() as block:
                tr.emit(block)
    return nc


def host_constants():
    ident = np.eye(128, dtype=np.float32)
    k = np.arange(128)
    triu = (k[:, None] <= k[None, :]).astype(np.float32)
    ones = np.ones((128, 128), np.float32)
    mask_s = np.where(k[None, :] > k[:, None], 0.0, NEG).astype(np.float32)
    mask_i = np.where(k[None, :] >= k[:, None], 0.0, NEG).astype(np.float32)
    m0 = (k < 64).astype(np.float32)[:, None]
    cst = np.concatenate([ident, triu, ones, mask_s, mask_i, m0, 1.0 - m0], axis=1)
    sel = np.zeros((128, 24, 128), np.float32)
    for r in range(3):
        for h in range(8):
            sel[r * 8 + h, r * 8 + h, :] = 1.0
            sel[24 + r * 8 + h, r * 8 + h, :] = 1.0
    def lmask(b):
        i = k[:, None]
        j = k[None, :]
        return ((i // (2 * b) == j // (2 * b)) & ((i // b) % 2 == 1) & ((j // b) % 2 == 0)).astype(np.float32)
    lms = [lmask(1)] + [lmask(b).T for b in (1, 2, 4, 8, 16, 32, 64)]
    lm = np.concatenate(lms, axis=1)
    return np.ascontiguousarray(cst), np.ascontiguousarray(sel.reshape(128, 24 * 128)), np.ascontiguousarray(lm)


def prep_inputs(inputs):
    f = lambda a: np.ascontiguousarray(np.asarray(a, dtype=np.float32))
    w_in = f(inputs["w_in"])[0]
    perm = np.concatenate([np.arange(0, 1536), np.arange(2064, 2832), np.arange(1536, 2048), np.arange(2832, 3344),
                           np.arange(2048, 2056), np.arange(2056, 2064), np.arange(3344, 3352)])
    w_in_r = np.ascontiguousarray(w_in[:, perm])
    gcw = f(inputs["gdn_conv_w"])[0]
    scw = f(inputs["ssm_conv_w"])[0]
    allw = np.concatenate([gcw, scw], axis=1)
    cw = np.ascontiguousarray(allw.reshape(4, 18, 128).transpose(2, 1, 0).reshape(128, 72))
    cb_full = np.concatenate([np.zeros(1536, np.float32), f(inputs["ssm_conv_b"])[0]])
    cb = np.ascontiguousarray(cb_full.reshape(18, 128).T)
    hp_row = np.concatenate([f(inputs["gdn_A_log"])[0], f(inputs["gdn_dt_bias"])[0], f(inputs["ssm_A_log"])[0],
                             f(inputs["ssm_dt_bias"])[0], f(inputs["ssm_D"])[0]])
    bc = lambda row: np.ascontiguousarray(np.broadcast_to(row[None, :], (128, row.shape[0])))
    lnw_row = np.concatenate([f(inputs["ln1_w"])[0], f(inputs["ln2_w"])[0], f(inputs["final_norm_w"])])
    cst, sel, lm = host_constants()
    shared = {
        "w_in_r": w_in_r, "w_out": f(inputs["w_out"])[0], "w_ff1": f(inputs["w_ff1"])[0], "w_ff2": f(inputs["w_ff2"])[0],
        "ada_w": f(inputs["ada_w"])[0], "ada_b_b": bc(f(inputs["ada_b"])[0]), "lnw_b": bc(lnw_row), "cw": cw, "cb": cb,
        "hp": bc(hp_row), "gnw": bc(f(inputs["gdn_norm_w"])[0]), "snw": bc(f(inputs["ssm_norm_w"])[0]),
        "cst": cst, "sel": sel, "lm": lm,
    }
    x = f(inputs["x"])
    c = f(inputs["c"])
    in_maps = []
    for b in range(NCORE):
        m = dict(shared)
        m["x"] = np.ascontiguousarray(x[b])
        m["c_l"] = np.ascontiguousarray(c[b].reshape(KC, 128).T)
        in_maps.append(m)
    return in_maps


def kernel(**inputs):
    in_maps = prep_inputs(inputs)
    nc = build_program(DEBUG)
    res = run_bass_kernel_spmd(nc, in_maps, core_ids=list(range(NCORE)))
    out = np.stack([np.asarray(r["out"], dtype=np.float32) for r in res.results], axis=0)
    if DEBUG:
        kernel.debug = [{k: np.asarray(v) for k, v in r.items() if k.startswith("dbg_")} for r in res.results]
    return out
```
